# Optimizing a Trainium2 kernel written in Bass

```python
import numpy as np
import jax, jax.numpy as jnp
from jax import lax

D_MODEL = 1024
BATCH = 2
SEQ = 8192
DEPTH = 1

CONV_CH = 512
CONV_GROUPS = 8
CONV_K = 3
N_HEADS = 8
N_KV_GROUPS = 2
HEADS_PER_GROUP = N_HEADS // N_KV_GROUPS
HEAD_DIM = 64
ATTN_WIDTH = N_HEADS * HEAD_DIM
KV_WIDTH = N_KV_GROUPS * HEAD_DIM
MIX_WIDTH = CONV_CH + ATTN_WIDTH
ROPE_THETA = 500000.0
ROT_DIM = HEAD_DIM // 4
CMP_BLOCK = 32
CMP_STRIDE = 16
CMP_HIDDEN = 256
SLC_BLOCK = 64
SLC_TOP_N = 16
WINDOW = 512
Q_BLOCK = 128
D_FF = 2816
MACARON_W = 0.5
N_MOD = 9
IN_WIDTH = 3 * CONV_CH + ATTN_WIDTH + 6 * KV_WIDTH + 3 * N_HEADS
EPS = 1e-6
NEG = -1e30
BIG = 1e9

kernel_name = "hymba_conv_nsa_macaron_adaln"


def _rmsnorm(x, g):
    xf = x.astype(jnp.float32)
    y = xf * lax.rsqrt(jnp.mean(xf * xf, axis=-1, keepdims=True) + EPS)
    return (y * g.astype(jnp.float32)).astype(x.dtype)


def _group_rmsnorm(y, g, n_groups):
    shp = y.shape
    yg = y.reshape(shp[:-1] + (n_groups, shp[-1] // n_groups)).astype(jnp.float32)
    yg = yg * lax.rsqrt(jnp.mean(yg * yg, axis=-1, keepdims=True) + EPS)
    return (yg.reshape(shp) * g.astype(jnp.float32)).astype(y.dtype)


def _modulate(h, shift, scale):
    return h * (1.0 + scale) + shift


def _swiglu(h, w_gate, w_up, w_down):
    return (jax.nn.silu(h @ w_gate) * (h @ w_up)) @ w_down


def _partial_rope(t, positions):
    half = ROT_DIM // 2
    freqs = jnp.power(ROPE_THETA, -2.0 * jnp.arange(half, dtype=jnp.float32) / ROT_DIM)
    ang = positions.astype(jnp.float32)[:, :, None] * freqs
    cos = jnp.cos(ang)[:, :, None, :].astype(t.dtype)
    sin = jnp.sin(ang)[:, :, None, :].astype(t.dtype)
    t1, t2, rest = t[..., :half], t[..., half:ROT_DIM], t[..., ROT_DIM:]
    return jnp.concatenate([t1 * cos - t2 * sin, t2 * cos + t1 * sin, rest], axis=-1)


def _compress(k, pos_emb, w1, w2):
    S = k.shape[2]
    n_cmp = (S - CMP_BLOCK) // CMP_STRIDE + 1
    idx = np.arange(n_cmp)[:, None] * CMP_STRIDE + np.arange(CMP_BLOCK)[None, :]
    blocks = k[:, :, idx] + pos_emb
    flat = blocks.reshape(k.shape[0], k.shape[1], n_cmp, CMP_BLOCK * k.shape[3])
    return jax.nn.silu(flat @ w1) @ w2


def _cmp_slc_overlap(n_cmp, n_slc):
    c0 = np.arange(n_cmp) * CMP_STRIDE
    c1 = c0 + CMP_BLOCK - 1
    s0 = np.arange(n_slc) * SLC_BLOCK
    s1 = s0 + SLC_BLOCK - 1
    return ((c0[:, None] <= s1[None, :]) & (c1[:, None] >= s0[None, :])).astype(np.float32)


def _masked_softmax(s, mask):
    return jax.nn.softmax(jnp.where(mask, s, NEG), axis=-1)


def _nsa_attention(q, kc, vc, ks, vs, kw, vw, gates):
    B, G, HPG, S, dk = q.shape
    n_cmp = kc.shape[2]
    n_slc = S // SLC_BLOCK
    top_n = min(SLC_TOP_N, n_slc)
    scale = dk ** -0.5
    cmp_end = jnp.arange(n_cmp) * CMP_STRIDE + CMP_BLOCK - 1
    overlap = jnp.asarray(_cmp_slc_overlap(n_cmp, n_slc))
    ks_blk = ks.reshape(B, G, n_slc, SLC_BLOCK, dk)
    vs_blk = vs.reshape(B, G, n_slc, SLC_BLOCK, dk)
    kw_pad = jnp.pad(kw, ((0, 0), (0, 0), (WINDOW, 0), (0, 0)))
    vw_pad = jnp.pad(vw, ((0, 0), (0, 0), (WINDOW, 0), (0, 0)))
    b_idx = jnp.arange(B)[:, None, None, None]
    g_idx = jnp.arange(G)[None, :, None, None]
    blk = jnp.arange(n_slc)
    in_blk = jnp.arange(SLC_BLOCK)
    win_off = jnp.arange(WINDOW + Q_BLOCK) - WINDOW

    def one_block(qb):
        t0 = qb * Q_BLOCK
        t = t0 + jnp.arange(Q_BLOCK)
        qblk = lax.dynamic_slice_in_dim(q, t0, Q_BLOCK, axis=3)
        gblk = jax.nn.sigmoid(lax.dynamic_slice_in_dim(gates, t0, Q_BLOCK, axis=3).astype(jnp.float32))

        s = jnp.einsum('bghqd,bgnd->bghqn', qblk, kc).astype(jnp.float32) * scale
        valid = cmp_end[None, :] <= t[:, None]
        p_cmp = _masked_softmax(s, valid) * valid
        o_cmp = jnp.einsum('bghqn,bgnd->bghqd', p_cmp.astype(vc.dtype), vc)

        imp = jnp.einsum('bghqn,nj->bgqj', p_cmp, overlap)
        cur = t // SLC_BLOCK
        future = blk[None, :] > cur[:, None]
        forced = (blk[None, :] == 0) | (blk[None, :] == cur[:, None]) | (blk[None, :] == cur[:, None] - 1)
        score = jnp.where(future, -BIG, jnp.where(forced, BIG, imp))
        _, idx = lax.top_k(score, top_n)
        k_sel = ks_blk[b_idx, g_idx, idx].reshape(B, G, Q_BLOCK, top_n * SLC_BLOCK, dk)
        v_sel = vs_blk[b_idx, g_idx, idx].reshape(B, G, Q_BLOCK, top_n * SLC_BLOCK, dk)
        pos_sel = (idx[..., None] * SLC_BLOCK + in_blk).reshape(B, G, Q_BLOCK, top_n * SLC_BLOCK)
        mask_sel = (pos_sel <= t[None, None, :, None])[:, :, None]
        s = jnp.einsum('bghqd,bgqkd->bghqk', qblk, k_sel).astype(jnp.float32) * scale
        p = _masked_softmax(s, mask_sel)
        o_slc = jnp.einsum('bghqk,bgqkd->bghqd', p.astype(v_sel.dtype), v_sel)

        k_win = lax.dynamic_slice_in_dim(kw_pad, t0, WINDOW + Q_BLOCK, axis=2)
        v_win = lax.dynamic_slice_in_dim(vw_pad, t0, WINDOW + Q_BLOCK, axis=2)
        pos_win = t0 + win_off
        dist = t[:, None] - pos_win[None, :]
        mask_win = (pos_win[None, :] >= 0) & (dist >= 0) & (dist < WINDOW)
        s = jnp.einsum('bghqd,bgkd->bghqk', qblk, k_win).astype(jnp.float32) * scale
        p = _masked_softmax(s, mask_win)
        o_win = jnp.einsum('bghqk,bgkd->bghqd', p.astype(v_win.dtype), v_win)

        o = gblk[..., 0:1] * o_cmp + gblk[..., 1:2] * o_slc + gblk[..., 2:3] * o_win
        return o.astype(q.dtype)

    out = lax.map(one_block, jnp.arange(S // Q_BLOCK))
    return out.transpose(1, 0, 4, 2, 3, 5).reshape(B, S, G * HPG * dk)


def _hybrid_mixer(h, positions, w_in, conv_w, cmp_pos_k, cmp_pos_v, w_cmpk1, w_cmpk2,
                  w_cmpv1, w_cmpv2, g_out_conv, g_out_attn, w_out):
    B, S, _ = h.shape
    proj = h @ w_in
    sizes = [CONV_CH, CONV_CH, CONV_CH, ATTN_WIDTH] + [KV_WIDTH] * 6 + [3 * N_HEADS]
    cuts = np.cumsum(sizes)[:-1].tolist()
    cb, cc, cx, q, kc, vc, ks, vs, kw, vw, gt = jnp.split(proj, cuts, axis=-1)

    u = cc * cx
    v = lax.conv_general_dilated(u, conv_w[:, None, :], window_strides=(1,),
                                 padding=[(CONV_K - 1, 0)],
                                 dimension_numbers=('NWC', 'WIO', 'NWC'),
                                 feature_group_count=CONV_CH)
    y_conv = cb * v

    qh = _partial_rope(q.reshape(B, S, N_HEADS, HEAD_DIM), positions)
    qh = qh.reshape(B, S, N_KV_GROUPS, HEADS_PER_GROUP, HEAD_DIM).transpose(0, 2, 3, 1, 4)

    def kv_heads(t, rope):
        t = t.reshape(B, S, N_KV_GROUPS, HEAD_DIM)
        if rope:
            t = _partial_rope(t, positions)
        return t.transpose(0, 2, 1, 3)

    kc_c = _compress(kv_heads(kc, True), cmp_pos_k, w_cmpk1, w_cmpk2)
    vc_c = _compress(kv_heads(vc, False), cmp_pos_v, w_cmpv1, w_cmpv2)
    gates = gt.reshape(B, S, N_KV_GROUPS, HEADS_PER_GROUP, 3).transpose(0, 2, 3, 1, 4)
    y_attn = _nsa_attention(qh, kc_c, vc_c, kv_heads(ks, True), kv_heads(vs, False),
                            kv_heads(kw, True), kv_heads(vw, False), gates)

    y = jnp.concatenate([_group_rmsnorm(y_conv, g_out_conv, CONV_GROUPS),
                         _group_rmsnorm(y_attn, g_out_attn, N_HEADS)], axis=-1)
    return y @ w_out


def setup_inputs(seed: int = 0) -> dict:
    key = jax.random.key(seed)
    ks = jax.random.split(key, 32)
    f32 = jnp.float32

    def nrm(k, shape, s):
        return jax.random.normal(k, shape, f32) * s

    def gain(k, shape):
        return 1.0 + 0.02 * jax.random.normal(k, shape, f32)

    L, D = DEPTH, D_MODEL
    return {
        "x": nrm(ks[0], (BATCH, SEQ, D), 1.0),
        "c": nrm(ks[1], (BATCH, D), 1.0),
        "positions": jnp.tile(jnp.arange(SEQ, dtype=jnp.int32)[None, :], (BATCH, 1)),
        "w_ada": nrm(ks[2], (L, D, N_MOD * D), 0.5 * D ** -0.5),
        "b_ada": nrm(ks[3], (L, N_MOD * D), 0.1),
        "g_ffn1": gain(ks[4], (L, D)),
        "w1_gate": nrm(ks[5], (L, D, D_FF), D ** -0.5),
        "w1_up": nrm(ks[6], (L, D, D_FF), D ** -0.5),
        "w1_down": nrm(ks[7], (L, D_FF, D), D_FF ** -0.5),
        "g_mix": gain(ks[8], (L, D)),
        "w_in": nrm(ks[9], (L, D, IN_WIDTH), D ** -0.5),
        "conv_w": nrm(ks[10], (L, CONV_K, CONV_CH), CONV_K ** -0.5),
        "cmp_pos_k": nrm(ks[11], (L, CMP_BLOCK, HEAD_DIM), 0.5),
        "cmp_pos_v": nrm(ks[12], (L, CMP_BLOCK, HEAD_DIM), 0.5),
        "w_cmpk1": nrm(ks[13], (L, CMP_BLOCK * HEAD_DIM, CMP_HIDDEN), (CMP_BLOCK * HEAD_DIM) ** -0.5),
        "w_cmpk2": nrm(ks[14], (L, CMP_HIDDEN, HEAD_DIM), CMP_HIDDEN ** -0.5),
        "w_cmpv1": nrm(ks[15], (L, CMP_BLOCK * HEAD_DIM, CMP_HIDDEN), (CMP_BLOCK * HEAD_DIM) ** -0.5),
        "w_cmpv2": nrm(ks[16], (L, CMP_HIDDEN, HEAD_DIM), CMP_HIDDEN ** -0.5),
        "g_out_conv": gain(ks[17], (L, CONV_CH)),
        "g_out_attn": gain(ks[18], (L, ATTN_WIDTH)),
        "w_out": nrm(ks[19], (L, MIX_WIDTH, D), MIX_WIDTH ** -0.5),
        "g_ffn2": gain(ks[20], (L, D)),
        "w2_gate": nrm(ks[21], (L, D, D_FF), D ** -0.5),
        "w2_up": nrm(ks[22], (L, D, D_FF), D ** -0.5),
        "w2_down": nrm(ks[23], (L, D_FF, D), D_FF ** -0.5),
        "g_final": gain(ks[24], (D,)),
    }


def reference(x, c, positions, w_ada, b_ada, g_ffn1, w1_gate, w1_up, w1_down, g_mix, w_in,
              conv_w, cmp_pos_k, cmp_pos_v, w_cmpk1, w_cmpk2, w_cmpv1, w_cmpv2, g_out_conv,
              g_out_attn, w_out, g_ffn2, w2_gate, w2_up, w2_down, g_final):
    B = x.shape[0]
    c_act = jax.nn.silu(c)
    for l in range(DEPTH):
        mod = (c_act @ w_ada[l] + b_ada[l]).reshape(B, N_MOD, D_MODEL)[:, :, None, :]
        sh1, sc1, gt1, sh2, sc2, gt2, sh3, sc3, gt3 = [mod[:, i] for i in range(N_MOD)]
        h = _modulate(_rmsnorm(x, g_ffn1[l]), sh1, sc1)
        x = x + MACARON_W * gt1 * _swiglu(h, w1_gate[l], w1_up[l], w1_down[l])
        h = _modulate(_rmsnorm(x, g_mix[l]), sh2, sc2)
        x = x + gt2 * _hybrid_mixer(h, positions, w_in[l], conv_w[l], cmp_pos_k[l], cmp_pos_v[l],
                                    w_cmpk1[l], w_cmpk2[l], w_cmpv1[l], w_cmpv2[l],
                                    g_out_conv[l], g_out_attn[l], w_out[l])
        h = _modulate(_rmsnorm(x, g_ffn2[l]), sh3, sc3)
        x = x + MACARON_W * gt3 * _swiglu(h, w2_gate[l], w2_up[l], w2_down[l])
    return _rmsnorm(x, g_final)
```

```python
import numpy as np
from contextlib import ExitStack
import concourse.bass as bass
import concourse.mybir as mybir
from concourse.bass_utils import run_bass_kernel_spmd

F32 = mybir.dt.float32
BF16 = mybir.dt.bfloat16
I32 = mybir.dt.int32
AF = mybir.ActivationFunctionType
ALU = mybir.AluOpType
AX = mybir.AxisListType

NCORES = 8
D = 1024
S = 8192
NT = 2048
NJ = 16
HAL = 32
DFF = 2816
NF = 22
EPS = 1e-6
NEGM = -30000.0
XROWS = 768
DEBUG = None


class Prog:
    def __init__(self, nc, same_engine_sync=True):
        self.nc = nc
        self.ops = []
        self.same_engine_sync = same_engine_sync

    def add(self, eng, fn, reads=(), writes=(), dma=False, semkey=None):
        self.ops.append(dict(eng=eng, fn=fn, reads=tuple(reads), writes=tuple(writes),
                             dma=dma, semkey=semkey, barrier=False))

    def barrier(self):
        for e in ('pe', 'act', 'dve', 'pool', 'sp'):
            self.ops.append(dict(eng=e, fn=None, reads=(), writes=(), dma=False,
                                 semkey=None, barrier=True))

    def finalize(self, stack):
        nc = self.nc
        ops = self.ops
        n = len(ops)
        last_w = {}
        readers = {}
        deps = [set() for _ in range(n)]
        last_on_eng = {}
        all_dmas = []
        for i, op in enumerate(ops):
            if op['barrier']:
                for e, j in last_on_eng.items():
                    deps[i].add(j)
                for j in all_dmas:
                    deps[i].add(j)
                continue
            for k in op['reads']:
                if k in last_w:
                    deps[i].add(last_w[k])
            for k in op['writes']:
                if k in last_w:
                    deps[i].add(last_w[k])
                for r in readers.get(k, ()):
                    deps[i].add(r)
            for k in op['reads']:
                readers.setdefault(k, []).append(i)
            for k in op['writes']:
                last_w[k] = i
                readers[k] = []
            deps[i].discard(i)
            if op['dma']:
                all_dmas.append(i)
            else:
                last_on_eng[op['eng']] = i
        needed = set()
        for i in range(n):
            op = ops[i]
            for d in deps[i]:
                od = ops[d]
                if od['barrier'] or od['dma']:
                    continue
                if od['eng'] == op['eng'] and not op['dma']:
                    if od['eng'] == 'pe' or not self.same_engine_sync:
                        continue
                needed.add(d)
        eng_sem = {}
        for e in ('pe', 'act', 'dve', 'pool'):
            eng_sem[e] = stack.enter_context(nc.semaphore("sem_" + e))
        dma_sem = {}
        eng_cnt = {e: 0 for e in eng_sem}
        dma_cnt = {}
        sig = [None] * n
        for i, op in enumerate(ops):
            if op['barrier']:
                continue
            if op['dma']:
                k = op['semkey']
                if k not in dma_sem:
                    dma_sem[k] = stack.enter_context(nc.semaphore("dsem_%d" % len(dma_sem)))
                    dma_cnt[k] = 0
                dma_cnt[k] += 16
                sig[i] = (dma_sem[k], dma_cnt[k], 16)
            elif i in needed:
                e = op['eng']
                eng_cnt[e] += 1
                sig[i] = (eng_sem[e], eng_cnt[e], 1)
        self.n_sems = len(dma_sem) + 4
        waited = {e: {} for e in ('pe', 'act', 'dve', 'pool', 'sp')}
        waits = [[] for _ in range(n)]
        for i, op in enumerate(ops):
            e = op['eng']
            req = {}
            for d in deps[i]:
                if sig[d] is None:
                    continue
                if (not ops[d]['dma']) and d not in needed:
                    continue
                if (not ops[d]['dma']) and ops[d]['eng'] == e and not op['dma'] and \
                        (e == 'pe' or not self.same_engine_sync):
                    continue
                s, v, _ = sig[d]
                key = id(s)
                if key not in req or req[key][1] < v:
                    req[key] = (s, v)
            for key, (s, v) in req.items():
                if waited[e].get(key, 0) >= v:
                    continue
                waited[e][key] = v
                waits[i].append((s, v))
        final_dma = [(s, dma_cnt[k]) for k, s in dma_sem.items()]
        block = stack.enter_context(nc.Block())

        def emitter(ename):
            def _f(eng):
                for i, op in enumerate(ops):
                    if op['eng'] != ename:
                        continue
                    for (s, v) in waits[i]:
                        eng.wait_ge(s, v)
                    if op['fn'] is None:
                        continue
                    inst = op['fn'](eng)
                    if sig[i] is not None:
                        s, v, inc = sig[i]
                        inst.then_inc(s, inc)
                if ename == 'sp':
                    for (s, v) in final_dma:
                        eng.wait_ge(s, v)
            return _f

        block.tensor(emitter('pe'))
        block.scalar(emitter('act'))
        block.vector(emitter('dve'))
        block.gpsimd(emitter('pool'))
        block.sync(emitter('sp'))


class Arena:
    def __init__(self, ap, nwords):
        self.ap = ap
        self.n = nwords
        self.top = 0
        self.peak = 0

    def mark(self):
        return self.top

    def release(self, m):
        self.top = m

    def alloc(self, shape, dt):
        nel = int(np.prod(shape[1:]))
        nw = nel if dt in (F32, I32) else (nel + 1) // 2
        a = self.top
        self.top += nw
        self.peak = max(self.peak, self.top)
        assert self.top <= self.n, ("arena overflow", self.top, self.n)
        v = self.ap[:, a:a + nw]
        if dt != F32:
            v = v.bitcast(dt)
            if dt == BF16 and nel % 2:
                v = v[:, 0:nel]
        if len(shape) == 3:
            v = v.rearrange("p (a b) -> p a b", a=shape[1])
        elif len(shape) == 4:
            v = v.rearrange("p (a b c) -> p a b c", a=shape[1], b=shape[2])
        return v


def build_program(debug=None):
    nc = bass.Bass("TRN2", target_bir_lowering=False)

    def din(name, shape, dt=F32):
        return nc.dram_tensor(name, list(shape), dt, kind="ExternalInput").ap()

    x_loc = din("x_loc", [NT, D])
    x_halo = din("x_halo", [HAL, D])
    halo_mask = din("halo_mask", [128, HAL])
    cT_d = din("cT", [128, 8])
    pos_d = din("pos", [128, NT], I32)
    w_ada = din("w_ada_q", [D, 2304])
    b_adaT = din("b_adaT_q", [128, 18])
    gvecs = din("gvecs", [128, 32])
    wg_d = [din("w1_gate", [D, DFF]), din("w2_gate", [D, DFF])]
    wu_d = [din("w1_up", [D, DFF]), din("w2_up", [D, DFF])]
    wd_d = [din("w1_down", [DFF, D]), din("w2_down", [DFF, D])]
    NFM = 27
    w_in_fm = din("w_in_fm", [D, NFM * 128])
    w_in_tm = din("w_in_tm", [D, 280])
    conv_wT = din("conv_wT", [128, 12])
    g_oT = din("g_oT", [128, 8])
    w_out_d = din("w_out", [D, D])
    w1r_d = [din("w1k_r", [128, 32 * 256]), din("w1v_r", [128, 32 * 256])]
    peT_d = din("peT", [128, 64])
    w2kpad_d = din("w2kpad", [128, 2 * 2 * 128])
    w2v_d = din("w2v", [128, 2 * 64])
    freqs_d = din("freqs", [128, 2])
    selstat_d = din("selstat", [128, 4 * 128])
    wmask_d = din("wmask", [128, 8 * 128])
    cmask_d = din("cmask", [128, 19 * 128])
    keepadd_d = din("keepadd", [128, 2 * NJ * 128])
    ovl_d = din("ovl", [128, 4 * 128])
    out_loc = nc.dram_tensor("out_loc", [NT, D], F32, kind="ExternalOutput").ap()
    dbg = None
    if debug is not None:
        dbg = nc.dram_tensor("dbg", [128, debug[1]], F32, kind="ExternalOutput").ap()
    xg_in = [nc.dram_tensor("xg_in%d" % i, [256, NT], BF16) for i in range(3)]
    xg_out = [nc.dram_tensor("xg_out%d" % i, [4 * 256, NT], BF16) for i in range(3)]
    xsave = nc.dram_tensor("xsave", [128, 8 * NT], F32)
    mg_in = nc.dram_tensor("mg_in", [18, 128], F32)
    mg_out = nc.dram_tensor("mg_out", [72, 128], F32)
    selD = [nc.dram_tensor("selD%d" % i, [2, 64 * 128], BF16) for i in range(2)]

    st = ExitStack()
    with st:
        def T(name, shape, dt):
            return st.enter_context(nc.sbuf_tensor(name, list(shape), dt))

        AW = 52000
        arena_t = T("arena", [128, AW], F32)
        AR = Arena(arena_t[:, :], AW)
        ident32 = T("ident32", [128, 128], F32)
        identb = T("identb", [128, 128], BF16)
        onesb = T("onesb", [128, 128], BF16)
        bdb = T("bdb", [128, 128], BF16)
        vecs = T("vecs", [128, 160], F32)
        gv = T("gv", [128, 32], F32)
        cT = T("cTt", [128, 16], F32)
        convw = T("convw", [128, 12], F32)
        goT = T("goT", [128, 8], F32)
        frq = T("frq", [128, 4], F32)
        hmask = T("hmask", [128, HAL], F32)
        epsc = T("epsc", [128, 2], F32)
        mqT = T("mqT", [128, 128], F32)
        mgT = T("mgT", [128, 128], F32)
        ps = [st.enter_context(nc.psum_tensor("ps%d" % i, [128, 512], F32)) for i in range(7)]
        pst = st.enter_context(nc.psum_tensor("pst", [128, 1024], BF16))

        P = Prog(nc)
        _cnt = [0]

        def uid(p):
            _cnt[0] += 1
            return "%s#%d" % (p, _cnt[0])

        def dma(q, out, in_, reads, writes, semkey=None):
            P.add(q, lambda e: e.dma_start(out=out, in_=in_), reads, writes, dma=True,
                  semkey=semkey or writes[0])

        def mm(out, lhsT, rhs, start, stop, reads, writes, skip=False):
            P.add('pe', lambda e: e.matmul(out, lhsT=lhsT, rhs=rhs, start=start, stop=stop,
                                           skip_group_check=skip), reads, writes)

        def tr(out, in_, ident, reads, writes):
            P.add('pe', lambda e: e.transpose(out=out, in_=in_, identity=ident), reads, writes)

        def act(out, in_, func, reads, writes, scale=1.0, bias=None):
            if bias is None:
                P.add('act', lambda e: e.activation(out=out, in_=in_, func=func, scale=scale), reads, writes)
            else:
                P.add('act', lambda e: e.activation(out=out, in_=in_, func=func, scale=scale, bias=bias),
                      reads, writes)

        def tt(eng, out, in0, in1, op, reads, writes):
            P.add(eng, lambda e: e.tensor_tensor(out=out, in0=in0, in1=in1, op=op), reads, writes)

        def ts(eng, out, in0, s1, s2, op0, op1, reads, writes):
            if s2 is None:
                P.add(eng, lambda e: e.tensor_scalar(out=out, in0=in0, scalar1=s1, scalar2=None, op0=op0),
                      reads, writes)
            else:
                P.add(eng, lambda e: e.tensor_scalar(out=out, in0=in0, scalar1=s1, scalar2=s2, op0=op0, op1=op1),
                      reads, writes)

        def stt(eng, out, in0, scalar, in1, op0, op1, reads, writes):
            P.add(eng, lambda e: e.scalar_tensor_tensor(out=out, in0=in0, scalar=scalar, in1=in1, op0=op0, op1=op1),
                  reads, writes)

        def cp(eng, out, in_, reads, writes):
            P.add(eng, lambda e: e.tensor_copy(out=out, in_=in_), reads, writes)

        def memset(eng, ap, val, writes, reads=()):
            P.add(eng, lambda e: e.memset(ap, val), reads, writes)

        def recip(out, in_, reads, writes):
            P.add('dve', lambda e: e.reciprocal(out=out, in_=in_), reads, writes)

        memset('pool', ident32[:], 0.0, ['ident32'])
        P.add('pool', lambda e: e.affine_select(out=ident32[:], in_=ident32[:], pattern=[[-1, 128]],
                                                compare_op=ALU.not_equal, fill=1.0, base=0, channel_multiplier=1),
              ['ident32'], ['ident32'])
        cp('dve', identb[:], ident32[:], ['ident32'], ['identb'])
        memset('pool', onesb[:], 1.0, ['onesb'])
        memset('pool', bdb[:], 0.0, ['bdb'])
        memset('pool', bdb[0:64, 0:64], 1.0, ['bdb'], ['bdb'])
        memset('pool', bdb[64:128, 64:128], 1.0, ['bdb'], ['bdb'])
        memset('pool', epsc[:], EPS, ['epsc'])
        dma('sp', gv[:], gvecs, [], ['gv'])
        dma('sp', cT[:, 0:8], cT_d, [], ['cT'])
        dma('sp', convw[:], conv_wT, [], ['convw'])
        dma('sp', goT[:], g_oT, [], ['goT'])
        dma('sp', frq[:, 0:2], freqs_d, [], ['frq'])
        dma('sp', hmask[:], halo_mask, [], ['hmask'])
        dma('sp', vecs[:, 128:146], b_adaT, [], ['modq'])

        xT = AR.alloc([128, 8, NT], F32)
        xhT = AR.alloc([128, 8, HAL], F32)
        M_R1 = AR.mark()

        m0 = AR.mark()
        xtm = [AR.alloc([128, D], F32) for _ in range(2)]
        xhm = AR.alloc([128, D], F32)
        wab = [AR.alloc([128, 2304], F32) for _ in range(2)]
        for j in range(NJ):
            b = j % 2
            dma('sp', xtm[b], x_loc[j * 128:(j + 1) * 128, :], [], ['xtm%d' % b])
            for hf in range(2):
                pb = ps[hf]
                for k4 in range(4):
                    kc = hf * 4 + k4
                    tr(pb[:, k4 * 128:(k4 + 1) * 128], xtm[b][:, kc * 128:(kc + 1) * 128], ident32[:],
                       ['xtm%d' % b, 'ident32'], ['ps%d' % hf])
                eng = 'act' if hf == 0 else 'dve'
                if eng == 'act':
                    act(xT[:, hf * 4:(hf + 1) * 4, j * 128:(j + 1) * 128],
                        pb[:, :].rearrange("p (k t) -> p k t", k=4), AF.Copy, ['ps%d' % hf], ['xT'])
                else:
                    cp('dve', xT[:, hf * 4:(hf + 1) * 4, j * 128:(j + 1) * 128],
                       pb[:, :].rearrange("p (k t) -> p k t", k=4), ['ps%d' % hf], ['xT'])
        dma('sp', xhm[0:HAL, :], x_halo, [], ['xhm'])
        for kc in range(8):
            tr(ps[2][:, kc * HAL:(kc + 1) * HAL], xhm[0:HAL, kc * 128:(kc + 1) * 128], ident32[0:HAL, 0:HAL],
               ['xhm', 'ident32'], ['ps2'])
        cp('dve', xhT[:, :, :], ps[2][:, 0:8 * HAL].rearrange("p (k t) -> p k t", k=8), ['ps2'], ['xhT'])

        act(cT[:, 8:16], cT[:, 0:8], AF.Silu, ['cT'], ['cact'])
        for kc in range(8):
            b = kc % 2
            dma('sp', wab[b], w_ada[kc * 128:(kc + 1) * 128, :], [], ['wab%d' % b])
            for m in range(18):
                mm(ps[3][:, m:m + 1], wab[b][:, m * 128:(m + 1) * 128], cT[:, 8 + kc:9 + kc], True, True,
                   ['wab%d' % b, 'cact'], ['ps3'])
            tt('dve', vecs[:, 128:146], vecs[:, 128:146], ps[3][:, 0:18], ALU.add, ['ps3', 'modq'], ['modq'])
        tr(ps[3][0:18, 128:256], vecs[:, 128:146], ident32[:], ['modq', 'ident32'], ['ps3'])
        cp('dve', mqT[0:18, :], ps[3][0:18, 128:256], ['ps3'], ['mqT'])
        dma('sp', mg_in.ap(), mqT[0:18, :], ['mqT'], ['mg_in'])
        P.add('pool', lambda e: e.collective_compute("AllGather", ALU.bypass,
                                                     replica_groups=[[0, 1, 2, 3], [4, 5, 6, 7]],
                                                     ins=[mg_in.ap().opt()], outs=[mg_out.ap().opt()]),
              ['mg_in'], ['mg_out'])
        dma('sp', mgT[0:72, :], mg_out.ap(), ['mg_out'], ['mgT'])
        tr(ps[3][:, 256:328], mgT[0:72, :], ident32[0:72, 0:72], ['mgT', 'ident32'], ['ps3'])
        cp('dve', vecs[:, 0:72], ps[3][:, 256:328], ['ps3'], ['vecs'])
        for i in range(3):
            stt('dve', vecs[:, 72 + 8 * i:80 + 8 * i], vecs[:, 24 * i + 8:24 * i + 16], 1.0, gv[:, 8 * i:8 * i + 8],
                ALU.add, ALU.mult, ['vecs', 'gv'], ['vecs'])
            ts('dve', vecs[:, 96 + 8 * i:104 + 8 * i], vecs[:, 24 * i + 16:24 * i + 24],
               0.5 if i != 1 else 1.0, None, ALU.mult, None, ['vecs'], ['vecs'])

        def A_(i):
            return vecs[:, 72 + 8 * i:80 + 8 * i]

        def SH_(i):
            return vecs[:, 24 * i:24 * i + 8]

        def GT_(i):
            return vecs[:, 96 + 8 * i:104 + 8 * i]

        AR.release(m0)
        P.barrier()

        def norm_mod(src, srckey, n, av, shv, dst, dstkey, sqb, rsb, tmpn, psb, psk):
            act(sqb[:, :, 0:n], src, AF.Square, [srckey], ['sqb'])
            for kc in range(8):
                mm(psb[:, 0:n], onesb[:], sqb[:, kc, 0:n], kc == 0, kc == 7, ['sqb', 'onesb'], [psk])
            act(rsb[:, 0:n], psb[:, 0:n], AF.Sqrt, [psk, 'epsc'], ['rsb'], scale=1.0 / D, bias=epsc[:, 0:1])
            recip(rsb[:, 0:n], rsb[:, 0:n], ['rsb'], ['rsb'])
            if dst is None:
                return
            for kc in range(8):
                tb = tmpn[kc % 2]
                stt('dve', tb[:, 0:n], src[:, kc, :], av[:, kc:kc + 1], rsb[:, 0:n], ALU.mult, ALU.mult,
                    [srckey, 'rsb', 'vecs'], ['tmpn%d' % (kc % 2)])
                act(dst[:, kc, :], tb[:, 0:n], AF.Identity, ['tmpn%d' % (kc % 2), 'vecs'], [dstkey],
                    bias=shv[:, kc:kc + 1])

        RT = Arena(arena_t[:, 45856:AW], AW - 45856)
        Ctab = RT.alloc([128, NT], F32)
        Stab = RT.alloc([128, NT], F32)
        posi = RT.alloc([128, 512], I32)
        angb = RT.alloc([128, 512], F32)
        tqb = RT.alloc([128, 512], F32)
        kib = RT.alloc([128, 512], I32)

        def emit_rope_tables():
            TWO_PI = float(2 * np.pi)

            def range_reduce(y, key):
                ts('dve', tqb[:], y, 1.0 / TWO_PI, None, ALU.mult, None, [key], ['tqb'])
                cp('dve', kib[:], tqb[:], ['tqb'], ['kib'])
                cp('dve', tqb[:], kib[:], ['kib'], ['tqb'])
                stt('dve', y, tqb[:], -TWO_PI, y, ALU.mult, ALU.add, ['tqb', key], [key])
                ts('dve', tqb[:], y, float(np.pi), -TWO_PI, ALU.is_gt, ALU.mult, [key], ['tqb'])
                tt('dve', y, y, tqb[:], ALU.add, [key, 'tqb'], [key])
                ts('dve', tqb[:], y, float(-np.pi), TWO_PI, ALU.is_lt, ALU.mult, [key], ['tqb'])
                tt('dve', y, y, tqb[:], ALU.add, [key, 'tqb'], [key])

            for t in range(4):
                dma('sp', posi[:], pos_d[:, t * 512:(t + 1) * 512], [], ['posi'])
                cp('dve', angb[:], posi[:], ['posi'], ['angb'])
                ts('dve', angb[:], angb[:], frq[:, 0:1], None, ALU.mult, None, ['angb', 'frq'], ['angb'])
                cp('dve', Stab[:, t * 512:(t + 1) * 512], angb[:], ['angb'], ['Stab'])
                ts('dve', angb[:], angb[:], float(np.pi / 2), None, ALU.add, None, ['angb'], ['angb'])
                range_reduce(angb[:], 'angb')
                act(Ctab[:, t * 512:(t + 1) * 512], angb[:], AF.Sin, ['angb'], ['Ctab'])
                cp('dve', angb[:], Stab[:, t * 512:(t + 1) * 512], ['Stab'], ['angb'])
                range_reduce(angb[:], 'angb')
                act(angb[:], angb[:], AF.Sin, ['angb'], ['angb'])
                ts('dve', Stab[:, t * 512:(t + 1) * 512], angb[:], frq[:, 1:2], None, ALU.mult, None,
                   ['angb', 'frq'], ['Stab'])

        def ffn(li, av, shv, gatev, with_halo, mid_hook=None):
            m = AR.mark()
            W = 1024 + (HAL if with_halo else 0)
            AT = AR.alloc([128, NF, 1056], BF16)
            hT = AR.alloc([128, 8, 1056], BF16)
            wgb = [AR.alloc([128, 8, 256], BF16) for _ in range(2)]
            wub = [AR.alloc([128, 8, 256], BF16) for _ in range(2)]
            wdb = [AR.alloc([128, NF, 128], BF16) for _ in range(2)]
            sqb = AR.alloc([128, 8, 512], BF16)
            rsb = AR.alloc([128, 512], F32)
            tmpn = [AR.alloc([128, 512], F32) for _ in range(2)]
            sgt = [AR.alloc([128, 512], F32) for _ in range(2)]
            wg_v = wg_d[li].rearrange("(k p) c -> p k c", p=128)
            wu_v = wu_d[li].rearrange("(k p) c -> p k c", p=128)
            wd_v = wd_d[li].rearrange("(f p) d -> p f d", p=128)
            for pi in range(2):
                tiles = []
                off = 0
                if pi == 0 and with_halo:
                    tiles.append(('h', 0, off, HAL))
                    off += HAL
                for t2 in range(2):
                    tiles.append(('m', pi * 1024 + t2 * 512, off, 512))
                    off += 512

                def xsrc(tl):
                    kind, c0, o, nn = tl
                    if kind == 'h':
                        return xhT[:, :, :], 'xhT'
                    return xT[:, :, c0:c0 + nn], 'xT'

                for ti, tl in enumerate(tiles):
                    src, sk = xsrc(tl)
                    norm_mod(src, sk, tl[3], av, shv, hT[:, :, tl[2]:tl[2] + tl[3]], 'hT%d' % ti,
                             sqb, rsb, tmpn, ps[6], 'ps6')
                side_ops = []
                if pi == 0 and mid_hook is not None:
                    saved_ops = P.ops
                    P.ops = []
                    mid_hook()
                    side_ops = P.ops
                    P.ops = saved_ops

                def load_gu(fg):
                    b = fg % 2
                    dma('pool', wgb[b], wg_v[:, :, fg * 256:(fg + 1) * 256], [], ['wgb%d' % b])
                    dma('pool', wub[b], wu_v[:, :, fg * 256:(fg + 1) * 256], [], ['wub%d' % b])

                def load_wd(dc):
                    b = dc % 2
                    dma('pool', wdb[b], wd_v[:, :, dc * 128:(dc + 1) * 128], [], ['wdb%d' % b])

                load_gu(0)
                load_gu(1)
                cnt = 0
                for fg in range(11):
                    b = fg % 2
                    for fc in range(2):
                        f = fg * 2 + fc
                        for ti, tl in enumerate(tiles):
                            _, _, o, nn = tl
                            pg = cnt % 2
                            pu = 2 + cnt % 2
                            cnt += 1
                            for kc in range(8):
                                mm(ps[pg][:, 0:nn], wgb[b][:, kc, fc * 128:(fc + 1) * 128], hT[:, kc, o:o + nn],
                                   kc == 0, kc == 7, ['wgb%d' % b, 'hT%d' % ti], ['ps%d' % pg])
                            for kc in range(8):
                                mm(ps[pu][:, 0:nn], wub[b][:, kc, fc * 128:(fc + 1) * 128], hT[:, kc, o:o + nn],
                                   kc == 0, kc == 7, ['wub%d' % b, 'hT%d' % ti], ['ps%d' % pu])
                            sb = sgt[cnt % 2]
                            act(sb[:, 0:nn], ps[pg][:, 0:nn], AF.Silu, ['ps%d' % pg], ['sgt%d' % (cnt % 2)])
                            tt('dve', AT[:, f, o:o + nn], ps[pu][:, 0:nn], sb[:, 0:nn], ALU.mult,
                               ['ps%d' % pu, 'sgt%d' % (cnt % 2)], ['AT%d_%d' % (f, ti)])
                            for _ in range(2):
                                if side_ops:
                                    P.ops.append(side_ops.pop(0))
                    if fg + 2 < 11:
                        load_gu(fg + 2)
                    if fg == 8:
                        load_wd(0)
                        load_wd(1)
                P.ops.extend(side_ops)
                side_ops = []
                cnt = 0
                for dc in range(8):
                    b = dc % 2
                    for ti, tl in enumerate(tiles):
                        kind, c0, o, nn = tl
                        pb = 4 + cnt % 2
                        cnt += 1
                        for f in range(NF):
                            mm(ps[pb][:, 0:nn], wdb[b][:, f, :], AT[:, f, o:o + nn], f == 0, f == NF - 1,
                               ['wdb%d' % b, 'AT%d_%d' % (f, ti)], ['ps%d' % pb])
                        if kind == 'h':
                            stt('dve', xhT[:, dc, :], ps[pb][:, 0:nn], gatev[:, dc:dc + 1], xhT[:, dc, :],
                                ALU.mult, ALU.add, ['ps%d' % pb, 'xhT', 'vecs'], ['xhT'])
                        else:
                            stt('dve', xT[:, dc, c0:c0 + nn], ps[pb][:, 0:nn], gatev[:, dc:dc + 1],
                                xT[:, dc, c0:c0 + nn], ALU.mult, ALU.add, ['ps%d' % pb, 'xT', 'vecs'], ['xT'])
                    if dc + 2 < 8:
                        load_wd(dc + 2)
            AR.release(m)
            P.barrier()

        def dump(ap, key, ncols):
            if ap.dtype != F32:
                raise ValueError
            dma('sp', dbg[:, 0:ncols], ap, [key], ['dbg'])

        ffn(0, A_(0), SH_(0), GT_(0), True, mid_hook=emit_rope_tables)
        if debug and debug[0] == 'x1':
            dump(xT[:, :, :].rearrange("p k t -> p (k t)"), 'xT', 8 * NT)

        stop_after = debug[2] if debug else None
        if stop_after == 'ffn1':
            P.finalize(st)
            return nc

        qT = AR.alloc([128, 4, NT], BF16)
        ycT = AR.alloc([128, 4, NT], BF16)
        gts = AR.alloc([128, NJ, 24], F32)
        M_R2 = AR.mark()
        WT = NT + HAL
        hT2 = AR.alloc([128, 8, WT], BF16)
        wfm = [AR.alloc([128, 8, 128], BF16) for _ in range(4)]
        wtm = AR.alloc([128, 8, 280], BF16)
        yt = [AR.alloc([128, 512], F32) for _ in range(2)]
        M_A = AR.mark()
        sqb = AR.alloc([128, 8, 512], BF16)
        rsb = AR.alloc([128, 512], F32)
        tmpn = [AR.alloc([128, 512], F32) for _ in range(2)]

        norm_mod(xhT[:, :, :], 'xhT', HAL, A_(1), SH_(1), hT2[:, :, 0:HAL], 'hT2', sqb, rsb, tmpn, ps[6], 'ps6')
        for t in range(4):
            norm_mod(xT[:, :, t * 512:(t + 1) * 512], 'xT', 512, A_(1), SH_(1),
                     hT2[:, :, HAL + t * 512:HAL + (t + 1) * 512], 'hT2', sqb, rsb, tmpn, ps[6], 'ps6')
        dma('sp', xsave.ap(), xT[:, :, :].rearrange("p k t -> p (k t)"), ['xT'], ['xsave'])

        if stop_after == 'A':
            dump(Ctab[:, :], 'Ctab', NT)
            P.finalize(st)
            return nc
        P.barrier()
        AR.release(M_A)
        kst = [AR.alloc([128, NT], BF16) for _ in range(2)]
        Vloc = AR.alloc([128, NJ, 256], BF16)
        wfm_v = w_in_fm.rearrange("(k p) c -> p k c", p=128)
        nload = [0]

        hw_path = [False]
        wst = [Arena(arena_t[:, 49952 + 1024 * i_:49952 + 1024 * (i_ + 1)], 1024).alloc([128, 8, 128], F32)
               for i_ in range(2)]

        def load_fm(ch):
            b = nload[0] % 4
            nload[0] += 1
            if hw_path[0]:
                sb_ = nload[0] % 2
                dma('sp', wst[sb_], wfm_v[:, :, ch * 128:(ch + 1) * 128], [], ['wst%d' % sb_])
                act(wfm[b][:, :, :], wst[sb_][:, :, :], AF.Copy, ['wst%d' % sb_], ['wfm%d' % b])
            else:
                dma('pool', wfm[b], wfm_v[:, :, ch * 128:(ch + 1) * 128], [], ['wfm%d' % b])
            return b

        pcnt = [0]

        def proj(b, o, nn, pbi):
            for kc in range(8):
                mm(ps[pbi][:, 0:nn], wfm[b][:, kc, :], hT2[:, kc, o:o + nn], kc == 0, kc == 7,
                   ['wfm%d' % b, 'hT2'], ['ps%d' % pbi])

        def rope_chunk(ch_x, ch_p, dst_fn, dstkey):
            bx = load_fm(ch_x)
            bp = load_fm(ch_p)
            for t in range(4):
                pa = t % 2
                pb = 2 + t % 2
                proj(bx, HAL + t * 512, 512, pa)
                proj(bp, HAL + t * 512, 512, pb)
                t1 = yt[0]
                t2 = yt[1]
                tt('dve', t1[:], ps[pa][:, :], Ctab[:, t * 512:(t + 1) * 512], ALU.mult, ['ps%d' % pa, 'Ctab'], ['yt0'])
                tt('dve', t2[:], ps[pb][:, :], Stab[:, t * 512:(t + 1) * 512], ALU.mult, ['ps%d' % pb, 'Stab'], ['yt1'])
                tt('pool', dst_fn(t), t1[:], t2[:], ALU.add, ['yt0', 'yt1'], [dstkey])

        xg_in_ap = [t_.ap() for t_ in xg_in]
        dma('pool', wtm[:], w_in_tm.rearrange("(k p) c -> p k c", p=128), [], ['wtm'])
        for j in range(NJ):
            pb = 4 + j % 2
            for kc in range(8):
                mm(ps[pb][:, 0:280], hT2[:, kc, HAL + j * 128:HAL + (j + 1) * 128], wtm[:, kc, :], kc == 0, kc == 7,
                   ['hT2', 'wtm'], ['ps%d' % pb])
            act(Vloc[:, j, :], ps[pb][:, 0:256], AF.Copy, ['ps%d' % pb], ['Vloc'])
            act(gts[:, j, :], ps[pb][:, 256:280], AF.Sigmoid, ['ps%d' % pb], ['gts'])
        dma('sp', xg_in_ap[2][:, :].rearrange("r (c f) -> (r c) f", f=256).rearrange("(j p) f -> p j f", p=128),
            Vloc[:, :, :], ['Vloc'], ['xg_in'], semkey='xg_in_w')
        P.add('pool', lambda e: e.collective_compute("AllGather", ALU.bypass,
                                                     replica_groups=[[0, 1, 2, 3], [4, 5, 6, 7]],
                                                     ins=[xg_in[2].ap().opt()], outs=[xg_out[2].ap().opt()]),
              ['xg_in'], ['xg_out'])
        for ki, (chx, chp) in enumerate([(20, 21), (22, 23), (24, 25)]):
            sbuf = kst[ki % 2]
            sk = 'kst%d' % (ki % 2)
            rope_chunk(chx, chp, lambda t, sbuf=sbuf: sbuf[:, t * 512:(t + 1) * 512], sk)
            dma('sp', xg_in_ap[ki // 2][(ki % 2) * 128:(ki % 2 + 1) * 128, :], sbuf[:], [sk], ['xg_in'], semkey='xg_in_w')
        bv = load_fm(26)
        sbuf = kst[1]
        for t in range(4):
            pa = t % 2
            proj(bv, HAL + t * 512, 512, pa)
            act(sbuf[:, t * 512:(t + 1) * 512], ps[pa][:, :], AF.Copy, ['ps%d' % pa], ['kst1'])
        dma('sp', xg_in_ap[1][128:256, :], sbuf[:], ['kst1'], ['xg_in'], semkey='xg_in_w')
        P.add('pool', lambda e: e.collective_compute("AllGather", ALU.bypass,
                                                     replica_groups=[[0, 1, 2, 3], [4, 5, 6, 7]],
                                                     ins=[xg_in[0].ap().opt()], outs=[xg_out[0].ap().opt()]),
              ['xg_in'], ['xg_out'])
        P.add('pool', lambda e: e.collective_compute("AllGather", ALU.bypass,
                                                     replica_groups=[[0, 1, 2, 3], [4, 5, 6, 7]],
                                                     ins=[xg_in[1].ap().opt()], outs=[xg_out[1].ap().opt()]),
              ['xg_in'], ['xg_out'])
        AR0 = Arena(arena_t[:, 0:8 * NT + 8 * HAL], 8 * NT + 8 * HAL)
        ksT = AR0.alloc([128, S], BF16)
        kwT = AR0.alloc([128, S], BF16)
        Vall = AR0.alloc([128, 64, 4, 65], BF16)
        xgo = [t_.ap() for t_ in xg_out]
        GR = 4

        def src_fm(row0, rr):
            bi, r0 = row0 // 256, row0 % 256
            return xgo[bi][rr * 256 + r0:rr * 256 + r0 + 128, :].rearrange("p (j i) -> p j i", i=128)

        hw_path[0] = True
        AR.release(M_A)
        ccs = AR.alloc([128, WT], F32)
        uT = AR.alloc([128, NJ, 130], F32)
        vT = AR.alloc([128, NJ, 128], F32)
        ysq = AR.alloc([128, 512], BF16)
        yrs = AR.alloc([128, 512], F32)
        assert AR.top <= 45856, AR.top
        for i in range(4):
            bcc = load_fm(4 + i)
            bcx = load_fm(8 + i)
            bcb = load_fm(i)
            proj(bcc, 0, HAL, 0)
            act(ccs[:, 0:HAL], ps[0][:, 0:HAL], AF.Copy, ['ps0'], ['ccs'] + (['kst0', 'kst1', 'Vloc'] if i == 0 else []))
            for t in range(4):
                pb = t % 2
                proj(bcc, HAL + t * 512, 512, pb)
                act(ccs[:, HAL + t * 512:HAL + (t + 1) * 512], ps[pb][:, :], AF.Copy, ['ps%d' % pb], ['ccs'])
            proj(bcx, 0, HAL, 2)
            tt('dve', uT[:, :, 0:2], ps[2][:, 0:HAL].rearrange("p (j c) -> p j c", c=2),
               ccs[:, 0:HAL].rearrange("p (j c) -> p j c", c=2), ALU.mult, ['ps2', 'ccs'], ['uT'])
            tt('dve', uT[:, :, 0:2], uT[:, :, 0:2], hmask[:, :].rearrange("p (j c) -> p j c", c=2), ALU.mult,
               ['uT', 'hmask'], ['uT'])
            for t in range(4):
                pb = 2 + t % 2
                proj(bcx, HAL + t * 512, 512, pb)
                tt('dve', uT[:, 4 * t:4 * t + 4, 2:130], ps[pb][:, :].rearrange("p (j c) -> p j c", c=128),
                   ccs[:, HAL + t * 512:HAL + (t + 1) * 512].rearrange("p (j c) -> p j c", c=128), ALU.mult,
                   ['ps%d' % pb, 'ccs'], ['uT'])
            ts('dve', vT[:, :, :], uT[:, :, 2:130], convw[:, 3 * i + 2:3 * i + 3], None, ALU.mult, None,
               ['uT', 'convw'], ['vT'])
            stt('dve', vT[:, :, :], uT[:, :, 1:129], convw[:, 3 * i + 1:3 * i + 2], vT[:, :, :], ALU.mult, ALU.add,
                ['uT', 'convw', 'vT'], ['vT'])
            stt('dve', vT[:, :, :], uT[:, :, 0:128], convw[:, 3 * i:3 * i + 1], vT[:, :, :], ALU.mult, ALU.add,
                ['uT', 'convw', 'vT'], ['vT'])
            for t in range(4):
                pb = 4 + t % 2
                proj(bcb, HAL + t * 512, 512, pb)
                yb = yt[t % 2]
                yk = 'yt%d' % (t % 2)
                tt('dve', yb[:], ps[pb][:, :], vT[:, 4 * t:4 * t + 4, :].rearrange("p j c -> p (j c)"), ALU.mult,
                   ['ps%d' % pb, 'vT'], [yk])
                act(ysq[:], yb[:], AF.Square, [yk], ['ysq'])
                mm(ps[6][:, :], bdb[:], ysq[:], True, True, ['ysq', 'bdb'], ['ps6'])
                act(yrs[:], ps[6][:, :], AF.Sqrt, ['ps6', 'epsc'], ['yrs'], scale=1.0 / 64, bias=epsc[:, 0:1])
                recip(yrs[:], yrs[:], ['yrs'], ['yrs'])
                stt('dve', ycT[:, i, t * 512:(t + 1) * 512], yb[:], goT[:, i:i + 1], yrs[:], ALU.mult, ALU.mult,
                    [yk, 'yrs', 'goT'], ['ycT'])

        for hg in range(4):
            rope_chunk(12 + hg, 16 + hg, lambda t, hg=hg: qT[:, hg, t * 512:(t + 1) * 512], 'qT')
        if debug and debug[0] == 'qT':
            cp('dve', Ctab[:, :], qT[:, 0, :], ['qT'], ['Ctab'])
            dump(Ctab[:, :], 'Ctab', NT)
        if debug and debug[0] == 'yc':
            cp('dve', Ctab[:, :], ycT[:, 0, :], ['ycT'], ['Ctab'])
            dump(Ctab[:, :], 'Ctab', NT)
        if stop_after == 'tm':
            P.finalize(st)
            return nc
        AR.release(M_R2)
        P.barrier()
        if stop_after == 'inproj':
            P.finalize(st)
            return nc

        yaT = AR.alloc([128, 4, NT], BF16)
        selstat = AR.alloc([128, 4, 128], BF16)
        wmaskt = AR.alloc([128, 8, 128], BF16)
        cmaskt = AR.alloc([128, 19, 128], BF16)
        keepadd = AR.alloc([128, 2 * NJ, 128], BF16)
        kccT = AR.alloc([128, 512], BF16)
        rhsc = AR.alloc([128, 4, 2, 194], BF16)
        Ebuf = [AR.alloc([128, 512], BF16) for _ in range(13)]
        oaccs = [AR.alloc([128, 4, 64], F32) for _ in range(2)]
        imp = AR.alloc([128, 128], F32)
        sct = AR.alloc([128, 128], F32)
        sct2 = AR.alloc([128, 128], F32)
        m8a = AR.alloc([128, 8], F32)
        m8b = AR.alloc([128, 8], F32)
        selb = AR.alloc([128, 128], BF16)
        selT = [AR.alloc([128, 128], BF16) for _ in range(2)]
        zt = AR.alloc([128, 16], F32)
        osq = AR.alloc([128, 4, 64], F32)
        yatm = AR.alloc([128, 512], BF16)
        M_C = AR.mark()

        dma('pool', selstat[:, :, :].rearrange("p a b -> p (a b)"), selstat_d, [], ['selstat'])
        dma('pool', wmaskt[:, :, :].rearrange("p a b -> p (a b)"), wmask_d, [], ['wmaskt'])
        dma('pool', cmaskt[:, :, :].rearrange("p a b -> p (a b)"), cmask_d, [], ['cmaskt'])
        dma('pool', keepadd[:, :, :].rearrange("p a b -> p (a b)"), keepadd_d, [], ['keepadd'])
        memset('pool', rhsc[:, :, :, :], 0.0, ['rhsc'])
        memset('pool', rhsc[:, :, :, 64:65], 1.0, ['rhsc'], ['rhsc'])
        for g in range(2):
            dma('pool', rhsc[:, :, g, 65:193], ovl_d.rearrange("p (c j) -> p c j", j=128), ['rhsc'], ['rhsc'])
        memset('pool', kccT[:, :], 0.0, ['kccT'])

        mc0 = AR.mark()
        srcT = AR.alloc([128, S], BF16)
        w1r = AR.alloc([128, 32, 256], BF16)
        hid = [AR.alloc([128, 2, 512], BF16) for _ in range(2)]
        peb = AR.alloc([128, 64], BF16)
        w2kp = AR.alloc([128, 2, 2, 128], BF16)
        w2vb = AR.alloc([128, 2, 64], BF16)
        biasT = AR.alloc([128, 4], F32)
        dma('pool', peb[:, :], peT_d, [], ['peb'])
        dma('pool', w2kp[:, :, :, :].rearrange("p a b c -> p (a b c)"), w2kpad_d, [], ['w2kp'])
        dma('pool', w2vb[:, :, :].rearrange("p a b -> p (a b)"), w2v_d, [], ['w2vb'])
        for kv in range(2):
            row0 = 0 if kv == 0 else 384
            for rr in range(GR):
                dma('sp', srcT[:, :].rearrange("p (j r i) -> p j r i", r=4, i=128)[:, :, rr, :], src_fm(row0, rr),
                    ['xg_out'], ['srcT'])
            dma('pool', w1r[:, :, :].rearrange("p a b -> p (a b)"), w1r_d[kv], [], ['w1r'])
            if kv == 0:
                memset('pool', Vall[:, :, :, 64:65], 1.0, ['Vall', 'xT', 'xhT'])
                for rr in range(GR):
                    dma('sp', ksT[:, :].rearrange("p (j r i) -> p j r i", r=4, i=128)[:, :, rr, :], src_fm(128, rr),
                        ['xg_out'], ['ksT', 'xT', 'xhT'])
                    dma('sp', kwT[:, :].rearrange("p (j r i) -> p j r i", r=4, i=128)[:, :, rr, :], src_fm(256, rr),
                        ['xg_out'], ['kwT', 'xT', 'xhT'])
                    vsrc = xgo[2][rr * 256:rr * 256 + 256, :].rearrange("r (c f) -> (r c) f", f=256) \
                        .rearrange("(j p) (t d) -> p j t d", p=128, d=64)
                    vdst = Vall[:, :, :, :].rearrange("p (j r) t d -> p j r t d", r=4)
                    for tg in range(4):
                        dma('sp', vdst[:, :, rr, tg, 0:64], vsrc[:, :, tg, :], ['xg_out'], ['Vall', 'xT', 'xhT'])
            for mc in range(2):
                for l in range(32):
                    mm(ps[6][:, mc:mc + 1], w1r[0:64, l, mc * 128:(mc + 1) * 128], peb[0:64, 32 * kv + l:32 * kv + l + 1],
                       l == 0, l == 31, ['w1r', 'peb'], ['ps6'])
            cp('dve', biasT[:, 2 * kv:2 * kv + 2], ps[6][:, 0:2], ['ps6'], ['biasT'])
            for mc in range(2):
                for l in range(32):
                    for g in range(2):
                        rhs = srcT[64 * g:64 * g + 64, l:l + 16 * 510 + 1:16]
                        mm(ps[g][:, 0:511], w1r[64 * g:64 * g + 64, l, mc * 128:(mc + 1) * 128], rhs,
                           l == 0, l == 31, ['w1r', 'srcT'], ['ps%d' % g])
                for g in range(2):
                    act(hid[g][:, mc, 0:511], ps[g][:, 0:511], AF.Silu, ['ps%d' % g, 'biasT'], ['hid%d' % g],
                        bias=biasT[:, 2 * kv + mc:2 * kv + mc + 1])
            if kv == 0:
                n4 = 0
                for g in range(2):
                    for mc in range(2):
                        mm(ps[2][:, 0:511], w2kp[:, mc, g, :], hid[g][:, mc, 0:511], n4 == 0, n4 == 3,
                           ['w2kp', 'hid%d' % g], ['ps2'])
                        n4 += 1
                act(kccT[:, 0:511], ps[2][:, 0:511], AF.Copy, ['ps2', 'kccT'], ['kccT'])
            else:
                for g in range(2):
                    for c4 in range(4):
                        nn = min(128, 511 - 128 * c4)
                        pb = 2 + (c4 % 2)
                        for mc in range(2):
                            mm(ps[pb][0:nn, 0:64], hid[g][:, mc, 128 * c4:128 * c4 + nn], w2vb[:, mc, :], mc == 0, mc == 1,
                               ['hid%d' % g, 'w2vb'], ['ps%d' % pb])
                        act(rhsc[0:nn, c4, g, 0:64], ps[pb][0:nn, 0:64], AF.Copy, ['ps%d' % pb, 'rhsc'], ['rhsc'])
        AR.release(mc0)
        P.barrier()
        wo = AR.alloc([128, 8, D], BF16)
        maskexp = [AR.alloc([128, 64, 128], BF16) for _ in range(2)]
        selsb = AR.alloc([128, 2, 128], BF16)
        dma('pool', wo[:, :, :], w_out_d.rearrange("(m p) d -> p m d", p=128), [], ['wo'])

        psS = [ps[0], ps[1], ps[2]]
        scnt = [0]
        ecnt = [0]
        mcnt = [0]
        NE = len(Ebuf)

        def emit_scores(item):
            nsb = item.get('nsb', 3)
            pi = scnt[0] % nsb
            scnt[0] += 1
            pk = 'ps%d' % pi
            extra = {'ps3': ['pm0', 'pm1'], 'ps4': ['pm2', 'pm3']}.get(pk, [])
            mm(ps[pi][:, :].rearrange("p (h q) -> p h q", h=4), item['k'], item['q'], True, True,
               [item['kkey'], 'qT'], [pk] + extra)
            eb = ecnt[0] % NE
            ecnt[0] += 1
            ek = 'E%d' % eb
            E = Ebuf[eb]
            dyn = item.get('dyn')
            if dyn is not None:
                mi = mcnt[0] % 2
                mcnt[0] += 1
                mk = 'ps%d' % (3 + mi)
                mreg = ps[3 + mi][:, 0:128]
                mm(mreg, dyn[0], dyn[1], True, True, dyn[2], [mk])
            act(E[:, :], ps[pi][:, :], AF.Exp, [pk], [ek], scale=0.125)
            E3 = E[:, :].rearrange("p (h q) -> p h q", h=4)
            if dyn is not None:
                tt('dve', E3, E3, mreg.unsqueeze(1).to_broadcast([128, 4, 128]), ALU.mult, [ek, mk], [ek])
            for (map_, mkeys) in item.get('stat', []):
                mcnt[0] += 1
                tt('dve', E3, E3, map_.unsqueeze(1).to_broadcast([128, 4, 128]), ALU.mult,
                   [ek] + mkeys, [ek])
            item['E'] = E
            item['ek'] = ek

        def emit_pv(item):
            E, ek = item['E'], item['ek']
            for h in range(4):
                mm(item['out'](h), E[:, h * 128:(h + 1) * 128], item['v'], item['first'] and item['hfirst'](h),
                   item['last'], [ek, item['vkey']], [item['okey'](h)], skip=True)

        def run_pipeline(items, depth=4, pair=1):
            n_ = len(items)
            for i in range(min(depth, n_)):
                emit_scores(items[i])
            for i in range(0, n_, pair):
                for k_ in range(pair):
                    if i + k_ + depth < n_:
                        emit_scores(items[i + k_ + depth])
                for k_ in range(pair):
                    if i + k_ < n_:
                        emit_pv(items[i + k_])
                        if items[i + k_].get('after') is not None:
                            items[i + k_]['after']()

        for j in range(NJ):
            cl = j // 4
            rhs_qs = [qT[64 * g:64 * g + 64, :, j * 128:(j + 1) * 128] for g in range(2)]
            gates = [gts[:, j, 12 * g:12 * g + 12].rearrange("p (h t) -> p h t", t=3) for g in range(2)]
            for g in range(2):
                gsl = slice(64 * g, 64 * g + 64)
                items = []
                for c in range(cl + 1):
                    stat = []
                    if c == cl:
                        stat.append((cmaskt[:, cm_base(j), :], ['cmaskt']))
                    elif c == cl - 1 and j % 4 == 0:
                        stat.append((cmaskt[:, cm_base(j) + 1, :], ['cmaskt']))
                    items.append(dict(
                        k=kccT[gsl, 128 * c:128 * (c + 1)], kkey='kccT', q=rhs_qs[g], stat=stat,
                        v=rhsc[:, c, g, 0:193], vkey='rhsc', first=(c == 0), last=(c == cl),
                        hfirst=lambda h: h % 2 == 0,
                        out=lambda h: ps[3 + h // 2][:, 0:386].rearrange("p (h x) -> p h x", h=2)[:, h % 2, :],
                        okey=lambda h: 'ps%d' % (3 + h // 2)))
                run_pipeline(items)
                for hp in range(2):
                    ts('dve', zt[:, 2 * hp:2 * hp + 2],
                       ps[3 + hp][:, 0:386].rearrange("p (h x) -> p h x", h=2)[:, :, 64], 1e-30, None, ALU.max, None,
                       ['ps%d' % (3 + hp)], ['zt'])
                recip(zt[:, 0:4], zt[:, 0:4], ['zt'], ['zt'])
                tt('dve', zt[:, 4:8], zt[:, 0:4], gates[g][:, :, 0], ALU.mult, ['zt', 'gts'], ['ztg'])
                for h in range(4):
                    pv = ps[3 + h // 2][:, 0:386].rearrange("p (h x) -> p h x", h=2)
                    ts('dve', oaccs[g][:, h, :], pv[:, h % 2, 0:64], zt[:, 4 + h:5 + h], None, ALU.mult, None,
                       ['ps%d' % (3 + h // 2), 'ztg'], ['oacc%d' % g])
                    if h == 0:
                        ts('dve', imp[:, :], pv[:, 0, 65:193], zt[:, 0:1], None, ALU.mult, None,
                           ['ps3', 'zt'], ['imp'])
                    else:
                        stt('dve', imp[:, :], pv[:, h % 2, 65:193], zt[:, h:h + 1], imp[:, :], ALU.mult, ALU.add,
                            ['ps%d' % (3 + h // 2), 'zt', 'imp'], ['imp'])
                tt('dve', sct[:, :], imp[:, :], keepadd[:, 2 * j, :], ALU.mult, ['imp', 'keepadd'], ['sct'])
                tt('dve', sct[:, :], sct[:, :], keepadd[:, 2 * j + 1, :], ALU.add, ['sct', 'keepadd'], ['sct'])
                P.add('dve', lambda e: e.max(out=m8a[:, :], in_=sct[:, :]), ['sct'], ['m8a'])
                P.add('dve', lambda e: e.match_replace(out=sct2[:, :], in_to_replace=m8a[:, :], in_values=sct[:, :],
                                                       imm_value=-3.0e38), ['sct', 'm8a'], ['sct2'])
                P.add('dve', lambda e: e.max(out=m8b[:, :], in_=sct2[:, :]), ['sct2'], ['m8b'])
                ts('dve', selb[:, :], sct[:, :], m8b[:, 7:8], None, ALU.is_ge, None, ['sct', 'm8b'], ['selb'])
                tr(pst[0:64, 0:128], selb[:, 0:128:2], identb[:], ['selb', 'identb'], ['pst'])
                tr(pst[0:64, 640:768], selb[:, 1:128:2], identb[:], ['selb', 'identb'], ['pst'])
                cp('dve', selsb[0:64, 0, :], pst[0:64, 0:128], ['pst'], ['selsb'])
                cp('dve', selsb[0:64, 1, :], pst[0:64, 640:768], ['pst', 'selsb'], ['selsb'])
                nck_ = 4 * j + 4
                dma('sp', selD[g].ap().rearrange("h (c q) -> c h q", q=128), selsb[0:64, :, :], ['selsb'],
                    ['selD%d' % g])
                for hh in range(2):
                    srcb = selD[g].ap()[hh:hh + 1, 0:nck_ * 128].to_broadcast([64, nck_ * 128]) \
                        .rearrange("p (c q) -> p c q", q=128)
                    dma('sp', maskexp[g][64 * hh:64 * hh + 64, 0:nck_, :], srcb, ['selD%d' % g], ['mexp%d' % g])

            def make_after(g, bi, pb):
                def _after():
                    pv = ps[pb][:, 0:260].rearrange("p (h x) -> p h x", h=4)
                    ts('dve', zt[:, 8:12], pv[:, :, 64], 1e-30, None, ALU.max, None, ['ps%d' % pb], ['zt2'])
                    recip(zt[:, 8:12], zt[:, 8:12], ['zt2'], ['zt2'])
                    tt('dve', zt[:, 12:16], zt[:, 8:12], gates[g][:, :, bi], ALU.mult, ['zt2', 'gts'], ['zt2g'])
                    for h in range(4):
                        stt('dve', oaccs[g][:, h, :], pv[:, h, 0:64], zt[:, 12 + h:13 + h], oaccs[g][:, h, :],
                            ALU.mult, ALU.add, ['ps%d' % pb, 'zt2g', 'oacc%d' % g], ['oacc%d' % g])
                    if bi == 1:
                        tt('dve', osq[:, :, :], oaccs[g][:, :, :], oaccs[g][:, :, :], ALU.mult, ['oacc%d' % g], ['osq'])
                        P.add('dve', lambda e: e.tensor_reduce(out=zt[:, 8:12], in_=osq[:, :, :], axis=AX.X, op=ALU.add),
                              ['osq'], ['zt3'])
                        act(zt[:, 8:12], zt[:, 8:12], AF.Sqrt, ['zt3', 'epsc'], ['zt3'], scale=1.0 / 64, bias=epsc[:, 0:1])
                        recip(zt[:, 8:12], zt[:, 8:12], ['zt3'], ['zt3'])
                        tt('dve', yatm[:, 256 * g:256 * (g + 1)].rearrange("p (h d) -> p h d", h=4), oaccs[g][:, :, :],
                           zt[:, 8:12].unsqueeze(2).to_broadcast([128, 4, 64]), ALU.mult, ['oacc%d' % g, 'zt3'], ['yatm'])
                return _after

            items = []
            wl = [w for w in range(8) if 4 * j - 4 + w >= 0]
            for wi, w in enumerate(wl):
                for g in range(2):
                    gsl = slice(64 * g, 64 * g + 64)
                    c = 4 * j - 4 + w
                    items.append(dict(
                        k=kwT[gsl, 128 * c:128 * (c + 1)], kkey='kwT', q=rhs_qs[g],
                        stat=[(wmaskt[:, w, :], ['wmaskt'])],
                        v=Vall[:, c, 2 + g, :], vkey='Vall', first=(wi == 0), last=(wi == len(wl) - 1),
                        hfirst=lambda h: h == 0,
                        out=lambda h, g=g: ps[5 + g][:, 0:260].rearrange("p (h x) -> p h x", h=4)[:, h, :],
                        okey=lambda h, g=g: 'ps%d' % (5 + g),
                        after=make_after(g, 2, 5 + g) if wi == len(wl) - 1 else None))
            nck = 4 * j + 4
            for c in range(nck):
                for g in range(2):
                    gsl = slice(64 * g, 64 * g + 64)
                    stat = [(maskexp[g][:, c, :], ['mexp%d' % g])]
                    if c >= 4 * j:
                        stat.append((selstat[:, c - 4 * j, :], ['selstat']))
                    items.append(dict(
                        k=ksT[gsl, 128 * c:128 * (c + 1)], kkey='ksT', q=rhs_qs[g], stat=stat,
                        v=Vall[:, c, g, :], vkey='Vall', first=(c == 0), last=(c == nck - 1),
                        hfirst=lambda h: h == 0,
                        out=lambda h, g=g: ps[5 + g][:, 0:260].rearrange("p (h x) -> p h x", h=4)[:, h, :],
                        okey=lambda h, g=g: 'ps%d' % (5 + g),
                        after=make_after(g, 1, 5 + g) if c == nck - 1 else None))
            for it_ in items:
                it_['nsb'] = 5
            run_pipeline(items, depth=10, pair=2)
            for ch in range(4):
                tr(pst[:, 128 + ch * 128:256 + ch * 128], yatm[:, ch * 128:(ch + 1) * 128], identb[:],
                   ['yatm', 'identb'], ['pst2'])
            for ch in range(4):
                act(yaT[:, ch, j * 128:(j + 1) * 128], pst[:, 128 + ch * 128:256 + ch * 128], AF.Copy,
                    ['pst2', 'goT'], ['yaT'], scale=goT[:, 4 + ch:5 + ch])
        if debug and debug[0] == 'ya':
            P.barrier()
            cp('dve', arena_t[:, 0:NT], yaT[:, 0, :], ['yaT'], ['dbgt'])
            dump(arena_t[:, 0:NT], 'dbgt', NT)
        P.barrier()

        dma('sp', xT[:, :, :].rearrange("p k t -> p (k t)"), xsave.ap(), ['xsave'], ['xT'])
        cnt = 0
        for t in range(4):
            for dc in range(8):
                pb = cnt % 2
                cnt += 1
                for m in range(8):
                    rhs = ycT[:, m, t * 512:(t + 1) * 512] if m < 4 else yaT[:, m - 4, t * 512:(t + 1) * 512]
                    mm(ps[pb][:, :], wo[:, m, dc * 128:(dc + 1) * 128], rhs, m == 0, m == 7,
                       ['wo', 'ycT', 'yaT'], ['ps%d' % pb])
                stt('dve', xT[:, dc, t * 512:(t + 1) * 512], ps[pb][:, :], GT_(1)[:, dc:dc + 1],
                    xT[:, dc, t * 512:(t + 1) * 512], ALU.mult, ALU.add, ['ps%d' % pb, 'xT', 'vecs'], ['xT'])
        AR.release(M_R1)
        P.barrier()
        if debug and debug[0] == 'x2':
            dump(xT[:, :, :].rearrange("p k t -> p (k t)"), 'xT', 8 * NT)

        ffn(1, A_(2), SH_(2), GT_(2), False)

        sqb = AR.alloc([128, 8, 512], BF16)
        rsb = AR.alloc([128, 512], F32)
        yfin = AR.alloc([128, 8, 512], F32)
        otm = [AR.alloc([128, D], F32) for _ in range(2)]
        for t in range(4):
            norm_mod(xT[:, :, t * 512:(t + 1) * 512], 'xT', 512, None, None, None, None, sqb, rsb, None, ps[6], 'ps6')
            for kc in range(8):
                stt('dve', yfin[:, kc, :], xT[:, kc, t * 512:(t + 1) * 512], gv[:, 24 + kc:25 + kc], rsb[:, :],
                    ALU.mult, ALU.mult, ['xT', 'rsb', 'gv'], ['yfin'])
            for jj in range(4):
                ob = otm[jj % 2]
                ok = 'otm%d' % (jj % 2)
                for hf in range(2):
                    pb = 2 * (jj % 2) + hf
                    for k4 in range(4):
                        kc = hf * 4 + k4
                        tr(ps[pb][:, k4 * 128:(k4 + 1) * 128], yfin[:, kc, jj * 128:(jj + 1) * 128], ident32[:],
                           ['yfin', 'ident32'], ['ps%d' % pb])
                    if hf == 0:
                        act(ob[:, 0:512], ps[pb][:, :], AF.Copy, ['ps%d' % pb], [ok])
                    else:
                        cp('dve', ob[:, 512:1024], ps[pb][:, :], ['ps%d' % pb], [ok])
                row = (4 * t + jj) * 128
                dma('sp', out_loc[row:row + 128, :], ob[:, :], [ok], ['out'], semkey='out')
        P.finalize(st)
    return nc


def cm_base(j):
    idx = 0
    for jj in range(j):
        idx += 2 if (jj % 4 == 0 and jj > 0) else 1
    return idx


_PROG = {}


def _pvec(v, nch=8):
    return np.ascontiguousarray(np.asarray(v, np.float32).reshape(nch, 128).T)


def _partner(d):
    return d + 8 if d < 8 else (d - 8 if d < 16 else d)


def _host_shared(c, positions, w_ada, b_ada, g_ffn1, w1_gate, w1_up, w1_down, g_mix, w_in, conv_w, cmp_pos_k,
                 cmp_pos_v, w_cmpk1, w_cmpk2, w_cmpv1, w_cmpv2, g_out_conv, g_out_attn, w_out, g_ffn2, w2_gate,
                 w2_up, w2_down, g_final):
    f = lambda a: np.ascontiguousarray(np.asarray(a, np.float32))
    sh = {}
    sh["_w_ada"] = f(w_ada[0])
    sh["_b_adaT"] = np.ascontiguousarray(f(b_ada[0]).reshape(72, 128).T)
    sh["gvecs"] = np.ascontiguousarray(np.concatenate([_pvec(g_ffn1[0]), _pvec(g_mix[0]), _pvec(g_ffn2[0]),
                                                       _pvec(g_final)], axis=1))
    sh["w1_gate"], sh["w1_up"], sh["w1_down"] = f(w1_gate[0]), f(w1_up[0]), f(w1_down[0])
    sh["w2_gate"], sh["w2_up"], sh["w2_down"] = f(w2_gate[0]), f(w2_up[0]), f(w2_down[0])
    win = f(w_in[0])
    cols = []
    for base in (0, 512, 1024):
        cols += list(range(base, base + 512))
    qb0 = 1536
    for hg in range(4):
        cols += [qb0 + 64 * hg + d for d in range(64)] + [qb0 + 64 * (4 + hg) + d for d in range(64)]
    for hg in range(4):
        cols += [qb0 + 64 * hg + _partner(d) for d in range(64)] + [qb0 + 64 * (4 + hg) + _partner(d) for d in range(64)]
    for kb in (2048, 2304, 2560):
        cols += [kb + i for i in range(128)]
        cols += [kb + 64 * g + _partner(d) for g in range(2) for d in range(64)]
    cols += [2176 + i for i in range(128)]
    assert len(cols) == 27 * 128
    sh["w_in_fm"] = np.ascontiguousarray(win[:, cols])
    tcols = list(range(2432, 2560)) + list(range(2688, 2816)) + list(range(2816, 2840))
    sh["w_in_tm"] = np.ascontiguousarray(win[:, tcols])
    cw = f(conv_w[0])
    sh["conv_wT"] = np.ascontiguousarray(cw.reshape(3, 4, 128).transpose(2, 1, 0).reshape(128, 12))
    sh["g_oT"] = np.ascontiguousarray(np.concatenate([_pvec(g_out_conv[0], 4), _pvec(g_out_attn[0], 4)], axis=1))
    sh["w_out"] = f(w_out[0])
    for nm, w1 in (("w1k_r", w_cmpk1), ("w1v_r", w_cmpv1)):
        a = f(w1[0]).reshape(32, 64, 256).transpose(1, 0, 2)
        sh[nm] = np.ascontiguousarray(np.concatenate([a, a], 0).reshape(128, 32 * 256))
    pk = f(cmp_pos_k[0]).T
    pv = f(cmp_pos_v[0]).T
    pe = np.concatenate([pk, pv], 1)
    sh["peT"] = np.ascontiguousarray(np.concatenate([pe, pe], 0))
    w2k = f(w_cmpk2[0]).reshape(2, 128, 64)
    pad = np.zeros((128, 2, 2, 128), np.float32)
    for mc in range(2):
        for g in range(2):
            pad[:, mc, g, 64 * g:64 * g + 64] = w2k[mc]
    sh["w2kpad"] = pad.reshape(128, 512)
    sh["w2v"] = np.ascontiguousarray(f(w_cmpv2[0]).reshape(2, 128, 64).transpose(1, 0, 2).reshape(128, 128))
    fr = np.zeros((128, 2), np.float32)
    freqs = np.power(np.float32(500000.0), (-2.0 * np.arange(8, dtype=np.float32) / np.float32(16.0))).astype(np.float32)
    for p in range(128):
        d = p % 64
        if d < 16:
            fr[p, 0] = freqs[d % 8]
            fr[p, 1] = -1.0 if d < 8 else 1.0
    sh["freqs"] = fr
    n_cmp = 511
    c0 = np.arange(n_cmp) * 16
    c1 = c0 + 31
    s0 = np.arange(128) * 64
    s1 = s0 + 63
    ov = ((c0[:, None] <= s1[None, :]) & (c1[:, None] >= s0[None, :])).astype(np.float32)
    ovp = np.zeros((512, 128), np.float32)
    ovp[:511] = ov
    sh["ovl"] = np.ascontiguousarray(ovp.reshape(4, 128, 128).transpose(1, 0, 2).reshape(128, 512))
    return sh


def _host_core(core, x, c, positions):
    b, r = core // 4, core % 4
    m = {}
    xb = np.asarray(x[b], np.float32).reshape(64, 128, D)
    m["x_loc"] = np.ascontiguousarray(xb[r::4].reshape(NT, D))
    xh = np.zeros((HAL, D), np.float32)
    hm = np.zeros((128, HAL), np.float32)
    xflat = np.asarray(x[b], np.float32)
    for j in range(NJ):
        qb = 4 * j + r
        if qb > 0:
            xh[2 * j] = xflat[128 * qb - 2]
            xh[2 * j + 1] = xflat[128 * qb - 1]
            hm[:, 2 * j:2 * j + 2] = 1.0
    m["x_halo"] = xh
    m["halo_mask"] = hm
    m["cT"] = _pvec(np.asarray(c[b], np.float32))
    pb = np.asarray(positions[b], np.int32).reshape(64, 128)[r::4].reshape(NT)
    m["pos"] = np.ascontiguousarray(np.tile(pb[None, :], (128, 1)))
    ik = np.arange(128)[:, None]
    iq = np.arange(128)[None, :]
    causal = np.where(ik <= iq, 1.0, 0.0).astype(np.float32)
    full = np.zeros((128, 128), np.float32)
    zero = np.ones((128, 128), np.float32)
    ss = np.stack([zero if rp < r else (causal if rp == r else full) for rp in range(4)], 1)
    m["selstat"] = np.ascontiguousarray(ss.reshape(128, 512))
    wm = []
    edge = np.where(ik > iq, 1.0, 0.0).astype(np.float32)
    for w in range(8):
        dd = w - 4 - r
        if dd < -4 or dd > 0:
            wm.append(full)
        elif dd == -4:
            wm.append(edge)
        elif dd == 0:
            wm.append(causal)
        else:
            wm.append(zero)
    m["wmask"] = np.ascontiguousarray(np.stack(wm, 1).reshape(128, 1024))
    cms = []
    for j in range(NJ):
        qb = 4 * j + r
        cl = j // 4
        chunks = [cl] + ([cl - 1] if (j % 4 == 0 and j > 0) else [])
        for cc in chunks:
            ig = 128 * cc + np.arange(128)[:, None]
            t = 128 * qb + np.arange(128)[None, :]
            valid = (16 * ig + 31 <= t) & (ig < 511)
            cms.append(np.where(valid, 1.0, 0.0).astype(np.float32))
    assert len(cms) == 19
    m["cmask"] = np.ascontiguousarray(np.stack(cms, 1).reshape(128, 19 * 128))
    ka = np.zeros((128, 2 * NJ, 128), np.float32)
    jb = np.arange(128)[None, :]
    for j in range(NJ):
        qb = 4 * j + r
        cur = (2 * qb + (np.arange(128) >= 64).astype(np.int64))[:, None]
        keep = np.ones((128, 128), np.float32)
        add = np.zeros((128, 128), np.float32)
        f0 = (jb == 0) & np.ones((128, 1), bool)
        fm1 = jb == cur - 1
        fc = jb == cur
        fut = jb > cur
        add[f0] = 1024.0
        add[fm1] = 2048.0
        add[fc] = 4096.0
        add[fut] = -1.0e9
        keep[f0 | fm1 | fc | fut] = 0.0
        ka[:, 2 * j] = keep
        ka[:, 2 * j + 1] = add
    m["keepadd"] = np.ascontiguousarray(ka.reshape(128, 2 * NJ * 128))
    return m


def kernel(x, c, positions, w_ada, b_ada, g_ffn1, w1_gate, w1_up, w1_down, g_mix, w_in, conv_w, cmp_pos_k,
           cmp_pos_v, w_cmpk1, w_cmpk2, w_cmpv1, w_cmpv2, g_out_conv, g_out_attn, w_out, g_ffn2, w2_gate,
           w2_up, w2_down, g_final, _debug=None):
    x = np.asarray(x)
    sh = _host_shared(c, positions, w_ada, b_ada, g_ffn1, w1_gate, w1_up, w1_down, g_mix, w_in, conv_w, cmp_pos_k,
                      cmp_pos_v, w_cmpk1, w_cmpk2, w_cmpv1, w_cmpv2, g_out_conv, g_out_attn, w_out, g_ffn2,
                      w2_gate, w2_up, w2_down, g_final)
    key = str(_debug)
    if key not in _PROG:
        _PROG[key] = build_program(_debug)
    nc = _PROG[key]
    in_maps = []
    for core in range(NCORES):
        m = {k_: v_ for k_, v_ in sh.items() if not k_.startswith("_")}
        m.update(_host_core(core, x, np.asarray(c), np.asarray(positions)))
        rq = core % 4
        m["w_ada_q"] = np.ascontiguousarray(sh["_w_ada"][:, rq * 2304:(rq + 1) * 2304])
        m["b_adaT_q"] = np.ascontiguousarray(sh["_b_adaT"][:, rq * 18:(rq + 1) * 18])
        in_maps.append(m)
    res = run_bass_kernel_spmd(nc, in_maps, core_ids=list(range(NCORES)))
    if _debug is not None:
        return res
    out = np.zeros((2, 64, 128, D), np.float32)
    for core in range(NCORES):
        b, r = core // 4, core % 4
        out[b, r::4] = np.asarray(res.results[core]["out_loc"], np.float32).reshape(NJ, 128, D)
    return out.reshape(2, S, D)
```

```python
import numpy as np
from contextlib import ExitStack
import concourse.bass as bass
import concourse.mybir as mybir
from concourse.bass_utils import run_bass_kernel_spmd

F32 = mybir.dt.float32
BF16 = mybir.dt.bfloat16
I32 = mybir.dt.int32
AF = mybir.ActivationFunctionType
ALU = mybir.AluOpType
AX = mybir.AxisListType

NCORES = 8
D = 1024
S = 8192
NT = 2048
NJ = 16
HAL = 32
DFF = 2816
NF = 22
EPS = 1e-6
NEGM = -30000.0
XROWS = 768
DEBUG = None


class Prog:
    def __init__(self, nc, same_engine_sync=True):
        self.nc = nc
        self.ops = []
        self.same_engine_sync = same_engine_sync

    def add(self, eng, fn, reads=(), writes=(), dma=False, semkey=None, inc=16):
        self.ops.append(dict(eng=eng, fn=fn, reads=tuple(reads), writes=tuple(writes),
                             dma=dma, semkey=semkey, barrier=False, inc=inc))

    def barrier(self):
        for e in ('pe', 'act', 'dve', 'pool', 'sp'):
            self.ops.append(dict(eng=e, fn=None, reads=(), writes=(), dma=False,
                                 semkey=None, barrier=True))

    def finalize(self, stack):
        nc = self.nc
        ops = self.ops
        n = len(ops)
        last_w = {}
        readers = {}
        deps = [set() for _ in range(n)]
        last_on_eng = {}
        all_dmas = []
        for i, op in enumerate(ops):
            if op['barrier']:
                for e, j in last_on_eng.items():
                    deps[i].add(j)
                for j in all_dmas:
                    deps[i].add(j)
                continue
            for k in op['reads']:
                if k in last_w:
                    deps[i].add(last_w[k])
            for k in op['writes']:
                if k in last_w:
                    deps[i].add(last_w[k])
                for r in readers.get(k, ()):
                    deps[i].add(r)
            for k in op['reads']:
                readers.setdefault(k, []).append(i)
            for k in op['writes']:
                last_w[k] = i
                readers[k] = []
            deps[i].discard(i)
            if op['dma']:
                all_dmas.append(i)
            else:
                last_on_eng[op['eng']] = i
        needed = set()
        for i in range(n):
            op = ops[i]
            for d in deps[i]:
                od = ops[d]
                if od['barrier'] or od['dma']:
                    continue
                if od['eng'] == op['eng'] and not op['dma']:
                    if od['eng'] == 'pe' or not self.same_engine_sync:
                        continue
                needed.add(d)
        eng_sem = {}
        for e in ('pe', 'act', 'dve', 'pool'):
            eng_sem[e] = stack.enter_context(nc.semaphore("sem_" + e))
        dma_sem = {}
        eng_cnt = {e: 0 for e in eng_sem}
        dma_cnt = {}
        sig = [None] * n
        for i, op in enumerate(ops):
            if op['barrier']:
                continue
            if op['dma']:
                k = op['semkey']
                if k not in dma_sem:
                    dma_sem[k] = stack.enter_context(nc.semaphore("dsem_%d" % len(dma_sem)))
                    dma_cnt[k] = 0
                dma_cnt[k] += op['inc']
                sig[i] = (dma_sem[k], dma_cnt[k], op['inc'])
            elif i in needed:
                e = op['eng']
                eng_cnt[e] += 1
                sig[i] = (eng_sem[e], eng_cnt[e], 1)
        self.n_sems = len(dma_sem) + 4
        waited = {e: {} for e in ('pe', 'act', 'dve', 'pool', 'sp')}
        waits = [[] for _ in range(n)]
        for i, op in enumerate(ops):
            e = op['eng']
            req = {}
            for d in deps[i]:
                if sig[d] is None:
                    continue
                if (not ops[d]['dma']) and d not in needed:
                    continue
                if (not ops[d]['dma']) and ops[d]['eng'] == e and not op['dma'] and \
                        (e == 'pe' or not self.same_engine_sync):
                    continue
                s, v, _ = sig[d]
                key = id(s)
                if key not in req or req[key][1] < v:
                    req[key] = (s, v)
            for key, (s, v) in req.items():
                if waited[e].get(key, 0) >= v:
                    continue
                waited[e][key] = v
                waits[i].append((s, v))
        final_dma = [(s, dma_cnt[k]) for k, s in dma_sem.items()]
        block = stack.enter_context(nc.Block())

        def emitter(ename):
            def _f(eng):
                for i, op in enumerate(ops):
                    if op['eng'] != ename:
                        continue
                    for (s, v) in waits[i]:
                        eng.wait_ge(s, v)
                    if op['fn'] is None:
                        continue
                    inst = op['fn'](eng)
                    if sig[i] is not None:
                        s, v, inc = sig[i]
                        inst.then_inc(s, inc)
                if ename == 'sp':
                    for (s, v) in final_dma:
                        eng.wait_ge(s, v)
            return _f

        block.tensor(emitter('pe'))
        block.scalar(emitter('act'))
        block.vector(emitter('dve'))
        block.gpsimd(emitter('pool'))
        block.sync(emitter('sp'))


class Arena:
    def __init__(self, ap, nwords):
        self.ap = ap
        self.n = nwords
        self.top = 0
        self.peak = 0

    def mark(self):
        return self.top

    def release(self, m):
        self.top = m

    def alloc(self, shape, dt):
        nel = int(np.prod(shape[1:]))
        nw = nel if dt in (F32, I32) else (nel + 1) // 2
        a = self.top
        self.top += nw
        self.peak = max(self.peak, self.top)
        assert self.top <= self.n, ("arena overflow", self.top, self.n)
        v = self.ap[:, a:a + nw]
        if dt != F32:
            v = v.bitcast(dt)
            if dt == BF16 and nel % 2:
                v = v[:, 0:nel]
        if len(shape) == 3:
            v = v.rearrange("p (a b) -> p a b", a=shape[1])
        elif len(shape) == 4:
            v = v.rearrange("p (a b c) -> p a b c", a=shape[1], b=shape[2])
        return v


def build_program(debug=None):
    nc = bass.Bass("TRN2", target_bir_lowering=False)

    def din(name, shape, dt=F32):
        return nc.dram_tensor(name, list(shape), dt, kind="ExternalInput").ap()

    x_loc = din("x_loc", [NT, D])
    x_halo = din("x_halo", [HAL, D])
    halo_mask = din("halo_mask", [128, HAL])
    cT_d = din("cT", [128, 8])
    pos_d = din("pos", [128, NT], I32)
    w_ada = din("w_ada_q", [D, 2304])
    b_adaT = din("b_adaT_q", [128, 18])
    gvecs = din("gvecs", [128, 32])
    wg_d = [din("w1_gate", [D, DFF]), din("w2_gate", [D, DFF])]
    wu_d = [din("w1_up", [D, DFF]), din("w2_up", [D, DFF])]
    wd_d = [din("w1_down", [DFF, D]), din("w2_down", [DFF, D])]
    NFM = 27
    w_in_fm = din("w_in_fm", [D, NFM * 128])
    w_in_tm = din("w_in_tm", [D, 280])
    conv_wT = din("conv_wT", [128, 12])
    g_oT = din("g_oT", [128, 8])
    w_out_d = din("w_out", [D, D])
    w1r_d = [din("w1k_r", [128, 32 * 256]), din("w1v_r", [128, 32 * 256])]
    peT_d = din("peT", [128, 64])
    w2kpad_d = din("w2kpad", [128, 2 * 2 * 128])
    w2v_d = din("w2v", [128, 2 * 64])
    freqs_d = din("freqs", [128, 2])
    selstat_d = din("selstat", [128, 4 * 128])
    wmask_d = din("wmask", [128, 8 * 128])
    cmask_d = din("cmask", [128, 19 * 128])
    keepadd_d = din("keepadd", [128, 2 * NJ * 128])
    ovl_d = din("ovl", [128, 4 * 128])
    out_loc = nc.dram_tensor("out_loc", [NT, D], F32, kind="ExternalOutput").ap()
    dbg = None
    if debug is not None:
        dbg = nc.dram_tensor("dbg", [128, debug[1]], F32, kind="ExternalOutput").ap()
    xg_in = [nc.dram_tensor("xg_in%d" % i, [256, NT], BF16) for i in range(3)]
    xg_out = [nc.dram_tensor("xg_out%d" % i, [4 * 256, NT], BF16) for i in range(3)]
    xsave = nc.dram_tensor("xsave", [128, 8 * NT], F32)
    mg_in = nc.dram_tensor("mg_in", [18, 128], F32)
    mg_out = nc.dram_tensor("mg_out", [72, 128], F32)
    selD = [nc.dram_tensor("selD%d" % i, [2, 64 * 128], BF16) for i in range(2)]

    st = ExitStack()
    with st:
        def T(name, shape, dt):
            return st.enter_context(nc.sbuf_tensor(name, list(shape), dt))

        AW = 52000
        arena_t = T("arena", [128, AW], F32)
        AR = Arena(arena_t[:, :], AW)
        ident32 = T("ident32", [128, 128], F32)
        identb = T("identb", [128, 128], BF16)
        onesb = T("onesb", [128, 128], BF16)
        bdb = T("bdb", [128, 128], BF16)
        vecs = T("vecs", [128, 160], F32)
        gv = T("gv", [128, 32], F32)
        cT = T("cTt", [128, 16], F32)
        convw = T("convw", [128, 12], F32)
        goT = T("goT", [128, 8], F32)
        frq = T("frq", [128, 4], F32)
        hmask = T("hmask", [128, HAL], F32)
        epsc = T("epsc", [128, 2], F32)
        mqT = T("mqT", [128, 128], F32)
        mgT = T("mgT", [128, 128], F32)
        ps = [st.enter_context(nc.psum_tensor("ps%d" % i, [128, 512], F32)) for i in range(7)]
        pst = st.enter_context(nc.psum_tensor("pst", [128, 1024], BF16))

        P = Prog(nc)
        _cnt = [0]

        def uid(p):
            _cnt[0] += 1
            return "%s#%d" % (p, _cnt[0])

        def dma(q, out, in_, reads, writes, semkey=None):
            P.add(q, lambda e: e.dma_start(out=out, in_=in_), reads, writes, dma=True,
                  semkey=semkey or writes[0])

        def mm(out, lhsT, rhs, start, stop, reads, writes, skip=False):
            P.add('pe', lambda e: e.matmul(out, lhsT=lhsT, rhs=rhs, start=start, stop=stop,
                                           skip_group_check=skip), reads, writes)

        def tr(out, in_, ident, reads, writes):
            P.add('pe', lambda e: e.transpose(out=out, in_=in_, identity=ident), reads, writes)

        def act(out, in_, func, reads, writes, scale=1.0, bias=None):
            if bias is None:
                P.add('act', lambda e: e.activation(out=out, in_=in_, func=func, scale=scale), reads, writes)
            else:
                P.add('act', lambda e: e.activation(out=out, in_=in_, func=func, scale=scale, bias=bias),
                      reads, writes)

        def tt(eng, out, in0, in1, op, reads, writes):
            P.add(eng, lambda e: e.tensor_tensor(out=out, in0=in0, in1=in1, op=op), reads, writes)

        def ts(eng, out, in0, s1, s2, op0, op1, reads, writes):
            if s2 is None:
                P.add(eng, lambda e: e.tensor_scalar(out=out, in0=in0, scalar1=s1, scalar2=None, op0=op0),
                      reads, writes)
            else:
                P.add(eng, lambda e: e.tensor_scalar(out=out, in0=in0, scalar1=s1, scalar2=s2, op0=op0, op1=op1),
                      reads, writes)

        def stt(eng, out, in0, scalar, in1, op0, op1, reads, writes):
            P.add(eng, lambda e: e.scalar_tensor_tensor(out=out, in0=in0, scalar=scalar, in1=in1, op0=op0, op1=op1),
                  reads, writes)

        def cp(eng, out, in_, reads, writes):
            P.add(eng, lambda e: e.tensor_copy(out=out, in_=in_), reads, writes)

        def memset(eng, ap, val, writes, reads=()):
            P.add(eng, lambda e: e.memset(ap, val), reads, writes)

        def recip(out, in_, reads, writes):
            P.add('dve', lambda e: e.reciprocal(out=out, in_=in_), reads, writes)

        memset('pool', ident32[:], 0.0, ['ident32'])
        P.add('pool', lambda e: e.affine_select(out=ident32[:], in_=ident32[:], pattern=[[-1, 128]],
                                                compare_op=ALU.not_equal, fill=1.0, base=0, channel_multiplier=1),
              ['ident32'], ['ident32'])
        cp('dve', identb[:], ident32[:], ['ident32'], ['identb'])
        memset('pool', onesb[:], 1.0, ['onesb'])
        memset('pool', bdb[:], 0.0, ['bdb'])
        memset('pool', bdb[0:64, 0:64], 1.0, ['bdb'], ['bdb'])
        memset('pool', bdb[64:128, 64:128], 1.0, ['bdb'], ['bdb'])
        memset('pool', epsc[:], EPS, ['epsc'])
        dma('sp', gv[:], gvecs, [], ['gv'])
        dma('sp', cT[:, 0:8], cT_d, [], ['cT'])
        dma('sp', convw[:], conv_wT, [], ['convw'])
        dma('sp', goT[:], g_oT, [], ['goT'])
        dma('sp', frq[:, 0:2], freqs_d, [], ['frq'])
        dma('sp', hmask[:], halo_mask, [], ['hmask'])
        dma('sp', vecs[:, 128:146], b_adaT, [], ['modq'])

        xT = AR.alloc([128, 8, NT], F32)
        xhT = AR.alloc([128, 8, HAL], F32)
        M_R1 = AR.mark()

        m0 = AR.mark()
        xtm = [AR.alloc([128, D], F32) for _ in range(2)]
        xhm = AR.alloc([128, D], F32)
        wab = [AR.alloc([128, 2304], F32) for _ in range(2)]
        for j in range(NJ):
            b = j % 2
            dma('sp', xtm[b], x_loc[j * 128:(j + 1) * 128, :], [], ['xtm%d' % b])
            for hf in range(2):
                pb = ps[hf]
                for k4 in range(4):
                    kc = hf * 4 + k4
                    tr(pb[:, k4 * 128:(k4 + 1) * 128], xtm[b][:, kc * 128:(kc + 1) * 128], ident32[:],
                       ['xtm%d' % b, 'ident32'], ['ps%d' % hf])
                eng = 'act' if hf == 0 else 'dve'
                if eng == 'act':
                    act(xT[:, hf * 4:(hf + 1) * 4, j * 128:(j + 1) * 128],
                        pb[:, :].rearrange("p (k t) -> p k t", k=4), AF.Copy, ['ps%d' % hf], ['xT'])
                else:
                    cp('dve', xT[:, hf * 4:(hf + 1) * 4, j * 128:(j + 1) * 128],
                       pb[:, :].rearrange("p (k t) -> p k t", k=4), ['ps%d' % hf], ['xT'])
        dma('sp', xhm[0:HAL, :], x_halo, [], ['xhm'])
        for kc in range(8):
            tr(ps[2][:, kc * HAL:(kc + 1) * HAL], xhm[0:HAL, kc * 128:(kc + 1) * 128], ident32[0:HAL, 0:HAL],
               ['xhm', 'ident32'], ['ps2'])
        cp('dve', xhT[:, :, :], ps[2][:, 0:8 * HAL].rearrange("p (k t) -> p k t", k=8), ['ps2'], ['xhT'])

        act(cT[:, 8:16], cT[:, 0:8], AF.Silu, ['cT'], ['cact'])
        for kc in range(8):
            b = kc % 2
            dma('sp', wab[b], w_ada[kc * 128:(kc + 1) * 128, :], [], ['wab%d' % b])
            for m in range(18):
                mm(ps[3][:, m:m + 1], wab[b][:, m * 128:(m + 1) * 128], cT[:, 8 + kc:9 + kc], True, True,
                   ['wab%d' % b, 'cact'], ['ps3'])
            tt('dve', vecs[:, 128:146], vecs[:, 128:146], ps[3][:, 0:18], ALU.add, ['ps3', 'modq'], ['modq'])
        tr(ps[3][0:18, 128:256], vecs[:, 128:146], ident32[:], ['modq', 'ident32'], ['ps3'])
        cp('dve', mqT[0:18, :], ps[3][0:18, 128:256], ['ps3'], ['mqT'])
        dma('sp', mg_in.ap(), mqT[0:18, :], ['mqT'], ['mg_in'])
        P.add('pool', lambda e: e.collective_compute("AllGather", ALU.bypass,
                                                     replica_groups=[[0, 1, 2, 3], [4, 5, 6, 7]],
                                                     ins=[mg_in.ap().opt()], outs=[mg_out.ap().opt()]),
              ['mg_in'], ['mg_out'], dma=True, semkey='cc_mg', inc=1)
        dma('sp', mgT[0:72, :], mg_out.ap(), ['mg_out'], ['mgT'])
        tr(ps[3][:, 256:328], mgT[0:72, :], ident32[0:72, 0:72], ['mgT', 'ident32'], ['ps3'])
        cp('dve', vecs[:, 0:72], ps[3][:, 256:328], ['ps3'], ['vecs'])
        for i in range(3):
            stt('dve', vecs[:, 72 + 8 * i:80 + 8 * i], vecs[:, 24 * i + 8:24 * i + 16], 1.0, gv[:, 8 * i:8 * i + 8],
                ALU.add, ALU.mult, ['vecs', 'gv'], ['vecs'])
            ts('dve', vecs[:, 96 + 8 * i:104 + 8 * i], vecs[:, 24 * i + 16:24 * i + 24],
               0.5 if i != 1 else 1.0, None, ALU.mult, None, ['vecs'], ['vecs'])

        def A_(i):
            return vecs[:, 72 + 8 * i:80 + 8 * i]

        def SH_(i):
            return vecs[:, 24 * i:24 * i + 8]

        def GT_(i):
            return vecs[:, 96 + 8 * i:104 + 8 * i]

        AR.release(m0)
        P.barrier()

        def norm_mod(src, srckey, n, av, shv, dst, dstkey, sqb, rsb, tmpn, psb, psk):
            act(sqb[:, :, 0:n], src, AF.Square, [srckey], ['sqb'])
            for kc in range(8):
                mm(psb[:, 0:n], onesb[:], sqb[:, kc, 0:n], kc == 0, kc == 7, ['sqb', 'onesb'], [psk])
            act(rsb[:, 0:n], psb[:, 0:n], AF.Sqrt, [psk, 'epsc'], ['rsb'], scale=1.0 / D, bias=epsc[:, 0:1])
            recip(rsb[:, 0:n], rsb[:, 0:n], ['rsb'], ['rsb'])
            if dst is None:
                return
            for kc in range(8):
                tb = tmpn[kc % 2]
                stt('dve', tb[:, 0:n], src[:, kc, :], av[:, kc:kc + 1], rsb[:, 0:n], ALU.mult, ALU.mult,
                    [srckey, 'rsb', 'vecs'], ['tmpn%d' % (kc % 2)])
                act(dst[:, kc, :], tb[:, 0:n], AF.Identity, ['tmpn%d' % (kc % 2), 'vecs'], [dstkey],
                    bias=shv[:, kc:kc + 1])

        RT = Arena(arena_t[:, 45856:AW], AW - 45856)
        Ctab = RT.alloc([128, NT], F32)
        Stab = RT.alloc([128, NT], F32)
        posi = RT.alloc([128, 512], I32)
        angb = RT.alloc([128, 512], F32)
        tqb = RT.alloc([128, 512], F32)
        kib = RT.alloc([128, 512], I32)

        def emit_rope_tables():
            TWO_PI = float(2 * np.pi)

            def range_reduce(y, key):
                ts('dve', tqb[:], y, 1.0 / TWO_PI, None, ALU.mult, None, [key], ['tqb'])
                cp('dve', kib[:], tqb[:], ['tqb'], ['kib'])
                cp('dve', tqb[:], kib[:], ['kib'], ['tqb'])
                stt('dve', y, tqb[:], -TWO_PI, y, ALU.mult, ALU.add, ['tqb', key], [key])
                ts('dve', tqb[:], y, float(np.pi), -TWO_PI, ALU.is_gt, ALU.mult, [key], ['tqb'])
                tt('dve', y, y, tqb[:], ALU.add, [key, 'tqb'], [key])
                ts('dve', tqb[:], y, float(-np.pi), TWO_PI, ALU.is_lt, ALU.mult, [key], ['tqb'])
                tt('dve', y, y, tqb[:], ALU.add, [key, 'tqb'], [key])

            for t in range(4):
                dma('sp', posi[:], pos_d[:, t * 512:(t + 1) * 512], [], ['posi'])
                cp('dve', angb[:], posi[:], ['posi'], ['angb'])
                ts('dve', angb[:], angb[:], frq[:, 0:1], None, ALU.mult, None, ['angb', 'frq'], ['angb'])
                cp('dve', Stab[:, t * 512:(t + 1) * 512], angb[:], ['angb'], ['Stab'])
                ts('dve', angb[:], angb[:], float(np.pi / 2), None, ALU.add, None, ['angb'], ['angb'])
                range_reduce(angb[:], 'angb')
                act(Ctab[:, t * 512:(t + 1) * 512], angb[:], AF.Sin, ['angb'], ['Ctab'])
                cp('dve', angb[:], Stab[:, t * 512:(t + 1) * 512], ['Stab'], ['angb'])
                range_reduce(angb[:], 'angb')
                act(angb[:], angb[:], AF.Sin, ['angb'], ['angb'])
                ts('dve', Stab[:, t * 512:(t + 1) * 512], angb[:], frq[:, 1:2], None, ALU.mult, None,
                   ['angb', 'frq'], ['Stab'])

        def ffn(li, av, shv, gatev, with_halo, mid_hook=None):
            m = AR.mark()
            W = 1024 + (HAL if with_halo else 0)
            AT = AR.alloc([128, NF, 1056], BF16)
            hT = AR.alloc([128, 8, 1056], BF16)
            wgb = [AR.alloc([128, 8, 256], BF16) for _ in range(2)]
            wub = [AR.alloc([128, 8, 256], BF16) for _ in range(2)]
            wdb = [AR.alloc([128, NF, 128], BF16) for _ in range(2)]
            sqb = AR.alloc([128, 8, 512], BF16)
            rsb = AR.alloc([128, 512], F32)
            tmpn = [AR.alloc([128, 512], F32) for _ in range(2)]
            sgt = [AR.alloc([128, 512], F32) for _ in range(2)]
            wg_v = wg_d[li].rearrange("(k p) c -> p k c", p=128)
            wu_v = wu_d[li].rearrange("(k p) c -> p k c", p=128)
            wd_v = wd_d[li].rearrange("(f p) d -> p f d", p=128)
            for pi in range(2):
                tiles = []
                off = 0
                if pi == 0 and with_halo:
                    tiles.append(('h', 0, off, HAL))
                    off += HAL
                for t2 in range(2):
                    tiles.append(('m', pi * 1024 + t2 * 512, off, 512))
                    off += 512

                def xsrc(tl):
                    kind, c0, o, nn = tl
                    if kind == 'h':
                        return xhT[:, :, :], 'xhT'
                    return xT[:, :, c0:c0 + nn], 'xT'

                for ti, tl in enumerate(tiles):
                    src, sk = xsrc(tl)
                    norm_mod(src, sk, tl[3], av, shv, hT[:, :, tl[2]:tl[2] + tl[3]], 'hT%d' % ti,
                             sqb, rsb, tmpn, ps[6], 'ps6')
                side_ops = []
                if pi == 0 and mid_hook is not None:
                    saved_ops = P.ops
                    P.ops = []
                    mid_hook()
                    side_ops = P.ops
                    P.ops = saved_ops

                def load_gu(fg):
                    b = fg % 2
                    dma('pool', wgb[b], wg_v[:, :, fg * 256:(fg + 1) * 256], [], ['wgb%d' % b])
                    dma('pool', wub[b], wu_v[:, :, fg * 256:(fg + 1) * 256], [], ['wub%d' % b])

                def load_wd(dc):
                    b = dc % 2
                    dma('pool', wdb[b], wd_v[:, :, dc * 128:(dc + 1) * 128], [], ['wdb%d' % b])

                load_gu(0)
                load_gu(1)
                cnt = 0
                for fg in range(11):
                    b = fg % 2
                    for fc in range(2):
                        f = fg * 2 + fc
                        for ti, tl in enumerate(tiles):
                            _, _, o, nn = tl
                            pg = cnt % 2
                            pu = 2 + cnt % 2
                            cnt += 1
                            for kc in range(8):
                                mm(ps[pg][:, 0:nn], wgb[b][:, kc, fc * 128:(fc + 1) * 128], hT[:, kc, o:o + nn],
                                   kc == 0, kc == 7, ['wgb%d' % b, 'hT%d' % ti], ['ps%d' % pg])
                            for kc in range(8):
                                mm(ps[pu][:, 0:nn], wub[b][:, kc, fc * 128:(fc + 1) * 128], hT[:, kc, o:o + nn],
                                   kc == 0, kc == 7, ['wub%d' % b, 'hT%d' % ti], ['ps%d' % pu])
                            sb = sgt[cnt % 2]
                            act(sb[:, 0:nn], ps[pg][:, 0:nn], AF.Silu, ['ps%d' % pg], ['sgt%d' % (cnt % 2)])
                            tt('dve', AT[:, f, o:o + nn], ps[pu][:, 0:nn], sb[:, 0:nn], ALU.mult,
                               ['ps%d' % pu, 'sgt%d' % (cnt % 2)], ['AT%d_%d' % (f, ti)])
                            for _ in range(2):
                                if side_ops:
                                    P.ops.append(side_ops.pop(0))
                    if fg + 2 < 11:
                        load_gu(fg + 2)
                    if fg == 8:
                        load_wd(0)
                        load_wd(1)
                P.ops.extend(side_ops)
                side_ops = []
                cnt = 0
                for dc in range(8):
                    b = dc % 2
                    for ti, tl in enumerate(tiles):
                        kind, c0, o, nn = tl
                        pb = 4 + cnt % 2
                        cnt += 1
                        for f in range(NF):
                            mm(ps[pb][:, 0:nn], wdb[b][:, f, :], AT[:, f, o:o + nn], f == 0, f == NF - 1,
                               ['wdb%d' % b, 'AT%d_%d' % (f, ti)], ['ps%d' % pb])
                        if kind == 'h':
                            stt('dve', xhT[:, dc, :], ps[pb][:, 0:nn], gatev[:, dc:dc + 1], xhT[:, dc, :],
                                ALU.mult, ALU.add, ['ps%d' % pb, 'xhT', 'vecs'], ['xhT'])
                        else:
                            stt('dve', xT[:, dc, c0:c0 + nn], ps[pb][:, 0:nn], gatev[:, dc:dc + 1],
                                xT[:, dc, c0:c0 + nn], ALU.mult, ALU.add, ['ps%d' % pb, 'xT', 'vecs'], ['xT'])
                    if dc + 2 < 8:
                        load_wd(dc + 2)
            AR.release(m)
            P.barrier()

        def dump(ap, key, ncols):
            if ap.dtype != F32:
                raise ValueError
            dma('sp', dbg[:, 0:ncols], ap, [key], ['dbg'])

        ffn(0, A_(0), SH_(0), GT_(0), True, mid_hook=emit_rope_tables)
        if debug and debug[0] == 'x1':
            dump(xT[:, :, :].rearrange("p k t -> p (k t)"), 'xT', 8 * NT)

        stop_after = debug[2] if debug else None
        if stop_after == 'ffn1':
            P.finalize(st)
            return nc

        qT = AR.alloc([128, 4, NT], BF16)
        ycT = AR.alloc([128, 4, NT], BF16)
        gts = AR.alloc([128, NJ, 24], F32)
        M_R2 = AR.mark()
        WT = NT + HAL
        hT2 = AR.alloc([128, 8, WT], BF16)
        wfm = [AR.alloc([128, 8, 128], BF16) for _ in range(4)]
        wtm = AR.alloc([128, 8, 280], BF16)
        yt = [AR.alloc([128, 512], F32) for _ in range(2)]
        M_A = AR.mark()
        sqb = AR.alloc([128, 8, 512], BF16)
        rsb = AR.alloc([128, 512], F32)
        tmpn = [AR.alloc([128, 512], F32) for _ in range(2)]

        norm_mod(xhT[:, :, :], 'xhT', HAL, A_(1), SH_(1), hT2[:, :, 0:HAL], 'hT2', sqb, rsb, tmpn, ps[6], 'ps6')
        for t in range(4):
            norm_mod(xT[:, :, t * 512:(t + 1) * 512], 'xT', 512, A_(1), SH_(1),
                     hT2[:, :, HAL + t * 512:HAL + (t + 1) * 512], 'hT2', sqb, rsb, tmpn, ps[6], 'ps6')
        dma('sp', xsave.ap(), xT[:, :, :].rearrange("p k t -> p (k t)"), ['xT'], ['xsave'])

        if stop_after == 'A':
            dump(Ctab[:, :], 'Ctab', NT)
            P.finalize(st)
            return nc
        P.barrier()
        AR.release(M_A)
        kst = [AR.alloc([128, NT], BF16) for _ in range(2)]
        Vloc = AR.alloc([128, NJ, 256], BF16)
        wfm_v = w_in_fm.rearrange("(k p) c -> p k c", p=128)
        nload = [0]

        hw_path = [False]
        wst = [Arena(arena_t[:, 49952 + 1024 * i_:49952 + 1024 * (i_ + 1)], 1024).alloc([128, 8, 128], F32)
               for i_ in range(2)]

        def load_fm(ch):
            b = nload[0] % 4
            nload[0] += 1
            if hw_path[0]:
                sb_ = nload[0] % 2
                dma('sp', wst[sb_], wfm_v[:, :, ch * 128:(ch + 1) * 128], [], ['wst%d' % sb_])
                act(wfm[b][:, :, :], wst[sb_][:, :, :], AF.Copy, ['wst%d' % sb_], ['wfm%d' % b])
            else:
                dma('pool', wfm[b], wfm_v[:, :, ch * 128:(ch + 1) * 128], [], ['wfm%d' % b])
            return b

        pcnt = [0]

        def proj(b, o, nn, pbi):
            for kc in range(8):
                mm(ps[pbi][:, 0:nn], wfm[b][:, kc, :], hT2[:, kc, o:o + nn], kc == 0, kc == 7,
                   ['wfm%d' % b, 'hT2'], ['ps%d' % pbi])

        def rope_chunk(ch_x, ch_p, dst_fn, dstkey):
            bx = load_fm(ch_x)
            bp = load_fm(ch_p)
            for t in range(4):
                pa = t % 2
                pb = 2 + t % 2
                proj(bx, HAL + t * 512, 512, pa)
                proj(bp, HAL + t * 512, 512, pb)
                t1 = yt[0]
                t2 = yt[1]
                tt('dve', t1[:], ps[pa][:, :], Ctab[:, t * 512:(t + 1) * 512], ALU.mult, ['ps%d' % pa, 'Ctab'], ['yt0'])
                tt('dve', t2[:], ps[pb][:, :], Stab[:, t * 512:(t + 1) * 512], ALU.mult, ['ps%d' % pb, 'Stab'], ['yt1'])
                tt('pool', dst_fn(t), t1[:], t2[:], ALU.add, ['yt0', 'yt1'], [dstkey])

        xg_in_ap = [t_.ap() for t_ in xg_in]
        dma('pool', wtm[:], w_in_tm.rearrange("(k p) c -> p k c", p=128), [], ['wtm'])
        for j in range(NJ):
            pb = 4 + j % 2
            for kc in range(8):
                mm(ps[pb][:, 0:280], hT2[:, kc, HAL + j * 128:HAL + (j + 1) * 128], wtm[:, kc, :], kc == 0, kc == 7,
                   ['hT2', 'wtm'], ['ps%d' % pb])
            act(Vloc[:, j, :], ps[pb][:, 0:256], AF.Copy, ['ps%d' % pb], ['Vloc'])
            act(gts[:, j, :], ps[pb][:, 256:280], AF.Sigmoid, ['ps%d' % pb], ['gts'])
        dma('sp', xg_in_ap[2][:, :].rearrange("r (c f) -> (r c) f", f=256).rearrange("(j p) f -> p j f", p=128),
            Vloc[:, :, :], ['Vloc'], ['xg_in'], semkey='xg_in_w')
        P.add('pool', lambda e: e.collective_compute("AllGather", ALU.bypass,
                                                     replica_groups=[[0, 1, 2, 3], [4, 5, 6, 7]],
                                                     ins=[xg_in[2].ap().opt()], outs=[xg_out[2].ap().opt()]),
              ['xg_in'], ['xg_out'], dma=True, semkey='cc_xg', inc=1)
        for ki, (chx, chp) in enumerate([(20, 21), (22, 23), (24, 25)]):
            sbuf = kst[ki % 2]
            sk = 'kst%d' % (ki % 2)
            rope_chunk(chx, chp, lambda t, sbuf=sbuf: sbuf[:, t * 512:(t + 1) * 512], sk)
            dma('sp', xg_in_ap[ki // 2][(ki % 2) * 128:(ki % 2 + 1) * 128, :], sbuf[:], [sk], ['xg_in'], semkey='xg_in_w')
        bv = load_fm(26)
        sbuf = kst[1]
        for t in range(4):
            pa = t % 2
            proj(bv, HAL + t * 512, 512, pa)
            act(sbuf[:, t * 512:(t + 1) * 512], ps[pa][:, :], AF.Copy, ['ps%d' % pa], ['kst1'])
        dma('sp', xg_in_ap[1][128:256, :], sbuf[:], ['kst1'], ['xg_in'], semkey='xg_in_w')
        P.add('pool', lambda e: e.collective_compute("AllGather", ALU.bypass,
                                                     replica_groups=[[0, 1, 2, 3], [4, 5, 6, 7]],
                                                     ins=[xg_in[0].ap().opt()], outs=[xg_out[0].ap().opt()]),
              ['xg_in'], ['xg_out'], dma=True, semkey='cc_xg', inc=1)
        P.add('pool', lambda e: e.collective_compute("AllGather", ALU.bypass,
                                                     replica_groups=[[0, 1, 2, 3], [4, 5, 6, 7]],
                                                     ins=[xg_in[1].ap().opt()], outs=[xg_out[1].ap().opt()]),
              ['xg_in'], ['xg_out'], dma=True, semkey='cc_xg', inc=1)
        AR0 = Arena(arena_t[:, 0:8 * NT + 8 * HAL], 8 * NT + 8 * HAL)
        ksT = AR0.alloc([128, S], BF16)
        kwT = AR0.alloc([128, S], BF16)
        Vall = AR0.alloc([128, 64, 4, 65], BF16)
        xgo = [t_.ap() for t_ in xg_out]
        GR = 4

        def src_fm(row0, rr):
            bi, r0 = row0 // 256, row0 % 256
            return xgo[bi][rr * 256 + r0:rr * 256 + r0 + 128, :].rearrange("p (j i) -> p j i", i=128)

        hw_path[0] = True
        AR.release(M_A)
        ccs = AR.alloc([128, WT], F32)
        uT = AR.alloc([128, NJ, 130], F32)
        vT = AR.alloc([128, NJ, 128], F32)
        ysq = AR.alloc([128, 512], BF16)
        yrs = AR.alloc([128, 512], F32)
        assert AR.top <= 45856, AR.top
        for i in range(4):
            bcc = load_fm(4 + i)
            bcx = load_fm(8 + i)
            bcb = load_fm(i)
            proj(bcc, 0, HAL, 0)
            act(ccs[:, 0:HAL], ps[0][:, 0:HAL], AF.Copy, ['ps0'], ['ccs'] + (['kst0', 'kst1', 'Vloc'] if i == 0 else []))
            for t in range(4):
                pb = t % 2
                proj(bcc, HAL + t * 512, 512, pb)
                act(ccs[:, HAL + t * 512:HAL + (t + 1) * 512], ps[pb][:, :], AF.Copy, ['ps%d' % pb], ['ccs'])
            proj(bcx, 0, HAL, 2)
            tt('dve', uT[:, :, 0:2], ps[2][:, 0:HAL].rearrange("p (j c) -> p j c", c=2),
               ccs[:, 0:HAL].rearrange("p (j c) -> p j c", c=2), ALU.mult, ['ps2', 'ccs'], ['uT'])
            tt('dve', uT[:, :, 0:2], uT[:, :, 0:2], hmask[:, :].rearrange("p (j c) -> p j c", c=2), ALU.mult,
               ['uT', 'hmask'], ['uT'])
            for t in range(4):
                pb = 2 + t % 2
                proj(bcx, HAL + t * 512, 512, pb)
                tt('dve', uT[:, 4 * t:4 * t + 4, 2:130], ps[pb][:, :].rearrange("p (j c) -> p j c", c=128),
                   ccs[:, HAL + t * 512:HAL + (t + 1) * 512].rearrange("p (j c) -> p j c", c=128), ALU.mult,
                   ['ps%d' % pb, 'ccs'], ['uT'])
            ts('dve', vT[:, :, :], uT[:, :, 2:130], convw[:, 3 * i + 2:3 * i + 3], None, ALU.mult, None,
               ['uT', 'convw'], ['vT'])
            stt('dve', vT[:, :, :], uT[:, :, 1:129], convw[:, 3 * i + 1:3 * i + 2], vT[:, :, :], ALU.mult, ALU.add,
                ['uT', 'convw', 'vT'], ['vT'])
            stt('dve', vT[:, :, :], uT[:, :, 0:128], convw[:, 3 * i:3 * i + 1], vT[:, :, :], ALU.mult, ALU.add,
                ['uT', 'convw', 'vT'], ['vT'])
            for t in range(4):
                pb = 4 + t % 2
                proj(bcb, HAL + t * 512, 512, pb)
                yb = yt[t % 2]
                yk = 'yt%d' % (t % 2)
                tt('dve', yb[:], ps[pb][:, :], vT[:, 4 * t:4 * t + 4, :].rearrange("p j c -> p (j c)"), ALU.mult,
                   ['ps%d' % pb, 'vT'], [yk])
                act(ysq[:], yb[:], AF.Square, [yk], ['ysq'])
                mm(ps[6][:, :], bdb[:], ysq[:], True, True, ['ysq', 'bdb'], ['ps6'])
                act(yrs[:], ps[6][:, :], AF.Sqrt, ['ps6', 'epsc'], ['yrs'], scale=1.0 / 64, bias=epsc[:, 0:1])
                recip(yrs[:], yrs[:], ['yrs'], ['yrs'])
                stt('dve', ycT[:, i, t * 512:(t + 1) * 512], yb[:], goT[:, i:i + 1], yrs[:], ALU.mult, ALU.mult,
                    [yk, 'yrs', 'goT'], ['ycT'])

        for hg in range(4):
            rope_chunk(12 + hg, 16 + hg, lambda t, hg=hg: qT[:, hg, t * 512:(t + 1) * 512], 'qT')
        if debug and debug[0] == 'qT':
            cp('dve', Ctab[:, :], qT[:, 0, :], ['qT'], ['Ctab'])
            dump(Ctab[:, :], 'Ctab', NT)
        if debug and debug[0] == 'yc':
            cp('dve', Ctab[:, :], ycT[:, 0, :], ['ycT'], ['Ctab'])
            dump(Ctab[:, :], 'Ctab', NT)
        if stop_after == 'tm':
            P.finalize(st)
            return nc
        AR.release(M_R2)
        P.barrier()
        if stop_after == 'inproj':
            P.finalize(st)
            return nc

        yaT = AR.alloc([128, 4, NT], BF16)
        selstat = AR.alloc([128, 4, 128], BF16)
        wmaskt = AR.alloc([128, 8, 128], BF16)
        cmaskt = AR.alloc([128, 19, 128], BF16)
        keepadd = AR.alloc([128, 2 * NJ, 128], BF16)
        kccT = AR.alloc([128, 512], BF16)
        rhsc = AR.alloc([128, 4, 2, 194], BF16)
        Ebuf = [AR.alloc([128, 512], BF16) for _ in range(10)]
        oaccs = [AR.alloc([128, 4, 64], F32) for _ in range(2)]
        imp = AR.alloc([128, 128], F32)
        sct = AR.alloc([128, 128], F32)
        sct2 = AR.alloc([128, 128], F32)
        m8a = AR.alloc([128, 8], F32)
        m8b = AR.alloc([128, 8], F32)
        selb = AR.alloc([128, 128], BF16)
        selT = [AR.alloc([128, 128], BF16) for _ in range(2)]
        zt = AR.alloc([128, 16], F32)
        osq = AR.alloc([128, 4, 64], F32)
        yatm = AR.alloc([128, 512], BF16)
        M_C = AR.mark()

        dma('pool', selstat[:, :, :].rearrange("p a b -> p (a b)"), selstat_d, [], ['selstat'])
        dma('pool', wmaskt[:, :, :].rearrange("p a b -> p (a b)"), wmask_d, [], ['wmaskt'])
        dma('pool', cmaskt[:, :, :].rearrange("p a b -> p (a b)"), cmask_d, [], ['cmaskt'])
        dma('pool', keepadd[:, :, :].rearrange("p a b -> p (a b)"), keepadd_d, [], ['keepadd'])
        memset('pool', rhsc[:, :, :, :], 0.0, ['rhsc'])
        memset('pool', rhsc[:, :, :, 64:65], 1.0, ['rhsc'], ['rhsc'])
        for g in range(2):
            dma('pool', rhsc[:, :, g, 65:193], ovl_d.rearrange("p (c j) -> p c j", j=128), ['rhsc'], ['rhsc'])
        memset('pool', kccT[:, :], 0.0, ['kccT'])

        mc0 = AR.mark()
        srcT = AR.alloc([128, S], BF16)
        w1r = AR.alloc([128, 32, 256], BF16)
        hid = [AR.alloc([128, 2, 512], BF16) for _ in range(2)]
        peb = AR.alloc([128, 64], BF16)
        w2kp = AR.alloc([128, 2, 2, 128], BF16)
        w2vb = AR.alloc([128, 2, 64], BF16)
        biasT = AR.alloc([128, 4], F32)
        dma('pool', peb[:, :], peT_d, [], ['peb'])
        dma('pool', w2kp[:, :, :, :].rearrange("p a b c -> p (a b c)"), w2kpad_d, [], ['w2kp'])
        dma('pool', w2vb[:, :, :].rearrange("p a b -> p (a b)"), w2v_d, [], ['w2vb'])
        for kv in range(2):
            row0 = 0 if kv == 0 else 384
            for rr in range(GR):
                dma('sp', srcT[:, :].rearrange("p (j r i) -> p j r i", r=4, i=128)[:, :, rr, :], src_fm(row0, rr),
                    ['xg_out'], ['srcT'])
            dma('pool', w1r[:, :, :].rearrange("p a b -> p (a b)"), w1r_d[kv], [], ['w1r'])
            if kv == 0:
                memset('pool', Vall[:, :, :, 64:65], 1.0, ['Vall', 'xT', 'xhT'])
                for rr in range(GR):
                    dma('sp', ksT[:, :].rearrange("p (j r i) -> p j r i", r=4, i=128)[:, :, rr, :], src_fm(128, rr),
                        ['xg_out'], ['ksT', 'xT', 'xhT'])
                    dma('sp', kwT[:, :].rearrange("p (j r i) -> p j r i", r=4, i=128)[:, :, rr, :], src_fm(256, rr),
                        ['xg_out'], ['kwT', 'xT', 'xhT'])
                    vsrc = xgo[2][rr * 256:rr * 256 + 256, :].rearrange("r (c f) -> (r c) f", f=256) \
                        .rearrange("(j p) (t d) -> p j t d", p=128, d=64)
                    vdst = Vall[:, :, :, :].rearrange("p (j r) t d -> p j r t d", r=4)
                    for tg in range(4):
                        dma('sp', vdst[:, :, rr, tg, 0:64], vsrc[:, :, tg, :], ['xg_out'], ['Vall', 'xT', 'xhT'])
            for mc in range(2):
                for l in range(32):
                    mm(ps[6][:, mc:mc + 1], w1r[0:64, l, mc * 128:(mc + 1) * 128], peb[0:64, 32 * kv + l:32 * kv + l + 1],
                       l == 0, l == 31, ['w1r', 'peb'], ['ps6'])
            cp('dve', biasT[:, 2 * kv:2 * kv + 2], ps[6][:, 0:2], ['ps6'], ['biasT'])
            for mc in range(2):
                for l in range(32):
                    for g in range(2):
                        rhs = srcT[64 * g:64 * g + 64, l:l + 16 * 510 + 1:16]
                        mm(ps[g][:, 0:511], w1r[64 * g:64 * g + 64, l, mc * 128:(mc + 1) * 128], rhs,
                           l == 0, l == 31, ['w1r', 'srcT'], ['ps%d' % g])
                for g in range(2):
                    act(hid[g][:, mc, 0:511], ps[g][:, 0:511], AF.Silu, ['ps%d' % g, 'biasT'], ['hid%d' % g],
                        bias=biasT[:, 2 * kv + mc:2 * kv + mc + 1])
            if kv == 0:
                n4 = 0
                for g in range(2):
                    for mc in range(2):
                        mm(ps[2][:, 0:511], w2kp[:, mc, g, :], hid[g][:, mc, 0:511], n4 == 0, n4 == 3,
                           ['w2kp', 'hid%d' % g], ['ps2'])
                        n4 += 1
                act(kccT[:, 0:511], ps[2][:, 0:511], AF.Copy, ['ps2', 'kccT'], ['kccT'])
            else:
                for g in range(2):
                    for c4 in range(4):
                        nn = min(128, 511 - 128 * c4)
                        pb = 2 + (c4 % 2)
                        for mc in range(2):
                            mm(ps[pb][0:nn, 0:64], hid[g][:, mc, 128 * c4:128 * c4 + nn], w2vb[:, mc, :], mc == 0, mc == 1,
                               ['hid%d' % g, 'w2vb'], ['ps%d' % pb])
                        act(rhsc[0:nn, c4, g, 0:64], ps[pb][0:nn, 0:64], AF.Copy, ['ps%d' % pb, 'rhsc'], ['rhsc'])
        AR.release(mc0)
        P.barrier()
        wo = AR.alloc([128, 8, D], BF16)
        maskexp = [AR.alloc([128, 64, 128], BF16) for _ in range(2)]
        selsb = AR.alloc([128, 2, 128], BF16)
        dma('pool', wo[:, :, :], w_out_d.rearrange("(m p) d -> p m d", p=128), [], ['wo'])

        psS = [ps[0], ps[1], ps[2]]
        scnt = [0]
        ecnt = [0]
        mcnt = [0]
        NE = len(Ebuf)

        def emit_scores(item):
            nsb = item.get('nsb', 3)
            pi = scnt[0] % nsb
            scnt[0] += 1
            pk = 'ps%d' % pi
            extra = {'ps3': ['pm0', 'pm1'], 'ps4': ['pm2', 'pm3']}.get(pk, [])
            mm(ps[pi][:, :].rearrange("p (h q) -> p h q", h=4), item['k'], item['q'], True, True,
               [item['kkey'], 'qT'], [pk] + extra)
            eb = ecnt[0] % NE
            ecnt[0] += 1
            ek = 'E%d' % eb
            E = Ebuf[eb]
            dyn = item.get('dyn')
            if dyn is not None:
                mi = mcnt[0] % 2
                mcnt[0] += 1
                mk = 'ps%d' % (3 + mi)
                mreg = ps[3 + mi][:, 0:128]
                mm(mreg, dyn[0], dyn[1], True, True, dyn[2], [mk])
            act(E[:, :], ps[pi][:, :], AF.Exp, [pk], [ek], scale=0.125)
            E3 = E[:, :].rearrange("p (h q) -> p h q", h=4)
            if dyn is not None:
                tt('dve', E3, E3, mreg.unsqueeze(1).to_broadcast([128, 4, 128]), ALU.mult, [ek, mk], [ek])
            for (map_, mkeys) in item.get('stat', []):
                mcnt[0] += 1
                tt('dve', E3, E3, map_.unsqueeze(1).to_broadcast([128, 4, 128]), ALU.mult,
                   [ek] + mkeys, [ek])
            item['E'] = E
            item['ek'] = ek

        def emit_pv(item):
            E, ek = item['E'], item['ek']
            for h in range(4):
                mm(item['out'](h), E[:, h * 128:(h + 1) * 128], item['v'], item['first'] and item['hfirst'](h),
                   item['last'], [ek, item['vkey']], [item['okey'](h)], skip=True)

        def run_pipeline(items, depth=4, pair=1):
            n_ = len(items)
            for i in range(min(depth, n_)):
                emit_scores(items[i])
            for i in range(0, n_, pair):
                for k_ in range(pair):
                    if i + k_ + depth < n_:
                        emit_scores(items[i + k_ + depth])
                for k_ in range(pair):
                    if i + k_ < n_:
                        emit_pv(items[i + k_])
                        if items[i + k_].get('after') is not None:
                            items[i + k_]['after']()

        for j in range(NJ):
            cl = j // 4
            rhs_qs = [qT[64 * g:64 * g + 64, :, j * 128:(j + 1) * 128] for g in range(2)]
            gates = [gts[:, j, 12 * g:12 * g + 12].rearrange("p (h t) -> p h t", t=3) for g in range(2)]
            for g in range(2):
                gsl = slice(64 * g, 64 * g + 64)
                items = []
                for c in range(cl + 1):
                    stat = []
                    if c == cl:
                        stat.append((cmaskt[:, cm_base(j), :], ['cmaskt']))
                    elif c == cl - 1 and j % 4 == 0:
                        stat.append((cmaskt[:, cm_base(j) + 1, :], ['cmaskt']))
                    items.append(dict(
                        k=kccT[gsl, 128 * c:128 * (c + 1)], kkey='kccT', q=rhs_qs[g], stat=stat,
                        v=rhsc[:, c, g, 0:193], vkey='rhsc', first=(c == 0), last=(c == cl),
                        hfirst=lambda h: h % 2 == 0,
                        out=lambda h: ps[3 + h // 2][:, 0:386].rearrange("p (h x) -> p h x", h=2)[:, h % 2, :],
                        okey=lambda h: 'ps%d' % (3 + h // 2)))
                run_pipeline(items)
                for hp in range(2):
                    ts('dve', zt[:, 2 * hp:2 * hp + 2],
                       ps[3 + hp][:, 0:386].rearrange("p (h x) -> p h x", h=2)[:, :, 64], 1e-30, None, ALU.max, None,
                       ['ps%d' % (3 + hp)], ['zt'])
                recip(zt[:, 0:4], zt[:, 0:4], ['zt'], ['zt'])
                tt('dve', zt[:, 4:8], zt[:, 0:4], gates[g][:, :, 0], ALU.mult, ['zt', 'gts'], ['ztg'])
                for h in range(4):
                    pv = ps[3 + h // 2][:, 0:386].rearrange("p (h x) -> p h x", h=2)
                    ts('dve', oaccs[g][:, h, :], pv[:, h % 2, 0:64], zt[:, 4 + h:5 + h], None, ALU.mult, None,
                       ['ps%d' % (3 + h // 2), 'ztg'], ['oacc%d' % g])
                    if h == 0:
                        ts('dve', imp[:, :], pv[:, 0, 65:193], zt[:, 0:1], None, ALU.mult, None,
                           ['ps3', 'zt'], ['imp'])
                    else:
                        stt('dve', imp[:, :], pv[:, h % 2, 65:193], zt[:, h:h + 1], imp[:, :], ALU.mult, ALU.add,
                            ['ps%d' % (3 + h // 2), 'zt', 'imp'], ['imp'])
                tt('dve', sct[:, :], imp[:, :], keepadd[:, 2 * j, :], ALU.mult, ['imp', 'keepadd'], ['sct'])
                tt('dve', sct[:, :], sct[:, :], keepadd[:, 2 * j + 1, :], ALU.add, ['sct', 'keepadd'], ['sct'])
                P.add('dve', lambda e: e.max(out=m8a[:, :], in_=sct[:, :]), ['sct'], ['m8a'])
                P.add('dve', lambda e: e.match_replace(out=sct2[:, :], in_to_replace=m8a[:, :], in_values=sct[:, :],
                                                       imm_value=-3.0e38), ['sct', 'm8a'], ['sct2'])
                P.add('dve', lambda e: e.max(out=m8b[:, :], in_=sct2[:, :]), ['sct2'], ['m8b'])
                ts('dve', selb[:, :], sct[:, :], m8b[:, 7:8], None, ALU.is_ge, None, ['sct', 'm8b'], ['selb'])
                tr(pst[0:64, 0:128], selb[:, 0:128:2], identb[:], ['selb', 'identb'], ['pst'])
                tr(pst[0:64, 640:768], selb[:, 1:128:2], identb[:], ['selb', 'identb'], ['pst'])
                cp('dve', selsb[0:64, 0, :], pst[0:64, 0:128], ['pst'], ['selsb'])
                cp('dve', selsb[0:64, 1, :], pst[0:64, 640:768], ['pst', 'selsb'], ['selsb'])
                nck_ = 4 * j + 4
                dma('sp', selD[g].ap().rearrange("h (c q) -> c h q", q=128), selsb[0:64, :, :], ['selsb'],
                    ['selD%d' % g])
                for hh in range(2):
                    srcb = selD[g].ap()[hh:hh + 1, 0:nck_ * 128].to_broadcast([64, nck_ * 128]) \
                        .rearrange("p (c q) -> p c q", q=128)
                    dma('sp', maskexp[g][64 * hh:64 * hh + 64, 0:nck_, :], srcb, ['selD%d' % g], ['mexp%d' % g])

            def make_after(g, bi, pb):
                def _after():
                    pv = ps[pb][:, 0:260].rearrange("p (h x) -> p h x", h=4)
                    ts('dve', zt[:, 8:12], pv[:, :, 64], 1e-30, None, ALU.max, None, ['ps%d' % pb], ['zt2'])
                    recip(zt[:, 8:12], zt[:, 8:12], ['zt2'], ['zt2'])
                    tt('dve', zt[:, 12:16], zt[:, 8:12], gates[g][:, :, bi], ALU.mult, ['zt2', 'gts'], ['zt2g'])
                    for h in range(4):
                        stt('dve', oaccs[g][:, h, :], pv[:, h, 0:64], zt[:, 12 + h:13 + h], oaccs[g][:, h, :],
                            ALU.mult, ALU.add, ['ps%d' % pb, 'zt2g', 'oacc%d' % g], ['oacc%d' % g])
                    if bi == 1:
                        tt('dve', osq[:, :, :], oaccs[g][:, :, :], oaccs[g][:, :, :], ALU.mult, ['oacc%d' % g], ['osq'])
                        P.add('dve', lambda e: e.tensor_reduce(out=zt[:, 8:12], in_=osq[:, :, :], axis=AX.X, op=ALU.add),
                              ['osq'], ['zt3'])
                        act(zt[:, 8:12], zt[:, 8:12], AF.Sqrt, ['zt3', 'epsc'], ['zt3'], scale=1.0 / 64, bias=epsc[:, 0:1])
                        recip(zt[:, 8:12], zt[:, 8:12], ['zt3'], ['zt3'])
                        tt('dve', yatm[:, 256 * g:256 * (g + 1)].rearrange("p (h d) -> p h d", h=4), oaccs[g][:, :, :],
                           zt[:, 8:12].unsqueeze(2).to_broadcast([128, 4, 64]), ALU.mult, ['oacc%d' % g, 'zt3'], ['yatm'])
                return _after

            items = []
            wl = [w for w in range(8) if 4 * j - 4 + w >= 0]
            for wi, w in enumerate(wl):
                for g in range(2):
                    gsl = slice(64 * g, 64 * g + 64)
                    c = 4 * j - 4 + w
                    items.append(dict(
                        k=kwT[gsl, 128 * c:128 * (c + 1)], kkey='kwT', q=rhs_qs[g],
                        stat=[(wmaskt[:, w, :], ['wmaskt'])],
                        v=Vall[:, c, 2 + g, :], vkey='Vall', first=(wi == 0), last=(wi == len(wl) - 1),
                        hfirst=lambda h: h == 0,
                        out=lambda h, g=g: ps[5 + g][:, 0:260].rearrange("p (h x) -> p h x", h=4)[:, h, :],
                        okey=lambda h, g=g: 'ps%d' % (5 + g),
                        after=make_after(g, 2, 5 + g) if wi == len(wl) - 1 else None))
            nck = 4 * j + 4
            for c in range(nck):
                for g in range(2):
                    gsl = slice(64 * g, 64 * g + 64)
                    stat = [(maskexp[g][:, c, :], ['mexp%d' % g])]
                    if c >= 4 * j:
                        stat.append((selstat[:, c - 4 * j, :], ['selstat']))
                    items.append(dict(
                        k=ksT[gsl, 128 * c:128 * (c + 1)], kkey='ksT', q=rhs_qs[g], stat=stat,
                        v=Vall[:, c, g, :], vkey='Vall', first=(c == 0), last=(c == nck - 1),
                        hfirst=lambda h: h == 0,
                        out=lambda h, g=g: ps[5 + g][:, 0:260].rearrange("p (h x) -> p h x", h=4)[:, h, :],
                        okey=lambda h, g=g: 'ps%d' % (5 + g),
                        after=make_after(g, 1, 5 + g) if c == nck - 1 else None))
            for it_ in items:
                it_['nsb'] = 5
            run_pipeline(items, depth=8, pair=2)
            for ch in range(4):
                tr(pst[:, 128 + ch * 128:256 + ch * 128], yatm[:, ch * 128:(ch + 1) * 128], identb[:],
                   ['yatm', 'identb'], ['pst2'])
            for ch in range(4):
                act(yaT[:, ch, j * 128:(j + 1) * 128], pst[:, 128 + ch * 128:256 + ch * 128], AF.Copy,
                    ['pst2', 'goT'], ['yaT'], scale=goT[:, 4 + ch:5 + ch])
        if debug and debug[0] == 'ya':
            P.barrier()
            cp('dve', arena_t[:, 0:NT], yaT[:, 0, :], ['yaT'], ['dbgt'])
            dump(arena_t[:, 0:NT], 'dbgt', NT)
        P.barrier()

        dma('sp', xT[:, :, :].rearrange("p k t -> p (k t)"), xsave.ap(), ['xsave'], ['xT'])
        cnt = 0
        for t in range(4):
            for dc in range(8):
                pb = cnt % 2
                cnt += 1
                for m in range(8):
                    rhs = ycT[:, m, t * 512:(t + 1) * 512] if m < 4 else yaT[:, m - 4, t * 512:(t + 1) * 512]
                    mm(ps[pb][:, :], wo[:, m, dc * 128:(dc + 1) * 128], rhs, m == 0, m == 7,
                       ['wo', 'ycT', 'yaT'], ['ps%d' % pb])
                stt('dve', xT[:, dc, t * 512:(t + 1) * 512], ps[pb][:, :], GT_(1)[:, dc:dc + 1],
                    xT[:, dc, t * 512:(t + 1) * 512], ALU.mult, ALU.add, ['ps%d' % pb, 'xT', 'vecs'], ['xT'])
        AR.release(M_R1)
        P.barrier()
        if debug and debug[0] == 'x2':
            dump(xT[:, :, :].rearrange("p k t -> p (k t)"), 'xT', 8 * NT)

        ffn(1, A_(2), SH_(2), GT_(2), False)

        sqb = AR.alloc([128, 8, 512], BF16)
        rsb = AR.alloc([128, 512], F32)
        yfin = AR.alloc([128, 8, 512], F32)
        otm = [AR.alloc([128, D], F32) for _ in range(2)]
        for t in range(4):
            norm_mod(xT[:, :, t * 512:(t + 1) * 512], 'xT', 512, None, None, None, None, sqb, rsb, None, ps[6], 'ps6')
            for kc in range(8):
                stt('dve', yfin[:, kc, :], xT[:, kc, t * 512:(t + 1) * 512], gv[:, 24 + kc:25 + kc], rsb[:, :],
                    ALU.mult, ALU.mult, ['xT', 'rsb', 'gv'], ['yfin'])
            for jj in range(4):
                ob = otm[jj % 2]
                ok = 'otm%d' % (jj % 2)
                for hf in range(2):
                    pb = 2 * (jj % 2) + hf
                    for k4 in range(4):
                        kc = hf * 4 + k4
                        tr(ps[pb][:, k4 * 128:(k4 + 1) * 128], yfin[:, kc, jj * 128:(jj + 1) * 128], ident32[:],
                           ['yfin', 'ident32'], ['ps%d' % pb])
                    if hf == 0:
                        act(ob[:, 0:512], ps[pb][:, :], AF.Copy, ['ps%d' % pb], [ok])
                    else:
                        cp('dve', ob[:, 512:1024], ps[pb][:, :], ['ps%d' % pb], [ok])
                row = (4 * t + jj) * 128
                dma('sp', out_loc[row:row + 128, :], ob[:, :], [ok], ['out'], semkey='out')
        P.finalize(st)
    return nc


def cm_base(j):
    idx = 0
    for jj in range(j):
        idx += 2 if (jj % 4 == 0 and jj > 0) else 1
    return idx


_PROG = {}


def _pvec(v, nch=8):
    return np.ascontiguousarray(np.asarray(v, np.float32).reshape(nch, 128).T)


def _partner(d):
    return d + 8 if d < 8 else (d - 8 if d < 16 else d)


def _host_shared(c, positions, w_ada, b_ada, g_ffn1, w1_gate, w1_up, w1_down, g_mix, w_in, conv_w, cmp_pos_k,
                 cmp_pos_v, w_cmpk1, w_cmpk2, w_cmpv1, w_cmpv2, g_out_conv, g_out_attn, w_out, g_ffn2, w2_gate,
                 w2_up, w2_down, g_final):
    f = lambda a: np.ascontiguousarray(np.asarray(a, np.float32))
    sh = {}
    sh["_w_ada"] = f(w_ada[0])
    sh["_b_adaT"] = np.ascontiguousarray(f(b_ada[0]).reshape(72, 128).T)
    sh["gvecs"] = np.ascontiguousarray(np.concatenate([_pvec(g_ffn1[0]), _pvec(g_mix[0]), _pvec(g_ffn2[0]),
                                                       _pvec(g_final)], axis=1))
    sh["w1_gate"], sh["w1_up"], sh["w1_down"] = f(w1_gate[0]), f(w1_up[0]), f(w1_down[0])
    sh["w2_gate"], sh["w2_up"], sh["w2_down"] = f(w2_gate[0]), f(w2_up[0]), f(w2_down[0])
    win = f(w_in[0])
    cols = []
    for base in (0, 512, 1024):
        cols += list(range(base, base + 512))
    qb0 = 1536
    for hg in range(4):
        cols += [qb0 + 64 * hg + d for d in range(64)] + [qb0 + 64 * (4 + hg) + d for d in range(64)]
    for hg in range(4):
        cols += [qb0 + 64 * hg + _partner(d) for d in range(64)] + [qb0 + 64 * (4 + hg) + _partner(d) for d in range(64)]
    for kb in (2048, 2304, 2560):
        cols += [kb + i for i in range(128)]
        cols += [kb + 64 * g + _partner(d) for g in range(2) for d in range(64)]
    cols += [2176 + i for i in range(128)]
    assert len(cols) == 27 * 128
    sh["w_in_fm"] = np.ascontiguousarray(win[:, cols])
    tcols = list(range(2432, 2560)) + list(range(2688, 2816)) + list(range(2816, 2840))
    sh["w_in_tm"] = np.ascontiguousarray(win[:, tcols])
    cw = f(conv_w[0])
    sh["conv_wT"] = np.ascontiguousarray(cw.reshape(3, 4, 128).transpose(2, 1, 0).reshape(128, 12))
    sh["g_oT"] = np.ascontiguousarray(np.concatenate([_pvec(g_out_conv[0], 4), _pvec(g_out_attn[0], 4)], axis=1))
    sh["w_out"] = f(w_out[0])
    for nm, w1 in (("w1k_r", w_cmpk1), ("w1v_r", w_cmpv1)):
        a = f(w1[0]).reshape(32, 64, 256).transpose(1, 0, 2)
        sh[nm] = np.ascontiguousarray(np.concatenate([a, a], 0).reshape(128, 32 * 256))
    pk = f(cmp_pos_k[0]).T
    pv = f(cmp_pos_v[0]).T
    pe = np.concatenate([pk, pv], 1)
    sh["peT"] = np.ascontiguousarray(np.concatenate([pe, pe], 0))
    w2k = f(w_cmpk2[0]).reshape(2, 128, 64)
    pad = np.zeros((128, 2, 2, 128), np.float32)
    for mc in range(2):
        for g in range(2):
            pad[:, mc, g, 64 * g:64 * g + 64] = w2k[mc]
    sh["w2kpad"] = pad.reshape(128, 512)
    sh["w2v"] = np.ascontiguousarray(f(w_cmpv2[0]).reshape(2, 128, 64).transpose(1, 0, 2).reshape(128, 128))
    fr = np.zeros((128, 2), np.float32)
    freqs = np.power(np.float32(500000.0), (-2.0 * np.arange(8, dtype=np.float32) / np.float32(16.0))).astype(np.float32)
    for p in range(128):
        d = p % 64
        if d < 16:
            fr[p, 0] = freqs[d % 8]
            fr[p, 1] = -1.0 if d < 8 else 1.0
    sh["freqs"] = fr
    n_cmp = 511
    c0 = np.arange(n_cmp) * 16
    c1 = c0 + 31
    s0 = np.arange(128) * 64
    s1 = s0 + 63
    ov = ((c0[:, None] <= s1[None, :]) & (c1[:, None] >= s0[None, :])).astype(np.float32)
    ovp = np.zeros((512, 128), np.float32)
    ovp[:511] = ov
    sh["ovl"] = np.ascontiguousarray(ovp.reshape(4, 128, 128).transpose(1, 0, 2).reshape(128, 512))
    return sh


def _host_core(core, x, c, positions):
    b, r = core // 4, core % 4
    m = {}
    xb = np.asarray(x[b], np.float32).reshape(64, 128, D)
    m["x_loc"] = np.ascontiguousarray(xb[r::4].reshape(NT, D))
    xh = np.zeros((HAL, D), np.float32)
    hm = np.zeros((128, HAL), np.float32)
    xflat = np.asarray(x[b], np.float32)
    for j in range(NJ):
        qb = 4 * j + r
        if qb > 0:
            xh[2 * j] = xflat[128 * qb - 2]
            xh[2 * j + 1] = xflat[128 * qb - 1]
            hm[:, 2 * j:2 * j + 2] = 1.0
    m["x_halo"] = xh
    m["halo_mask"] = hm
    m["cT"] = _pvec(np.asarray(c[b], np.float32))
    pb = np.asarray(positions[b], np.int32).reshape(64, 128)[r::4].reshape(NT)
    m["pos"] = np.ascontiguousarray(np.tile(pb[None, :], (128, 1)))
    ik = np.arange(128)[:, None]
    iq = np.arange(128)[None, :]
    causal = np.where(ik <= iq, 1.0, 0.0).astype(np.float32)
    full = np.zeros((128, 128), np.float32)
    zero = np.ones((128, 128), np.float32)
    ss = np.stack([zero if rp < r else (causal if rp == r else full) for rp in range(4)], 1)
    m["selstat"] = np.ascontiguousarray(ss.reshape(128, 512))
    wm = []
    edge = np.where(ik > iq, 1.0, 0.0).astype(np.float32)
    for w in range(8):
        dd = w - 4 - r
        if dd < -4 or dd > 0:
            wm.append(full)
        elif dd == -4:
            wm.append(edge)
        elif dd == 0:
            wm.append(causal)
        else:
            wm.append(zero)
    m["wmask"] = np.ascontiguousarray(np.stack(wm, 1).reshape(128, 1024))
    cms = []
    for j in range(NJ):
        qb = 4 * j + r
        cl = j // 4
        chunks = [cl] + ([cl - 1] if (j % 4 == 0 and j > 0) else [])
        for cc in chunks:
            ig = 128 * cc + np.arange(128)[:, None]
            t = 128 * qb + np.arange(128)[None, :]
            valid = (16 * ig + 31 <= t) & (ig < 511)
            cms.append(np.where(valid, 1.0, 0.0).astype(np.float32))
    assert len(cms) == 19
    m["cmask"] = np.ascontiguousarray(np.stack(cms, 1).reshape(128, 19 * 128))
    ka = np.zeros((128, 2 * NJ, 128), np.float32)
    jb = np.arange(128)[None, :]
    for j in range(NJ):
        qb = 4 * j + r
        cur = (2 * qb + (np.arange(128) >= 64).astype(np.int64))[:, None]
        keep = np.ones((128, 128), np.float32)
        add = np.zeros((128, 128), np.float32)
        f0 = (jb == 0) & np.ones((128, 1), bool)
        fm1 = jb == cur - 1
        fc = jb == cur
        fut = jb > cur
        add[f0] = 1024.0
        add[fm1] = 2048.0
        add[fc] = 4096.0
        add[fut] = -1.0e9
        keep[f0 | fm1 | fc | fut] = 0.0
        ka[:, 2 * j] = keep
        ka[:, 2 * j + 1] = add
    m["keepadd"] = np.ascontiguousarray(ka.reshape(128, 2 * NJ * 128))
    return m


def kernel(x, c, positions, w_ada, b_ada, g_ffn1, w1_gate, w1_up, w1_down, g_mix, w_in, conv_w, cmp_pos_k,
           cmp_pos_v, w_cmpk1, w_cmpk2, w_cmpv1, w_cmpv2, g_out_conv, g_out_attn, w_out, g_ffn2, w2_gate,
           w2_up, w2_down, g_final, _debug=None):
    x = np.asarray(x)
    sh = _host_shared(c, positions, w_ada, b_ada, g_ffn1, w1_gate, w1_up, w1_down, g_mix, w_in, conv_w, cmp_pos_k,
                      cmp_pos_v, w_cmpk1, w_cmpk2, w_cmpv1, w_cmpv2, g_out_conv, g_out_attn, w_out, g_ffn2,
                      w2_gate, w2_up, w2_down, g_final)
    key = str(_debug)
    if key not in _PROG:
        _PROG[key] = build_program(_debug)
    nc = _PROG[key]
    in_maps = []
    for core in range(NCORES):
        m = {k_: v_ for k_, v_ in sh.items() if not k_.startswith("_")}
        m.update(_host_core(core, x, np.asarray(c), np.asarray(positions)))
        rq = core % 4
        m["w_ada_q"] = np.ascontiguousarray(sh["_w_ada"][:, rq * 2304:(rq + 1) * 2304])
        m["b_adaT_q"] = np.ascontiguousarray(sh["_b_adaT"][:, rq * 18:(rq + 1) * 18])
        in_maps.append(m)
    res = run_bass_kernel_spmd(nc, in_maps, core_ids=list(range(NCORES)))
    if _debug is not None:
        return res
    out = np.zeros((2, 64, 128, D), np.float32)
    for core in range(NCORES):
        b, r = core // 4, core % 4
        out[b, r::4] = np.asarray(res.results[core]["out_loc"], np.float32).reshape(NJ, 128, D)
    return out.reshape(2, S, D)
```

```python
import numpy as np
from contextlib import ExitStack
import concourse.bass as bass
import concourse.mybir as mybir
from concourse.bass_utils import run_bass_kernel_spmd

F32 = mybir.dt.float32
BF16 = mybir.dt.bfloat16
I32 = mybir.dt.int32
AF = mybir.ActivationFunctionType
ALU = mybir.AluOpType
AX = mybir.AxisListType

NCORES = 8
D = 1024
S = 8192
NT = 2048
NJ = 16
HAL = 32
DFF = 2816
NF = 22
EPS = 1e-6
NEGM = -30000.0
XROWS = 768
DEBUG = None


class Prog:
    def __init__(self, nc, same_engine_sync=True):
        self.nc = nc
        self.ops = []
        self.same_engine_sync = same_engine_sync

    def add(self, eng, fn, reads=(), writes=(), dma=False, semkey=None, inc=16):
        self.ops.append(dict(eng=eng, fn=fn, reads=tuple(reads), writes=tuple(writes),
                             dma=dma, semkey=semkey, barrier=False, inc=inc))

    def barrier(self):
        for e in ('pe', 'act', 'dve', 'pool', 'sp'):
            self.ops.append(dict(eng=e, fn=None, reads=(), writes=(), dma=False,
                                 semkey=None, barrier=True))

    def finalize(self, stack):
        nc = self.nc
        ops = self.ops
        n = len(ops)
        last_w = {}
        readers = {}
        deps = [set() for _ in range(n)]
        last_on_eng = {}
        all_dmas = []
        for i, op in enumerate(ops):
            if op['barrier']:
                for e, j in last_on_eng.items():
                    deps[i].add(j)
                for j in all_dmas:
                    deps[i].add(j)
                continue
            for k in op['reads']:
                if k in last_w:
                    deps[i].add(last_w[k])
            for k in op['writes']:
                if k in last_w:
                    deps[i].add(last_w[k])
                for r in readers.get(k, ()):
                    deps[i].add(r)
            for k in op['reads']:
                readers.setdefault(k, []).append(i)
            for k in op['writes']:
                last_w[k] = i
                readers[k] = []
            deps[i].discard(i)
            if op['dma']:
                all_dmas.append(i)
            else:
                last_on_eng[op['eng']] = i
        needed = set()
        for i in range(n):
            op = ops[i]
            for d in deps[i]:
                od = ops[d]
                if od['barrier'] or od['dma']:
                    continue
                if od['eng'] == op['eng'] and not op['dma']:
                    if od['eng'] == 'pe' or not self.same_engine_sync:
                        continue
                needed.add(d)
        eng_sem = {}
        for e in ('pe', 'act', 'dve', 'pool'):
            eng_sem[e] = stack.enter_context(nc.semaphore("sem_" + e))
        dma_sem = {}
        eng_cnt = {e: 0 for e in eng_sem}
        dma_cnt = {}
        sig = [None] * n
        for i, op in enumerate(ops):
            if op['barrier']:
                continue
            if op['dma']:
                k = op['semkey']
                if k not in dma_sem:
                    dma_sem[k] = stack.enter_context(nc.semaphore("dsem_%d" % len(dma_sem)))
                    dma_cnt[k] = 0
                dma_cnt[k] += op['inc']
                sig[i] = (dma_sem[k], dma_cnt[k], op['inc'])
            elif i in needed:
                e = op['eng']
                eng_cnt[e] += 1
                sig[i] = (eng_sem[e], eng_cnt[e], 1)
        self.n_sems = len(dma_sem) + 4
        waited = {e: {} for e in ('pe', 'act', 'dve', 'pool', 'sp')}
        waits = [[] for _ in range(n)]
        for i, op in enumerate(ops):
            e = op['eng']
            req = {}
            for d in deps[i]:
                if sig[d] is None:
                    continue
                if (not ops[d]['dma']) and d not in needed:
                    continue
                if (not ops[d]['dma']) and ops[d]['eng'] == e and not op['dma'] and \
                        (e == 'pe' or not self.same_engine_sync):
                    continue
                s, v, _ = sig[d]
                key = id(s)
                if key not in req or req[key][1] < v:
                    req[key] = (s, v)
            for key, (s, v) in req.items():
                if waited[e].get(key, 0) >= v:
                    continue
                waited[e][key] = v
                waits[i].append((s, v))
        final_dma = [(s, dma_cnt[k]) for k, s in dma_sem.items()]
        block = stack.enter_context(nc.Block())

        def emitter(ename):
            def _f(eng):
                for i, op in enumerate(ops):
                    if op['eng'] != ename:
                        continue
                    for (s, v) in waits[i]:
                        eng.wait_ge(s, v)
                    if op['fn'] is None:
                        continue
                    inst = op['fn'](eng)
                    if sig[i] is not None:
                        s, v, inc = sig[i]
                        inst.then_inc(s, inc)
                if ename == 'sp':
                    for (s, v) in final_dma:
                        eng.wait_ge(s, v)
            return _f

        block.tensor(emitter('pe'))
        block.scalar(emitter('act'))
        block.vector(emitter('dve'))
        block.gpsimd(emitter('pool'))
        block.sync(emitter('sp'))


class Arena:
    def __init__(self, ap, nwords):
        self.ap = ap
        self.n = nwords
        self.top = 0
        self.peak = 0

    def mark(self):
        return self.top

    def release(self, m):
        self.top = m

    def alloc(self, shape, dt):
        nel = int(np.prod(shape[1:]))
        nw = nel if dt in (F32, I32) else (nel + 1) // 2
        a = self.top
        self.top += nw
        self.peak = max(self.peak, self.top)
        assert self.top <= self.n, ("arena overflow", self.top, self.n)
        v = self.ap[:, a:a + nw]
        if dt != F32:
            v = v.bitcast(dt)
            if dt == BF16 and nel % 2:
                v = v[:, 0:nel]
        if len(shape) == 3:
            v = v.rearrange("p (a b) -> p a b", a=shape[1])
        elif len(shape) == 4:
            v = v.rearrange("p (a b c) -> p a b c", a=shape[1], b=shape[2])
        return v


def build_program(debug=None):
    nc = bass.Bass("TRN2", target_bir_lowering=False)

    def din(name, shape, dt=F32):
        return nc.dram_tensor(name, list(shape), dt, kind="ExternalInput").ap()

    x_loc = din("x_loc", [NT, D])
    x_halo = din("x_halo", [HAL, D])
    halo_mask = din("halo_mask", [128, HAL])
    cT_d = din("cT", [128, 8])
    pos_d = din("pos", [128, NT], I32)
    w_ada = din("w_ada_q", [D, 2304])
    b_adaT = din("b_adaT_q", [128, 18])
    gvecs = din("gvecs", [128, 32])
    wg_d = [din("w1_gate", [D, DFF]), din("w2_gate", [D, DFF])]
    wu_d = [din("w1_up", [D, DFF]), din("w2_up", [D, DFF])]
    wd_d = [din("w1_down", [DFF, D]), din("w2_down", [DFF, D])]
    NFM = 27
    w_in_fm = din("w_in_fm", [D, NFM * 128])
    w_in_tm = din("w_in_tm", [D, 280])
    conv_wT = din("conv_wT", [128, 12])
    g_oT = din("g_oT", [128, 8])
    w_out_d = din("w_out", [D, D])
    w1r_d = [din("w1k_r", [128, 32 * 256]), din("w1v_r", [128, 32 * 256])]
    peT_d = din("peT", [128, 64])
    w2kpad_d = din("w2kpad", [128, 2 * 2 * 128])
    w2v_d = din("w2v", [128, 2 * 64])
    freqs_d = din("freqs", [128, 2])
    selstat_d = din("selstat", [128, 4 * 128])
    wmask_d = din("wmask", [128, 8 * 128])
    cmask_d = din("cmask", [128, 19 * 128])
    keepadd_d = din("keepadd", [128, 2 * NJ * 128])
    ovl_d = din("ovl", [128, 4 * 128])
    out_loc = nc.dram_tensor("out_loc", [NT, D], F32, kind="ExternalOutput").ap()
    dbg = None
    if debug is not None:
        dbg = nc.dram_tensor("dbg", [128, debug[1]], F32, kind="ExternalOutput").ap()
    xg_in = [nc.dram_tensor("xg_in%d" % i, [256, NT], BF16) for i in range(3)]
    xg_out = [nc.dram_tensor("xg_out%d" % i, [4 * 256, NT], BF16) for i in range(3)]
    xsave = nc.dram_tensor("xsave", [128, 8 * NT], F32)
    mg_in = nc.dram_tensor("mg_in", [18, 128], F32)
    mg_out = nc.dram_tensor("mg_out", [72, 128], F32)
    selD = [nc.dram_tensor("selD%d" % i, [2, 64 * 128], BF16) for i in range(2)]

    st = ExitStack()
    with st:
        def T(name, shape, dt):
            return st.enter_context(nc.sbuf_tensor(name, list(shape), dt))

        AW = 52000
        arena_t = T("arena", [128, AW], F32)
        AR = Arena(arena_t[:, :], AW)
        ident32 = T("ident32", [128, 128], F32)
        identb = T("identb", [128, 128], BF16)
        onesb = T("onesb", [128, 128], BF16)
        bdb = T("bdb", [128, 128], BF16)
        vecs = T("vecs", [128, 160], F32)
        gv = T("gv", [128, 32], F32)
        cT = T("cTt", [128, 16], F32)
        convw = T("convw", [128, 12], F32)
        goT = T("goT", [128, 8], F32)
        frq = T("frq", [128, 4], F32)
        hmask = T("hmask", [128, HAL], F32)
        epsc = T("epsc", [128, 2], F32)
        mqT = T("mqT", [128, 128], F32)
        mgT = T("mgT", [128, 128], F32)
        ps = [st.enter_context(nc.psum_tensor("ps%d" % i, [128, 512], F32)) for i in range(7)]
        pst = st.enter_context(nc.psum_tensor("pst", [128, 1024], BF16))

        P = Prog(nc)
        _cnt = [0]

        def uid(p):
            _cnt[0] += 1
            return "%s#%d" % (p, _cnt[0])

        def dma(q, out, in_, reads, writes, semkey=None):
            P.add(q, lambda e: e.dma_start(out=out, in_=in_), reads, writes, dma=True,
                  semkey=semkey or writes[0])

        def mm(out, lhsT, rhs, start, stop, reads, writes, skip=False):
            P.add('pe', lambda e: e.matmul(out, lhsT=lhsT, rhs=rhs, start=start, stop=stop,
                                           skip_group_check=skip), reads, writes)

        def tr(out, in_, ident, reads, writes):
            P.add('pe', lambda e: e.transpose(out=out, in_=in_, identity=ident), reads, writes)

        def act(out, in_, func, reads, writes, scale=1.0, bias=None):
            if bias is None:
                P.add('act', lambda e: e.activation(out=out, in_=in_, func=func, scale=scale), reads, writes)
            else:
                P.add('act', lambda e: e.activation(out=out, in_=in_, func=func, scale=scale, bias=bias),
                      reads, writes)

        def tt(eng, out, in0, in1, op, reads, writes):
            P.add(eng, lambda e: e.tensor_tensor(out=out, in0=in0, in1=in1, op=op), reads, writes)

        def ts(eng, out, in0, s1, s2, op0, op1, reads, writes):
            if s2 is None:
                P.add(eng, lambda e: e.tensor_scalar(out=out, in0=in0, scalar1=s1, scalar2=None, op0=op0),
                      reads, writes)
            else:
                P.add(eng, lambda e: e.tensor_scalar(out=out, in0=in0, scalar1=s1, scalar2=s2, op0=op0, op1=op1),
                      reads, writes)

        def stt(eng, out, in0, scalar, in1, op0, op1, reads, writes):
            P.add(eng, lambda e: e.scalar_tensor_tensor(out=out, in0=in0, scalar=scalar, in1=in1, op0=op0, op1=op1),
                  reads, writes)

        def cp(eng, out, in_, reads, writes):
            P.add(eng, lambda e: e.tensor_copy(out=out, in_=in_), reads, writes)

        def memset(eng, ap, val, writes, reads=()):
            P.add(eng, lambda e: e.memset(ap, val), reads, writes)

        def recip(out, in_, reads, writes):
            P.add('dve', lambda e: e.reciprocal(out=out, in_=in_), reads, writes)

        memset('pool', ident32[:], 0.0, ['ident32'])
        P.add('pool', lambda e: e.affine_select(out=ident32[:], in_=ident32[:], pattern=[[-1, 128]],
                                                compare_op=ALU.not_equal, fill=1.0, base=0, channel_multiplier=1),
              ['ident32'], ['ident32'])
        cp('dve', identb[:], ident32[:], ['ident32'], ['identb'])
        memset('pool', onesb[:], 1.0, ['onesb'])
        memset('pool', bdb[:], 0.0, ['bdb'])
        memset('pool', bdb[0:64, 0:64], 1.0, ['bdb'], ['bdb'])
        memset('pool', bdb[64:128, 64:128], 1.0, ['bdb'], ['bdb'])
        memset('pool', epsc[:], EPS, ['epsc'])
        dma('sp', gv[:], gvecs, [], ['gv'])
        dma('sp', cT[:, 0:8], cT_d, [], ['cT'])
        dma('sp', convw[:], conv_wT, [], ['convw'])
        dma('sp', goT[:], g_oT, [], ['goT'])
        dma('sp', frq[:, 0:2], freqs_d, [], ['frq'])
        dma('sp', hmask[:], halo_mask, [], ['hmask'])
        dma('sp', vecs[:, 128:146], b_adaT, [], ['modq'])

        xT = AR.alloc([128, 8, NT], F32)
        xhT = AR.alloc([128, 8, HAL], F32)
        M_R1 = AR.mark()

        m0 = AR.mark()
        xtm = [AR.alloc([128, D], F32) for _ in range(2)]
        xhm = AR.alloc([128, D], F32)
        wab = [AR.alloc([128, 2304], F32) for _ in range(2)]
        for j in range(NJ):
            b = j % 2
            dma('sp', xtm[b], x_loc[j * 128:(j + 1) * 128, :], [], ['xtm%d' % b])
            for hf in range(2):
                pb = ps[hf]
                for k4 in range(4):
                    kc = hf * 4 + k4
                    tr(pb[:, k4 * 128:(k4 + 1) * 128], xtm[b][:, kc * 128:(kc + 1) * 128], ident32[:],
                       ['xtm%d' % b, 'ident32'], ['ps%d' % hf])
                eng = 'act' if hf == 0 else 'dve'
                if eng == 'act':
                    act(xT[:, hf * 4:(hf + 1) * 4, j * 128:(j + 1) * 128],
                        pb[:, :].rearrange("p (k t) -> p k t", k=4), AF.Copy, ['ps%d' % hf], ['xT'])
                else:
                    cp('dve', xT[:, hf * 4:(hf + 1) * 4, j * 128:(j + 1) * 128],
                       pb[:, :].rearrange("p (k t) -> p k t", k=4), ['ps%d' % hf], ['xT'])
        dma('sp', xhm[0:HAL, :], x_halo, [], ['xhm'])
        for kc in range(8):
            tr(ps[2][:, kc * HAL:(kc + 1) * HAL], xhm[0:HAL, kc * 128:(kc + 1) * 128], ident32[0:HAL, 0:HAL],
               ['xhm', 'ident32'], ['ps2'])
        cp('dve', xhT[:, :, :], ps[2][:, 0:8 * HAL].rearrange("p (k t) -> p k t", k=8), ['ps2'], ['xhT'])

        act(cT[:, 8:16], cT[:, 0:8], AF.Silu, ['cT'], ['cact'])
        for kc in range(8):
            b = kc % 2
            dma('sp', wab[b], w_ada[kc * 128:(kc + 1) * 128, :], [], ['wab%d' % b])
            for m in range(18):
                mm(ps[3][:, m:m + 1], wab[b][:, m * 128:(m + 1) * 128], cT[:, 8 + kc:9 + kc], True, True,
                   ['wab%d' % b, 'cact'], ['ps3'])
            tt('dve', vecs[:, 128:146], vecs[:, 128:146], ps[3][:, 0:18], ALU.add, ['ps3', 'modq'], ['modq'])
        tr(ps[3][0:18, 128:256], vecs[:, 128:146], ident32[:], ['modq', 'ident32'], ['ps3'])
        cp('dve', mqT[0:18, :], ps[3][0:18, 128:256], ['ps3'], ['mqT'])
        dma('sp', mg_in.ap(), mqT[0:18, :], ['mqT'], ['mg_in'])
        P.add('pool', lambda e: e.collective_compute("AllGather", ALU.bypass,
                                                     replica_groups=[[0, 1, 2, 3], [4, 5, 6, 7]],
                                                     ins=[mg_in.ap().opt()], outs=[mg_out.ap().opt()]),
              ['mg_in'], ['mg_out'], dma=True, semkey='cc_mg', inc=1)
        dma('sp', mgT[0:72, :], mg_out.ap(), ['mg_out'], ['mgT'])
        tr(ps[3][:, 256:328], mgT[0:72, :], ident32[0:72, 0:72], ['mgT', 'ident32'], ['ps3'])
        cp('dve', vecs[:, 0:72], ps[3][:, 256:328], ['ps3'], ['vecs'])
        for i in range(3):
            stt('dve', vecs[:, 72 + 8 * i:80 + 8 * i], vecs[:, 24 * i + 8:24 * i + 16], 1.0, gv[:, 8 * i:8 * i + 8],
                ALU.add, ALU.mult, ['vecs', 'gv'], ['vecs'])
            ts('dve', vecs[:, 96 + 8 * i:104 + 8 * i], vecs[:, 24 * i + 16:24 * i + 24],
               0.5 if i != 1 else 1.0, None, ALU.mult, None, ['vecs'], ['vecs'])

        def A_(i):
            return vecs[:, 72 + 8 * i:80 + 8 * i]

        def SH_(i):
            return vecs[:, 24 * i:24 * i + 8]

        def GT_(i):
            return vecs[:, 96 + 8 * i:104 + 8 * i]

        AR.release(m0)
        P.barrier()

        def norm_mod(src, srckey, n, av, shv, dst, dstkey, sqb, rsb, tmpn, psb, psk):
            act(sqb[:, :, 0:n], src, AF.Square, [srckey], ['sqb'])
            for kc in range(8):
                mm(psb[:, 0:n], onesb[:], sqb[:, kc, 0:n], kc == 0, kc == 7, ['sqb', 'onesb'], [psk])
            act(rsb[:, 0:n], psb[:, 0:n], AF.Sqrt, [psk, 'epsc'], ['rsb'], scale=1.0 / D, bias=epsc[:, 0:1])
            recip(rsb[:, 0:n], rsb[:, 0:n], ['rsb'], ['rsb'])
            if dst is None:
                return
            for kc in range(8):
                tb = tmpn[kc % 2]
                stt('dve', tb[:, 0:n], src[:, kc, :], av[:, kc:kc + 1], rsb[:, 0:n], ALU.mult, ALU.mult,
                    [srckey, 'rsb', 'vecs'], ['tmpn%d' % (kc % 2)])
                act(dst[:, kc, :], tb[:, 0:n], AF.Identity, ['tmpn%d' % (kc % 2), 'vecs'], [dstkey],
                    bias=shv[:, kc:kc + 1])

        RT = Arena(arena_t[:, 45856:AW], AW - 45856)
        Ctab = RT.alloc([128, NT], F32)
        Stab = RT.alloc([128, NT], F32)
        posi = RT.alloc([128, 512], I32)
        angb = RT.alloc([128, 512], F32)
        tqb = RT.alloc([128, 512], F32)
        kib = RT.alloc([128, 512], I32)

        def emit_rope_tables():
            TWO_PI = float(2 * np.pi)

            def range_reduce(y, key):
                ts('dve', tqb[:], y, 1.0 / TWO_PI, None, ALU.mult, None, [key], ['tqb'])
                cp('dve', kib[:], tqb[:], ['tqb'], ['kib'])
                cp('dve', tqb[:], kib[:], ['kib'], ['tqb'])
                stt('dve', y, tqb[:], -TWO_PI, y, ALU.mult, ALU.add, ['tqb', key], [key])
                ts('dve', tqb[:], y, float(np.pi), -TWO_PI, ALU.is_gt, ALU.mult, [key], ['tqb'])
                tt('dve', y, y, tqb[:], ALU.add, [key, 'tqb'], [key])
                ts('dve', tqb[:], y, float(-np.pi), TWO_PI, ALU.is_lt, ALU.mult, [key], ['tqb'])
                tt('dve', y, y, tqb[:], ALU.add, [key, 'tqb'], [key])

            for t in range(4):
                dma('sp', posi[:], pos_d[:, t * 512:(t + 1) * 512], [], ['posi'])
                cp('dve', angb[:], posi[:], ['posi'], ['angb'])
                ts('dve', angb[:], angb[:], frq[:, 0:1], None, ALU.mult, None, ['angb', 'frq'], ['angb'])
                cp('dve', Stab[:, t * 512:(t + 1) * 512], angb[:], ['angb'], ['Stab'])
                ts('dve', angb[:], angb[:], float(np.pi / 2), None, ALU.add, None, ['angb'], ['angb'])
                range_reduce(angb[:], 'angb')
                act(Ctab[:, t * 512:(t + 1) * 512], angb[:], AF.Sin, ['angb'], ['Ctab'])
                cp('dve', angb[:], Stab[:, t * 512:(t + 1) * 512], ['Stab'], ['angb'])
                range_reduce(angb[:], 'angb')
                act(angb[:], angb[:], AF.Sin, ['angb'], ['angb'])
                ts('dve', Stab[:, t * 512:(t + 1) * 512], angb[:], frq[:, 1:2], None, ALU.mult, None,
                   ['angb', 'frq'], ['Stab'])

        def ffn(li, av, shv, gatev, with_halo, mid_hook=None):
            m = AR.mark()
            W = 1024 + (HAL if with_halo else 0)
            AT = AR.alloc([128, NF, 1056], BF16)
            hT = AR.alloc([128, 8, 1056], BF16)
            wgb = [AR.alloc([128, 8, 256], BF16) for _ in range(2)]
            wub = [AR.alloc([128, 8, 256], BF16) for _ in range(2)]
            wdb = [AR.alloc([128, NF, 128], BF16) for _ in range(2)]
            sqb = AR.alloc([128, 8, 512], BF16)
            rsb = AR.alloc([128, 512], F32)
            tmpn = [AR.alloc([128, 512], F32) for _ in range(2)]
            sgt = [AR.alloc([128, 512], F32) for _ in range(2)]
            wg_v = wg_d[li].rearrange("(k p) c -> p k c", p=128)
            wu_v = wu_d[li].rearrange("(k p) c -> p k c", p=128)
            wd_v = wd_d[li].rearrange("(f p) d -> p f d", p=128)
            for pi in range(2):
                tiles = []
                off = 0
                if pi == 0 and with_halo:
                    tiles.append(('h', 0, off, HAL))
                    off += HAL
                for t2 in range(2):
                    tiles.append(('m', pi * 1024 + t2 * 512, off, 512))
                    off += 512

                def xsrc(tl):
                    kind, c0, o, nn = tl
                    if kind == 'h':
                        return xhT[:, :, :], 'xhT'
                    return xT[:, :, c0:c0 + nn], 'xT'

                for ti, tl in enumerate(tiles):
                    src, sk = xsrc(tl)
                    norm_mod(src, sk, tl[3], av, shv, hT[:, :, tl[2]:tl[2] + tl[3]], 'hT%d' % ti,
                             sqb, rsb, tmpn, ps[6], 'ps6')
                side_ops = []
                if pi == 0 and mid_hook is not None:
                    saved_ops = P.ops
                    P.ops = []
                    mid_hook()
                    side_ops = P.ops
                    P.ops = saved_ops

                def load_gu(fg):
                    b = fg % 2
                    dma('pool', wgb[b], wg_v[:, :, fg * 256:(fg + 1) * 256], [], ['wgb%d' % b])
                    dma('pool', wub[b], wu_v[:, :, fg * 256:(fg + 1) * 256], [], ['wub%d' % b])

                def load_wd(dc):
                    b = dc % 2
                    dma('pool', wdb[b], wd_v[:, :, dc * 128:(dc + 1) * 128], [], ['wdb%d' % b])

                load_gu(0)
                load_gu(1)
                cnt = 0
                for fg in range(11):
                    b = fg % 2
                    for fc in range(2):
                        f = fg * 2 + fc
                        for ti, tl in enumerate(tiles):
                            _, _, o, nn = tl
                            pg = cnt % 2
                            pu = 2 + cnt % 2
                            cnt += 1
                            for kc in range(8):
                                mm(ps[pg][:, 0:nn], wgb[b][:, kc, fc * 128:(fc + 1) * 128], hT[:, kc, o:o + nn],
                                   kc == 0, kc == 7, ['wgb%d' % b, 'hT%d' % ti], ['ps%d' % pg])
                            for kc in range(8):
                                mm(ps[pu][:, 0:nn], wub[b][:, kc, fc * 128:(fc + 1) * 128], hT[:, kc, o:o + nn],
                                   kc == 0, kc == 7, ['wub%d' % b, 'hT%d' % ti], ['ps%d' % pu])
                            sb = sgt[cnt % 2]
                            act(sb[:, 0:nn], ps[pg][:, 0:nn], AF.Silu, ['ps%d' % pg], ['sgt%d' % (cnt % 2)])
                            tt('dve', AT[:, f, o:o + nn], ps[pu][:, 0:nn], sb[:, 0:nn], ALU.mult,
                               ['ps%d' % pu, 'sgt%d' % (cnt % 2)], ['AT%d_%d' % (f, ti)])
                            for _ in range(2):
                                if side_ops:
                                    P.ops.append(side_ops.pop(0))
                    if fg + 2 < 11:
                        load_gu(fg + 2)
                    if fg == 8:
                        load_wd(0)
                        load_wd(1)
                P.ops.extend(side_ops)
                side_ops = []
                cnt = 0
                for dc in range(8):
                    b = dc % 2
                    for ti, tl in enumerate(tiles):
                        kind, c0, o, nn = tl
                        pb = 4 + cnt % 2
                        cnt += 1
                        for f in range(NF):
                            mm(ps[pb][:, 0:nn], wdb[b][:, f, :], AT[:, f, o:o + nn], f == 0, f == NF - 1,
                               ['wdb%d' % b, 'AT%d_%d' % (f, ti)], ['ps%d' % pb])
                        if kind == 'h':
                            stt('dve', xhT[:, dc, :], ps[pb][:, 0:nn], gatev[:, dc:dc + 1], xhT[:, dc, :],
                                ALU.mult, ALU.add, ['ps%d' % pb, 'xhT', 'vecs'], ['xhT'])
                        else:
                            stt('dve', xT[:, dc, c0:c0 + nn], ps[pb][:, 0:nn], gatev[:, dc:dc + 1],
                                xT[:, dc, c0:c0 + nn], ALU.mult, ALU.add, ['ps%d' % pb, 'xT', 'vecs'], ['xT'])
                    if dc + 2 < 8:
                        load_wd(dc + 2)
            AR.release(m)
            P.barrier()

        def dump(ap, key, ncols):
            if ap.dtype != F32:
                raise ValueError
            dma('sp', dbg[:, 0:ncols], ap, [key], ['dbg'])

        ffn(0, A_(0), SH_(0), GT_(0), True, mid_hook=emit_rope_tables)
        if debug and debug[0] == 'x1':
            dump(xT[:, :, :].rearrange("p k t -> p (k t)"), 'xT', 8 * NT)

        stop_after = debug[2] if debug else None
        if stop_after == 'ffn1':
            P.finalize(st)
            return nc

        qT = AR.alloc([128, 4, NT], BF16)
        ycT = AR.alloc([128, 4, NT], BF16)
        gts = AR.alloc([128, NJ, 24], F32)
        M_R2 = AR.mark()
        WT = NT + HAL
        hT2 = AR.alloc([128, 8, WT], BF16)
        wfm = [AR.alloc([128, 8, 128], BF16) for _ in range(4)]
        wtm = AR.alloc([128, 8, 280], BF16)
        yt = [AR.alloc([128, 512], F32) for _ in range(2)]
        M_A = AR.mark()
        sqb = AR.alloc([128, 8, 512], BF16)
        rsb = AR.alloc([128, 512], F32)
        tmpn = [AR.alloc([128, 512], F32) for _ in range(2)]

        norm_mod(xhT[:, :, :], 'xhT', HAL, A_(1), SH_(1), hT2[:, :, 0:HAL], 'hT2', sqb, rsb, tmpn, ps[6], 'ps6')
        for t in range(4):
            norm_mod(xT[:, :, t * 512:(t + 1) * 512], 'xT', 512, A_(1), SH_(1),
                     hT2[:, :, HAL + t * 512:HAL + (t + 1) * 512], 'hT2', sqb, rsb, tmpn, ps[6], 'ps6')
        dma('sp', xsave.ap(), xT[:, :, :].rearrange("p k t -> p (k t)"), ['xT'], ['xsave'])

        if stop_after == 'A':
            dump(Ctab[:, :], 'Ctab', NT)
            P.finalize(st)
            return nc
        P.barrier()
        AR.release(M_A)
        kst = [AR.alloc([128, NT], BF16) for _ in range(2)]
        Vloc = AR.alloc([128, NJ, 256], BF16)
        wfm_v = w_in_fm.rearrange("(k p) c -> p k c", p=128)
        nload = [0]

        hw_path = [False]
        wst = [Arena(arena_t[:, 49952 + 1024 * i_:49952 + 1024 * (i_ + 1)], 1024).alloc([128, 8, 128], F32)
               for i_ in range(2)]

        def load_fm(ch):
            b = nload[0] % 4
            nload[0] += 1
            if hw_path[0]:
                sb_ = nload[0] % 2
                dma('sp', wst[sb_], wfm_v[:, :, ch * 128:(ch + 1) * 128], [], ['wst%d' % sb_])
                act(wfm[b][:, :, :], wst[sb_][:, :, :], AF.Copy, ['wst%d' % sb_], ['wfm%d' % b])
            else:
                dma('pool', wfm[b], wfm_v[:, :, ch * 128:(ch + 1) * 128], [], ['wfm%d' % b])
            return b

        pcnt = [0]

        def proj(b, o, nn, pbi):
            for kc in range(8):
                mm(ps[pbi][:, 0:nn], wfm[b][:, kc, :], hT2[:, kc, o:o + nn], kc == 0, kc == 7,
                   ['wfm%d' % b, 'hT2'], ['ps%d' % pbi])

        def rope_chunk(ch_x, ch_p, dst_fn, dstkey):
            bx = load_fm(ch_x)
            bp = load_fm(ch_p)
            for t in range(4):
                pa = t % 2
                pb = 2 + t % 2
                proj(bx, HAL + t * 512, 512, pa)
                proj(bp, HAL + t * 512, 512, pb)
                t1 = yt[0]
                t2 = yt[1]
                tt('dve', t1[:], ps[pa][:, :], Ctab[:, t * 512:(t + 1) * 512], ALU.mult, ['ps%d' % pa, 'Ctab'], ['yt0'])
                tt('dve', t2[:], ps[pb][:, :], Stab[:, t * 512:(t + 1) * 512], ALU.mult, ['ps%d' % pb, 'Stab'], ['yt1'])
                tt('pool', dst_fn(t), t1[:], t2[:], ALU.add, ['yt0', 'yt1'], [dstkey])

        xg_in_ap = [t_.ap() for t_ in xg_in]
        dma('pool', wtm[:], w_in_tm.rearrange("(k p) c -> p k c", p=128), [], ['wtm'])
        for j in range(NJ):
            pb = 4 + j % 2
            for kc in range(8):
                mm(ps[pb][:, 0:280], hT2[:, kc, HAL + j * 128:HAL + (j + 1) * 128], wtm[:, kc, :], kc == 0, kc == 7,
                   ['hT2', 'wtm'], ['ps%d' % pb])
            act(Vloc[:, j, :], ps[pb][:, 0:256], AF.Copy, ['ps%d' % pb], ['Vloc'])
            act(gts[:, j, :], ps[pb][:, 256:280], AF.Sigmoid, ['ps%d' % pb], ['gts'])
        dma('sp', xg_in_ap[2][:, :].rearrange("r (c f) -> (r c) f", f=256).rearrange("(j p) f -> p j f", p=128),
            Vloc[:, :, :], ['Vloc'], ['xg_in'], semkey='xg_in_w')
        P.add('pool', lambda e: e.collective_compute("AllGather", ALU.bypass,
                                                     replica_groups=[[0, 1, 2, 3], [4, 5, 6, 7]],
                                                     ins=[xg_in[2].ap().opt()], outs=[xg_out[2].ap().opt()]),
              ['xg_in'], ['xg_out'], dma=True, semkey='cc_xg', inc=1)
        for ki, (chx, chp) in enumerate([(20, 21), (22, 23), (24, 25)]):
            sbuf = kst[ki % 2]
            sk = 'kst%d' % (ki % 2)
            rope_chunk(chx, chp, lambda t, sbuf=sbuf: sbuf[:, t * 512:(t + 1) * 512], sk)
            dma('sp', xg_in_ap[ki // 2][(ki % 2) * 128:(ki % 2 + 1) * 128, :], sbuf[:], [sk], ['xg_in'], semkey='xg_in_w')
        bv = load_fm(26)
        sbuf = kst[1]
        for t in range(4):
            pa = t % 2
            proj(bv, HAL + t * 512, 512, pa)
            act(sbuf[:, t * 512:(t + 1) * 512], ps[pa][:, :], AF.Copy, ['ps%d' % pa], ['kst1'])
        dma('sp', xg_in_ap[1][128:256, :], sbuf[:], ['kst1'], ['xg_in'], semkey='xg_in_w')
        P.add('pool', lambda e: e.collective_compute("AllGather", ALU.bypass,
                                                     replica_groups=[[0, 1, 2, 3], [4, 5, 6, 7]],
                                                     ins=[xg_in[0].ap().opt()], outs=[xg_out[0].ap().opt()]),
              ['xg_in'], ['xg_out'], dma=True, semkey='cc_xg', inc=1)
        P.add('pool', lambda e: e.collective_compute("AllGather", ALU.bypass,
                                                     replica_groups=[[0, 1, 2, 3], [4, 5, 6, 7]],
                                                     ins=[xg_in[1].ap().opt()], outs=[xg_out[1].ap().opt()]),
              ['xg_in'], ['xg_out'], dma=True, semkey='cc_xg', inc=1)
        AR0 = Arena(arena_t[:, 0:8 * NT + 8 * HAL], 8 * NT + 8 * HAL)
        ksT = AR0.alloc([128, S], BF16)
        kwT = AR0.alloc([128, S], BF16)
        Vall = AR0.alloc([128, 64, 4, 65], BF16)
        xgo = [t_.ap() for t_ in xg_out]
        GR = 4

        def src_fm(row0, rr):
            bi, r0 = row0 // 256, row0 % 256
            return xgo[bi][rr * 256 + r0:rr * 256 + r0 + 128, :].rearrange("p (j i) -> p j i", i=128)

        hw_path[0] = True
        AR.release(M_A)
        ccs = AR.alloc([128, WT], F32)
        uT = AR.alloc([128, NJ, 130], F32)
        vT = AR.alloc([128, NJ, 128], F32)
        ysq = AR.alloc([128, 512], BF16)
        yrs = AR.alloc([128, 512], F32)
        assert AR.top <= 45856, AR.top
        for i in range(4):
            bcc = load_fm(4 + i)
            bcx = load_fm(8 + i)
            bcb = load_fm(i)
            proj(bcc, 0, HAL, 0)
            act(ccs[:, 0:HAL], ps[0][:, 0:HAL], AF.Copy, ['ps0'], ['ccs'] + (['kst0', 'kst1', 'Vloc'] if i == 0 else []))
            for t in range(4):
                pb = t % 2
                proj(bcc, HAL + t * 512, 512, pb)
                act(ccs[:, HAL + t * 512:HAL + (t + 1) * 512], ps[pb][:, :], AF.Copy, ['ps%d' % pb], ['ccs'])
            proj(bcx, 0, HAL, 2)
            tt('dve', uT[:, :, 0:2], ps[2][:, 0:HAL].rearrange("p (j c) -> p j c", c=2),
               ccs[:, 0:HAL].rearrange("p (j c) -> p j c", c=2), ALU.mult, ['ps2', 'ccs'], ['uT'])
            tt('dve', uT[:, :, 0:2], uT[:, :, 0:2], hmask[:, :].rearrange("p (j c) -> p j c", c=2), ALU.mult,
               ['uT', 'hmask'], ['uT'])
            for t in range(4):
                pb = 2 + t % 2
                proj(bcx, HAL + t * 512, 512, pb)
                tt('dve', uT[:, 4 * t:4 * t + 4, 2:130], ps[pb][:, :].rearrange("p (j c) -> p j c", c=128),
                   ccs[:, HAL + t * 512:HAL + (t + 1) * 512].rearrange("p (j c) -> p j c", c=128), ALU.mult,
                   ['ps%d' % pb, 'ccs'], ['uT'])
            ts('dve', vT[:, :, :], uT[:, :, 2:130], convw[:, 3 * i + 2:3 * i + 3], None, ALU.mult, None,
               ['uT', 'convw'], ['vT'])
            stt('dve', vT[:, :, :], uT[:, :, 1:129], convw[:, 3 * i + 1:3 * i + 2], vT[:, :, :], ALU.mult, ALU.add,
                ['uT', 'convw', 'vT'], ['vT'])
            stt('dve', vT[:, :, :], uT[:, :, 0:128], convw[:, 3 * i:3 * i + 1], vT[:, :, :], ALU.mult, ALU.add,
                ['uT', 'convw', 'vT'], ['vT'])
            for t in range(4):
                pb = 4 + t % 2
                proj(bcb, HAL + t * 512, 512, pb)
                yb = yt[t % 2]
                yk = 'yt%d' % (t % 2)
                tt('dve', yb[:], ps[pb][:, :], vT[:, 4 * t:4 * t + 4, :].rearrange("p j c -> p (j c)"), ALU.mult,
                   ['ps%d' % pb, 'vT'], [yk])
                act(ysq[:], yb[:], AF.Square, [yk], ['ysq'])
                mm(ps[6][:, :], bdb[:], ysq[:], True, True, ['ysq', 'bdb'], ['ps6'])
                act(yrs[:], ps[6][:, :], AF.Sqrt, ['ps6', 'epsc'], ['yrs'], scale=1.0 / 64, bias=epsc[:, 0:1])
                recip(yrs[:], yrs[:], ['yrs'], ['yrs'])
                stt('dve', ycT[:, i, t * 512:(t + 1) * 512], yb[:], goT[:, i:i + 1], yrs[:], ALU.mult, ALU.mult,
                    [yk, 'yrs', 'goT'], ['ycT'])

        for hg in range(4):
            rope_chunk(12 + hg, 16 + hg, lambda t, hg=hg: qT[:, hg, t * 512:(t + 1) * 512], 'qT')
        if debug and debug[0] == 'qT':
            cp('dve', Ctab[:, :], qT[:, 0, :], ['qT'], ['Ctab'])
            dump(Ctab[:, :], 'Ctab', NT)
        if debug and debug[0] == 'yc':
            cp('dve', Ctab[:, :], ycT[:, 0, :], ['ycT'], ['Ctab'])
            dump(Ctab[:, :], 'Ctab', NT)
        if stop_after == 'tm':
            P.finalize(st)
            return nc
        AR.release(M_R2)
        P.barrier()
        if stop_after == 'inproj':
            P.finalize(st)
            return nc

        yaT = AR.alloc([128, 4, NT], BF16)
        selstat = AR.alloc([128, 4, 128], BF16)
        wmaskt = AR.alloc([128, 8, 128], BF16)
        cmaskt = AR.alloc([128, 19, 128], BF16)
        keepadd = AR.alloc([128, 2 * NJ, 128], BF16)
        kccT = AR.alloc([128, 512], BF16)
        rhsc = AR.alloc([128, 4, 2, 194], BF16)
        Ebuf = [AR.alloc([128, 512], BF16) for _ in range(10)]
        oaccs = [AR.alloc([128, 4, 64], F32) for _ in range(2)]
        imp = AR.alloc([128, 128], F32)
        sct = AR.alloc([128, 128], F32)
        sct2 = AR.alloc([128, 128], F32)
        m8a = AR.alloc([128, 8], F32)
        m8b = AR.alloc([128, 8], F32)
        selb = AR.alloc([128, 128], BF16)
        selT = [AR.alloc([128, 128], BF16) for _ in range(2)]
        zt = AR.alloc([128, 16], F32)
        osq = AR.alloc([128, 4, 64], F32)
        yatm = AR.alloc([128, 512], BF16)
        M_C = AR.mark()

        dma('pool', selstat[:, :, :].rearrange("p a b -> p (a b)"), selstat_d, [], ['selstat'])
        dma('pool', wmaskt[:, :, :].rearrange("p a b -> p (a b)"), wmask_d, [], ['wmaskt'])
        dma('pool', cmaskt[:, :, :].rearrange("p a b -> p (a b)"), cmask_d, [], ['cmaskt'])
        dma('pool', keepadd[:, :, :].rearrange("p a b -> p (a b)"), keepadd_d, [], ['keepadd'])
        memset('pool', rhsc[:, :, :, :], 0.0, ['rhsc'])
        memset('pool', rhsc[:, :, :, 64:65], 1.0, ['rhsc'], ['rhsc'])
        for g in range(2):
            dma('pool', rhsc[:, :, g, 65:193], ovl_d.rearrange("p (c j) -> p c j", j=128), ['rhsc'], ['rhsc'])
        memset('pool', kccT[:, :], 0.0, ['kccT'])

        mc0 = AR.mark()
        srcT = AR.alloc([128, S], BF16)
        w1r = AR.alloc([128, 32, 256], BF16)
        hid = [AR.alloc([128, 2, 512], BF16) for _ in range(2)]
        peb = AR.alloc([128, 64], BF16)
        w2kp = AR.alloc([128, 2, 2, 128], BF16)
        w2vb = AR.alloc([128, 2, 64], BF16)
        biasT = AR.alloc([128, 4], F32)
        dma('pool', peb[:, :], peT_d, [], ['peb'])
        dma('pool', w2kp[:, :, :, :].rearrange("p a b c -> p (a b c)"), w2kpad_d, [], ['w2kp'])
        dma('pool', w2vb[:, :, :].rearrange("p a b -> p (a b)"), w2v_d, [], ['w2vb'])
        for kv in range(2):
            row0 = 0 if kv == 0 else 384
            for rr in range(GR):
                dma('sp', srcT[:, :].rearrange("p (j r i) -> p j r i", r=4, i=128)[:, :, rr, :], src_fm(row0, rr),
                    ['xg_out'], ['srcT'])
            dma('pool', w1r[:, :, :].rearrange("p a b -> p (a b)"), w1r_d[kv], [], ['w1r'])
            if kv == 0:
                memset('pool', Vall[:, :, :, 64:65], 1.0, ['Vall', 'xT', 'xhT'])
                for rr in range(GR):
                    dma('sp', ksT[:, :].rearrange("p (j r i) -> p j r i", r=4, i=128)[:, :, rr, :], src_fm(128, rr),
                        ['xg_out'], ['ksT', 'xT', 'xhT'])
                    dma('sp', kwT[:, :].rearrange("p (j r i) -> p j r i", r=4, i=128)[:, :, rr, :], src_fm(256, rr),
                        ['xg_out'], ['kwT', 'xT', 'xhT'])
                    vsrc = xgo[2][rr * 256:rr * 256 + 256, :].rearrange("r (c f) -> (r c) f", f=256) \
                        .rearrange("(j p) (t d) -> p j t d", p=128, d=64)
                    vdst = Vall[:, :, :, :].rearrange("p (j r) t d -> p j r t d", r=4)
                    for tg in range(4):
                        dma('sp', vdst[:, :, rr, tg, 0:64], vsrc[:, :, tg, :], ['xg_out'], ['Vall', 'xT', 'xhT'])
            for mc in range(2):
                for l in range(32):
                    mm(ps[6][:, mc:mc + 1], w1r[0:64, l, mc * 128:(mc + 1) * 128], peb[0:64, 32 * kv + l:32 * kv + l + 1],
                       l == 0, l == 31, ['w1r', 'peb'], ['ps6'])
            cp('dve', biasT[:, 2 * kv:2 * kv + 2], ps[6][:, 0:2], ['ps6'], ['biasT'])
            for mc in range(2):
                for l in range(32):
                    for g in range(2):
                        rhs = srcT[64 * g:64 * g + 64, l:l + 16 * 510 + 1:16]
                        mm(ps[g][:, 0:511], w1r[64 * g:64 * g + 64, l, mc * 128:(mc + 1) * 128], rhs,
                           l == 0, l == 31, ['w1r', 'srcT'], ['ps%d' % g])
                for g in range(2):
                    act(hid[g][:, mc, 0:511], ps[g][:, 0:511], AF.Silu, ['ps%d' % g, 'biasT'], ['hid%d' % g],
                        bias=biasT[:, 2 * kv + mc:2 * kv + mc + 1])
            if kv == 0:
                n4 = 0
                for g in range(2):
                    for mc in range(2):
                        mm(ps[2][:, 0:511], w2kp[:, mc, g, :], hid[g][:, mc, 0:511], n4 == 0, n4 == 3,
                           ['w2kp', 'hid%d' % g], ['ps2'])
                        n4 += 1
                act(kccT[:, 0:511], ps[2][:, 0:511], AF.Copy, ['ps2', 'kccT'], ['kccT'])
            else:
                for g in range(2):
                    for c4 in range(4):
                        nn = min(128, 511 - 128 * c4)
                        pb = 2 + (c4 % 2)
                        for mc in range(2):
                            mm(ps[pb][0:nn, 0:64], hid[g][:, mc, 128 * c4:128 * c4 + nn], w2vb[:, mc, :], mc == 0, mc == 1,
                               ['hid%d' % g, 'w2vb'], ['ps%d' % pb])
                        act(rhsc[0:nn, c4, g, 0:64], ps[pb][0:nn, 0:64], AF.Copy, ['ps%d' % pb, 'rhsc'], ['rhsc'])
        AR.release(mc0)
        P.barrier()
        wo = AR.alloc([128, 8, D], BF16)
        maskexp = [AR.alloc([128, 64, 128], BF16) for _ in range(2)]
        selsb = AR.alloc([128, 2, 128], BF16)
        dma('pool', wo[:, :, :], w_out_d.rearrange("(m p) d -> p m d", p=128), [], ['wo'])

        psS = [ps[0], ps[1], ps[2]]
        scnt = [0]
        ecnt = [0]
        mcnt = [0]
        NE = len(Ebuf)

        def emit_scores(item):
            nsb = item.get('nsb', 3)
            pi = scnt[0] % nsb
            scnt[0] += 1
            pk = 'ps%d' % pi
            extra = {'ps3': ['pm0', 'pm1'], 'ps4': ['pm2', 'pm3']}.get(pk, [])
            mm(ps[pi][:, :].rearrange("p (h q) -> p h q", h=4), item['k'], item['q'], True, True,
               [item['kkey'], 'qT'], [pk] + extra)
            eb = ecnt[0] % NE
            ecnt[0] += 1
            ek = 'E%d' % eb
            E = Ebuf[eb]
            dyn = item.get('dyn')
            if dyn is not None:
                mi = mcnt[0] % 2
                mcnt[0] += 1
                mk = 'ps%d' % (3 + mi)
                mreg = ps[3 + mi][:, 0:128]
                mm(mreg, dyn[0], dyn[1], True, True, dyn[2], [mk])
            act(E[:, :], ps[pi][:, :], AF.Exp, [pk], [ek], scale=0.125)
            E3 = E[:, :].rearrange("p (h q) -> p h q", h=4)
            if dyn is not None:
                tt('dve', E3, E3, mreg.unsqueeze(1).to_broadcast([128, 4, 128]), ALU.mult, [ek, mk], [ek])
            for (map_, mkeys) in item.get('stat', []):
                mcnt[0] += 1
                tt('dve', E3, E3, map_.unsqueeze(1).to_broadcast([128, 4, 128]), ALU.mult,
                   [ek] + mkeys, [ek])
            item['E'] = E
            item['ek'] = ek

        def emit_pv(item):
            E, ek = item['E'], item['ek']
            for h in range(4):
                mm(item['out'](h), E[:, h * 128:(h + 1) * 128], item['v'], item['first'] and item['hfirst'](h),
                   item['last'], [ek, item['vkey']], [item['okey'](h)], skip=True)

        def run_pipeline(items, depth=4, pair=1):
            n_ = len(items)
            for i in range(min(depth, n_)):
                emit_scores(items[i])
            for i in range(0, n_, pair):
                for k_ in range(pair):
                    if i + k_ + depth < n_:
                        emit_scores(items[i + k_ + depth])
                for k_ in range(pair):
                    if i + k_ < n_:
                        emit_pv(items[i + k_])
                        if items[i + k_].get('after') is not None:
                            items[i + k_]['after']()

        for j in range(NJ):
            cl = j // 4
            rhs_qs = [qT[64 * g:64 * g + 64, :, j * 128:(j + 1) * 128] for g in range(2)]
            gates = [gts[:, j, 12 * g:12 * g + 12].rearrange("p (h t) -> p h t", t=3) for g in range(2)]
            for g in range(2):
                gsl = slice(64 * g, 64 * g + 64)
                items = []
                for c in range(cl + 1):
                    stat = []
                    if c == cl:
                        stat.append((cmaskt[:, cm_base(j), :], ['cmaskt']))
                    elif c == cl - 1 and j % 4 == 0:
                        stat.append((cmaskt[:, cm_base(j) + 1, :], ['cmaskt']))
                    items.append(dict(
                        k=kccT[gsl, 128 * c:128 * (c + 1)], kkey='kccT', q=rhs_qs[g], stat=stat,
                        v=rhsc[:, c, g, 0:193], vkey='rhsc', first=(c == 0), last=(c == cl),
                        hfirst=lambda h: h % 2 == 0,
                        out=lambda h: ps[3 + h // 2][:, 0:386].rearrange("p (h x) -> p h x", h=2)[:, h % 2, :],
                        okey=lambda h: 'ps%d' % (3 + h // 2)))
                run_pipeline(items)
                for hp in range(2):
                    ts('dve', zt[:, 2 * hp:2 * hp + 2],
                       ps[3 + hp][:, 0:386].rearrange("p (h x) -> p h x", h=2)[:, :, 64], 1e-30, None, ALU.max, None,
                       ['ps%d' % (3 + hp)], ['zt'])
                recip(zt[:, 0:4], zt[:, 0:4], ['zt'], ['zt'])
                tt('dve', zt[:, 4:8], zt[:, 0:4], gates[g][:, :, 0], ALU.mult, ['zt', 'gts'], ['ztg'])
                for h in range(4):
                    pv = ps[3 + h // 2][:, 0:386].rearrange("p (h x) -> p h x", h=2)
                    ts('dve', oaccs[g][:, h, :], pv[:, h % 2, 0:64], zt[:, 4 + h:5 + h], None, ALU.mult, None,
                       ['ps%d' % (3 + h // 2), 'ztg'], ['oacc%d' % g])
                    if h == 0:
                        ts('dve', imp[:, :], pv[:, 0, 65:193], zt[:, 0:1], None, ALU.mult, None,
                           ['ps3', 'zt'], ['imp'])
                    else:
                        stt('dve', imp[:, :], pv[:, h % 2, 65:193], zt[:, h:h + 1], imp[:, :], ALU.mult, ALU.add,
                            ['ps%d' % (3 + h // 2), 'zt', 'imp'], ['imp'])
                tt('dve', sct[:, :], imp[:, :], keepadd[:, 2 * j, :], ALU.mult, ['imp', 'keepadd'], ['sct'])
                tt('dve', sct[:, :], sct[:, :], keepadd[:, 2 * j + 1, :], ALU.add, ['sct', 'keepadd'], ['sct'])
                P.add('dve', lambda e: e.max(out=m8a[:, :], in_=sct[:, :]), ['sct'], ['m8a'])
                P.add('dve', lambda e: e.match_replace(out=sct2[:, :], in_to_replace=m8a[:, :], in_values=sct[:, :],
                                                       imm_value=-3.0e38), ['sct', 'm8a'], ['sct2'])
                P.add('dve', lambda e: e.max(out=m8b[:, :], in_=sct2[:, :]), ['sct2'], ['m8b'])
                ts('dve', selb[:, :], sct[:, :], m8b[:, 7:8], None, ALU.is_ge, None, ['sct', 'm8b'], ['selb'])
                tr(pst[0:64, 0:128], selb[:, 0:128:2], identb[:], ['selb', 'identb'], ['pst'])
                tr(pst[0:64, 640:768], selb[:, 1:128:2], identb[:], ['selb', 'identb'], ['pst'])
                cp('dve', selsb[0:64, 0, :], pst[0:64, 0:128], ['pst'], ['selsb'])
                cp('dve', selsb[0:64, 1, :], pst[0:64, 640:768], ['pst', 'selsb'], ['selsb'])
                nck_ = 4 * j + 4
                dma('sp', selD[g].ap().rearrange("h (c q) -> c h q", q=128), selsb[0:64, :, :], ['selsb'],
                    ['selD%d' % g])
                for hh in range(2):
                    srcb = selD[g].ap()[hh:hh + 1, 0:nck_ * 128].to_broadcast([64, nck_ * 128]) \
                        .rearrange("p (c q) -> p c q", q=128)
                    dma('sp', maskexp[g][64 * hh:64 * hh + 64, 0:nck_, :], srcb, ['selD%d' % g], ['mexp%d' % g])

            def make_after(g, bi, pb):
                def _after():
                    pv = ps[pb][:, 0:260].rearrange("p (h x) -> p h x", h=4)
                    ts('dve', zt[:, 8:12], pv[:, :, 64], 1e-30, None, ALU.max, None, ['ps%d' % pb], ['zt2'])
                    recip(zt[:, 8:12], zt[:, 8:12], ['zt2'], ['zt2'])
                    tt('dve', zt[:, 12:16], zt[:, 8:12], gates[g][:, :, bi], ALU.mult, ['zt2', 'gts'], ['zt2g'])
                    tt('dve', osq[:, :, :], pv[:, :, 0:64], zt[:, 12:16].unsqueeze(2).to_broadcast([128, 4, 64]),
                       ALU.mult, ['ps%d' % pb, 'zt2g'], ['osq'])
                    tt('dve', oaccs[g][:, :, :], oaccs[g][:, :, :], osq[:, :, :], ALU.add,
                       ['osq', 'oacc%d' % g], ['oacc%d' % g])
                    if bi == 1:
                        tt('dve', osq[:, :, :], oaccs[g][:, :, :], oaccs[g][:, :, :], ALU.mult, ['oacc%d' % g], ['osq'])
                        P.add('dve', lambda e: e.tensor_reduce(out=zt[:, 8:12], in_=osq[:, :, :], axis=AX.X, op=ALU.add),
                              ['osq'], ['zt3'])
                        act(zt[:, 8:12], zt[:, 8:12], AF.Sqrt, ['zt3', 'epsc'], ['zt3'], scale=1.0 / 64, bias=epsc[:, 0:1])
                        recip(zt[:, 8:12], zt[:, 8:12], ['zt3'], ['zt3'])
                        tt('dve', yatm[:, 256 * g:256 * (g + 1)].rearrange("p (h d) -> p h d", h=4), oaccs[g][:, :, :],
                           zt[:, 8:12].unsqueeze(2).to_broadcast([128, 4, 64]), ALU.mult, ['oacc%d' % g, 'zt3'], ['yatm'])
                return _after

            items = []
            wl = [w for w in range(8) if 4 * j - 4 + w >= 0]
            for wi, w in enumerate(wl):
                for g in range(2):
                    gsl = slice(64 * g, 64 * g + 64)
                    c = 4 * j - 4 + w
                    items.append(dict(
                        k=kwT[gsl, 128 * c:128 * (c + 1)], kkey='kwT', q=rhs_qs[g],
                        stat=[(wmaskt[:, w, :], ['wmaskt'])],
                        v=Vall[:, c, 2 + g, :], vkey='Vall', first=(wi == 0), last=(wi == len(wl) - 1),
                        hfirst=lambda h: h == 0,
                        out=lambda h, g=g: ps[5 + g][:, 0:260].rearrange("p (h x) -> p h x", h=4)[:, h, :],
                        okey=lambda h, g=g: 'ps%d' % (5 + g),
                        after=make_after(g, 2, 5 + g) if wi == len(wl) - 1 else None))
            nck = 4 * j + 4
            for c in range(nck):
                for g in range(2):
                    gsl = slice(64 * g, 64 * g + 64)
                    stat = [(maskexp[g][:, c, :], ['mexp%d' % g])]
                    if c >= 4 * j:
                        stat.append((selstat[:, c - 4 * j, :], ['selstat']))
                    items.append(dict(
                        k=ksT[gsl, 128 * c:128 * (c + 1)], kkey='ksT', q=rhs_qs[g], stat=stat,
                        v=Vall[:, c, g, :], vkey='Vall', first=(c == 0), last=(c == nck - 1),
                        hfirst=lambda h: h == 0,
                        out=lambda h, g=g: ps[5 + g][:, 0:260].rearrange("p (h x) -> p h x", h=4)[:, h, :],
                        okey=lambda h, g=g: 'ps%d' % (5 + g),
                        after=make_after(g, 1, 5 + g) if c == nck - 1 else None))
            for it_ in items:
                it_['nsb'] = 5
            run_pipeline(items, depth=8, pair=2)
            for ch in range(4):
                tr(pst[:, 128 + ch * 128:256 + ch * 128], yatm[:, ch * 128:(ch + 1) * 128], identb[:],
                   ['yatm', 'identb'], ['pst2'])
            for ch in range(4):
                act(yaT[:, ch, j * 128:(j + 1) * 128], pst[:, 128 + ch * 128:256 + ch * 128], AF.Copy,
                    ['pst2', 'goT'], ['yaT'], scale=goT[:, 4 + ch:5 + ch])
        if debug and debug[0] == 'ya':
            P.barrier()
            cp('dve', arena_t[:, 0:NT], yaT[:, 0, :], ['yaT'], ['dbgt'])
            dump(arena_t[:, 0:NT], 'dbgt', NT)
        P.barrier()

        dma('sp', xT[:, :, :].rearrange("p k t -> p (k t)"), xsave.ap(), ['xsave'], ['xT'])
        cnt = 0
        for t in range(4):
            for dc in range(8):
                pb = cnt % 2
                cnt += 1
                for m in range(8):
                    rhs = ycT[:, m, t * 512:(t + 1) * 512] if m < 4 else yaT[:, m - 4, t * 512:(t + 1) * 512]
                    mm(ps[pb][:, :], wo[:, m, dc * 128:(dc + 1) * 128], rhs, m == 0, m == 7,
                       ['wo', 'ycT', 'yaT'], ['ps%d' % pb])
                stt('dve', xT[:, dc, t * 512:(t + 1) * 512], ps[pb][:, :], GT_(1)[:, dc:dc + 1],
                    xT[:, dc, t * 512:(t + 1) * 512], ALU.mult, ALU.add, ['ps%d' % pb, 'xT', 'vecs'], ['xT'])
        AR.release(M_R1)
        P.barrier()
        if debug and debug[0] == 'x2':
            dump(xT[:, :, :].rearrange("p k t -> p (k t)"), 'xT', 8 * NT)

        ffn(1, A_(2), SH_(2), GT_(2), False)

        sqb = AR.alloc([128, 8, 512], BF16)
        rsb = AR.alloc([128, 512], F32)
        yfin = AR.alloc([128, 8, 512], F32)
        otm = [AR.alloc([128, D], F32) for _ in range(2)]
        for t in range(4):
            norm_mod(xT[:, :, t * 512:(t + 1) * 512], 'xT', 512, None, None, None, None, sqb, rsb, None, ps[6], 'ps6')
            for kc in range(8):
                stt('dve', yfin[:, kc, :], xT[:, kc, t * 512:(t + 1) * 512], gv[:, 24 + kc:25 + kc], rsb[:, :],
                    ALU.mult, ALU.mult, ['xT', 'rsb', 'gv'], ['yfin'])
            for jj in range(4):
                ob = otm[jj % 2]
                ok = 'otm%d' % (jj % 2)
                for hf in range(2):
                    pb = 2 * (jj % 2) + hf
                    for k4 in range(4):
                        kc = hf * 4 + k4
                        tr(ps[pb][:, k4 * 128:(k4 + 1) * 128], yfin[:, kc, jj * 128:(jj + 1) * 128], ident32[:],
                           ['yfin', 'ident32'], ['ps%d' % pb])
                    if hf == 0:
                        act(ob[:, 0:512], ps[pb][:, :], AF.Copy, ['ps%d' % pb], [ok])
                    else:
                        cp('dve', ob[:, 512:1024], ps[pb][:, :], ['ps%d' % pb], [ok])
                row = (4 * t + jj) * 128
                dma('sp', out_loc[row:row + 128, :], ob[:, :], [ok], ['out'], semkey='out')
        P.finalize(st)
    return nc


def cm_base(j):
    idx = 0
    for jj in range(j):
        idx += 2 if (jj % 4 == 0 and jj > 0) else 1
    return idx


_PROG = {}


def _pvec(v, nch=8):
    return np.ascontiguousarray(np.asarray(v, np.float32).reshape(nch, 128).T)


def _partner(d):
    return d + 8 if d < 8 else (d - 8 if d < 16 else d)


def _host_shared(c, positions, w_ada, b_ada, g_ffn1, w1_gate, w1_up, w1_down, g_mix, w_in, conv_w, cmp_pos_k,
                 cmp_pos_v, w_cmpk1, w_cmpk2, w_cmpv1, w_cmpv2, g_out_conv, g_out_attn, w_out, g_ffn2, w2_gate,
                 w2_up, w2_down, g_final):
    f = lambda a: np.ascontiguousarray(np.asarray(a, np.float32))
    sh = {}
    sh["_w_ada"] = f(w_ada[0])
    sh["_b_adaT"] = np.ascontiguousarray(f(b_ada[0]).reshape(72, 128).T)
    sh["gvecs"] = np.ascontiguousarray(np.concatenate([_pvec(g_ffn1[0]), _pvec(g_mix[0]), _pvec(g_ffn2[0]),
                                                       _pvec(g_final)], axis=1))
    sh["w1_gate"], sh["w1_up"], sh["w1_down"] = f(w1_gate[0]), f(w1_up[0]), f(w1_down[0])
    sh["w2_gate"], sh["w2_up"], sh["w2_down"] = f(w2_gate[0]), f(w2_up[0]), f(w2_down[0])
    win = f(w_in[0])
    cols = []
    for base in (0, 512, 1024):
        cols += list(range(base, base + 512))
    qb0 = 1536
    for hg in range(4):
        cols += [qb0 + 64 * hg + d for d in range(64)] + [qb0 + 64 * (4 + hg) + d for d in range(64)]
    for hg in range(4):
        cols += [qb0 + 64 * hg + _partner(d) for d in range(64)] + [qb0 + 64 * (4 + hg) + _partner(d) for d in range(64)]
    for kb in (2048, 2304, 2560):
        cols += [kb + i for i in range(128)]
        cols += [kb + 64 * g + _partner(d) for g in range(2) for d in range(64)]
    cols += [2176 + i for i in range(128)]
    assert len(cols) == 27 * 128
    sh["w_in_fm"] = np.ascontiguousarray(win[:, cols])
    tcols = list(range(2432, 2560)) + list(range(2688, 2816)) + list(range(2816, 2840))
    sh["w_in_tm"] = np.ascontiguousarray(win[:, tcols])
    cw = f(conv_w[0])
    sh["conv_wT"] = np.ascontiguousarray(cw.reshape(3, 4, 128).transpose(2, 1, 0).reshape(128, 12))
    sh["g_oT"] = np.ascontiguousarray(np.concatenate([_pvec(g_out_conv[0], 4), _pvec(g_out_attn[0], 4)], axis=1))
    sh["w_out"] = f(w_out[0])
    for nm, w1 in (("w1k_r", w_cmpk1), ("w1v_r", w_cmpv1)):
        a = f(w1[0]).reshape(32, 64, 256).transpose(1, 0, 2)
        sh[nm] = np.ascontiguousarray(np.concatenate([a, a], 0).reshape(128, 32 * 256))
    pk = f(cmp_pos_k[0]).T
    pv = f(cmp_pos_v[0]).T
    pe = np.concatenate([pk, pv], 1)
    sh["peT"] = np.ascontiguousarray(np.concatenate([pe, pe], 0))
    w2k = f(w_cmpk2[0]).reshape(2, 128, 64)
    pad = np.zeros((128, 2, 2, 128), np.float32)
    for mc in range(2):
        for g in range(2):
            pad[:, mc, g, 64 * g:64 * g + 64] = w2k[mc]
    sh["w2kpad"] = pad.reshape(128, 512)
    sh["w2v"] = np.ascontiguousarray(f(w_cmpv2[0]).reshape(2, 128, 64).transpose(1, 0, 2).reshape(128, 128))
    fr = np.zeros((128, 2), np.float32)
    freqs = np.power(np.float32(500000.0), (-2.0 * np.arange(8, dtype=np.float32) / np.float32(16.0))).astype(np.float32)
    for p in range(128):
        d = p % 64
        if d < 16:
            fr[p, 0] = freqs[d % 8]
            fr[p, 1] = -1.0 if d < 8 else 1.0
    sh["freqs"] = fr
    n_cmp = 511
    c0 = np.arange(n_cmp) * 16
    c1 = c0 + 31
    s0 = np.arange(128) * 64
    s1 = s0 + 63
    ov = ((c0[:, None] <= s1[None, :]) & (c1[:, None] >= s0[None, :])).astype(np.float32)
    ovp = np.zeros((512, 128), np.float32)
    ovp[:511] = ov
    sh["ovl"] = np.ascontiguousarray(ovp.reshape(4, 128, 128).transpose(1, 0, 2).reshape(128, 512))
    return sh


def _host_core(core, x, c, positions):
    b, r = core // 4, core % 4
    m = {}
    xb = np.asarray(x[b], np.float32).reshape(64, 128, D)
    m["x_loc"] = np.ascontiguousarray(xb[r::4].reshape(NT, D))
    xh = np.zeros((HAL, D), np.float32)
    hm = np.zeros((128, HAL), np.float32)
    xflat = np.asarray(x[b], np.float32)
    for j in range(NJ):
        qb = 4 * j + r
        if qb > 0:
            xh[2 * j] = xflat[128 * qb - 2]
            xh[2 * j + 1] = xflat[128 * qb - 1]
            hm[:, 2 * j:2 * j + 2] = 1.0
    m["x_halo"] = xh
    m["halo_mask"] = hm
    m["cT"] = _pvec(np.asarray(c[b], np.float32))
    pb = np.asarray(positions[b], np.int32).reshape(64, 128)[r::4].reshape(NT)
    m["pos"] = np.ascontiguousarray(np.tile(pb[None, :], (128, 1)))
    ik = np.arange(128)[:, None]
    iq = np.arange(128)[None, :]
    causal = np.where(ik <= iq, 1.0, 0.0).astype(np.float32)
    full = np.zeros((128, 128), np.float32)
    zero = np.ones((128, 128), np.float32)
    ss = np.stack([zero if rp < r else (causal if rp == r else full) for rp in range(4)], 1)
    m["selstat"] = np.ascontiguousarray(ss.reshape(128, 512))
    wm = []
    edge = np.where(ik > iq, 1.0, 0.0).astype(np.float32)
    for w in range(8):
        dd = w - 4 - r
        if dd < -4 or dd > 0:
            wm.append(full)
        elif dd == -4:
            wm.append(edge)
        elif dd == 0:
            wm.append(causal)
        else:
            wm.append(zero)
    m["wmask"] = np.ascontiguousarray(np.stack(wm, 1).reshape(128, 1024))
    cms = []
    for j in range(NJ):
        qb = 4 * j + r
        cl = j // 4
        chunks = [cl] + ([cl - 1] if (j % 4 == 0 and j > 0) else [])
        for cc in chunks:
            ig = 128 * cc + np.arange(128)[:, None]
            t = 128 * qb + np.arange(128)[None, :]
            valid = (16 * ig + 31 <= t) & (ig < 511)
            cms.append(np.where(valid, 1.0, 0.0).astype(np.float32))
    assert len(cms) == 19
    m["cmask"] = np.ascontiguousarray(np.stack(cms, 1).reshape(128, 19 * 128))
    ka = np.zeros((128, 2 * NJ, 128), np.float32)
    jb = np.arange(128)[None, :]
    for j in range(NJ):
        qb = 4 * j + r
        cur = (2 * qb + (np.arange(128) >= 64).astype(np.int64))[:, None]
        keep = np.ones((128, 128), np.float32)
        add = np.zeros((128, 128), np.float32)
        f0 = (jb == 0) & np.ones((128, 1), bool)
        fm1 = jb == cur - 1
        fc = jb == cur
        fut = jb > cur
        add[f0] = 1024.0
        add[fm1] = 2048.0
        add[fc] = 4096.0
        add[fut] = -1.0e9
        keep[f0 | fm1 | fc | fut] = 0.0
        ka[:, 2 * j] = keep
        ka[:, 2 * j + 1] = add
    m["keepadd"] = np.ascontiguousarray(ka.reshape(128, 2 * NJ * 128))
    return m


def kernel(x, c, positions, w_ada, b_ada, g_ffn1, w1_gate, w1_up, w1_down, g_mix, w_in, conv_w, cmp_pos_k,
           cmp_pos_v, w_cmpk1, w_cmpk2, w_cmpv1, w_cmpv2, g_out_conv, g_out_attn, w_out, g_ffn2, w2_gate,
           w2_up, w2_down, g_final, _debug=None):
    x = np.asarray(x)
    sh = _host_shared(c, positions, w_ada, b_ada, g_ffn1, w1_gate, w1_up, w1_down, g_mix, w_in, conv_w, cmp_pos_k,
                      cmp_pos_v, w_cmpk1, w_cmpk2, w_cmpv1, w_cmpv2, g_out_conv, g_out_attn, w_out, g_ffn2,
                      w2_gate, w2_up, w2_down, g_final)
    key = str(_debug)
    if key not in _PROG:
        _PROG[key] = build_program(_debug)
    nc = _PROG[key]
    in_maps = []
    for core in range(NCORES):
        m = {k_: v_ for k_, v_ in sh.items() if not k_.startswith("_")}
        m.update(_host_core(core, x, np.asarray(c), np.asarray(positions)))
        rq = core % 4
        m["w_ada_q"] = np.ascontiguousarray(sh["_w_ada"][:, rq * 2304:(rq + 1) * 2304])
        m["b_adaT_q"] = np.ascontiguousarray(sh["_b_adaT"][:, rq * 18:(rq + 1) * 18])
        in_maps.append(m)
    res = run_bass_kernel_spmd(nc, in_maps, core_ids=list(range(NCORES)))
    if _debug is not None:
        return res
    out = np.zeros((2, 64, 128, D), np.float32)
    for core in range(NCORES):
        b, r = core // 4, core % 4
        out[b, r::4] = np.asarray(res.results[core]["out_loc"], np.float32).reshape(NJ, 128, D)
    return out.reshape(2, S, D)
```
